# Optimizing a Trainium2 kernel written in Bass

```python
import math
import jax, jax.numpy as jnp
from jax import lax
import numpy as np

D_MODEL = 1024
BATCH = 16
SEQ = 2048
DEPTH = 2

HEAD_DIM = 64
SSM_WIDTH = 3 * D_MODEL // 8
FFT_WIDTH = D_MODEL // 4
ATT_WIDTH = D_MODEL - SSM_WIDTH - FFT_WIDTH
SSM_GROUP = 16
SSM_GROUPS = SSM_WIDTH // SSM_GROUP
SSM_STATE = 64
FFT_HEADS = FFT_WIDTH // HEAD_DIM
N_Q_HEADS = ATT_WIDTH // HEAD_DIM
N_KV_HEADS = 2
GQA_GROUP = N_Q_HEADS // N_KV_HEADS
KV_WIDTH = N_KV_HEADS * HEAD_DIM
IN_WIDTH = SSM_WIDTH + FFT_WIDTH + ATT_WIDTH + 2 * KV_WIDTH
WINDOW = 128
BLOCK = 128
KV_SPAN = 3 * BLOCK
D_FF = ((8 * D_MODEL // 3 + 255) // 256) * 256
DT_MIN = 1e-3
DT_MAX = 1e-1
RMS_EPS = 1e-6
NEG_INF = -1e30

kernel_name = 'hybrid_s5_fnet_swa_convffn_encoder'


def rms_norm(x, gain):
    xf = x.astype(jnp.float32)
    y = xf * lax.rsqrt(jnp.mean(xf * xf, axis=-1, keepdims=True) + RMS_EPS)
    return (y * gain.astype(jnp.float32)).astype(x.dtype)


def _scan_combine(left, right):
    a_l, h_l = left
    a_r, h_r = right
    return a_l * a_r, a_r * h_l + h_r


def s5_mixer(u, lam_re, lam_im, log_dt, b_re, b_im, c_re, c_im, d_skip, w_glu):
    bsz, seq, _ = u.shape
    f32 = jnp.float32
    uf = u.astype(f32).reshape(bsz, seq, SSM_GROUPS, SSM_GROUP)
    lam = lax.complex(lam_re.astype(f32), lam_im.astype(f32))
    dt = jnp.exp(log_dt.astype(f32))[..., None]
    lam_bar = jnp.exp(lam * dt)
    b = lax.complex(b_re.astype(f32), b_im.astype(f32))
    b_bar = ((lam_bar - 1.0) / lam)[..., None] * b
    states = []
    for d, rev in ((0, False), (1, True)):
        bu = lax.complex(jnp.einsum('bsgc,gpc->bsgp', uf, b_bar[d].real),
                         jnp.einsum('bsgc,gpc->bsgp', uf, b_bar[d].imag))
        a = jnp.broadcast_to(lam_bar[d], bu.shape)
        _, h_dir = lax.associative_scan(_scan_combine, (a, bu), reverse=rev, axis=1)
        states.append(h_dir)
    h = states[0] + states[1]
    y = (jnp.einsum('bsgp,gcp->bsgc', h.real, c_re.astype(f32))
         - jnp.einsum('bsgp,gcp->bsgc', h.imag, c_im.astype(f32)))
    y = y.reshape(bsz, seq, SSM_WIDTH) + d_skip.astype(f32) * u.astype(f32)
    y = jax.nn.gelu(y).astype(u.dtype)
    return y * jax.nn.sigmoid(y @ w_glu)


def fourier_mixer(f, w_fft):
    bsz, seq, _ = f.shape
    ff = f.astype(jnp.float32).reshape(bsz, seq, FFT_HEADS, HEAD_DIM)
    mixed = jnp.fft.fft2(ff, axes=(1, 3), norm='ortho').real.astype(f.dtype)
    return jnp.einsum('bshd,hde->bshe', mixed, w_fft).reshape(bsz, seq, FFT_WIDTH)


def window_attention(q, k, v, sink):
    bsz, seq, _ = q.shape
    nblk = seq // BLOCK
    qb = (q * HEAD_DIM ** -0.5).reshape(bsz, nblk, BLOCK, N_KV_HEADS, GQA_GROUP, HEAD_DIM)

    def band(t):
        t = t.reshape(bsz, seq, N_KV_HEADS, HEAD_DIM)
        t = jnp.pad(t, ((0, 0), (BLOCK, BLOCK), (0, 0), (0, 0)))
        t = t.reshape(bsz, nblk + 2, BLOCK, N_KV_HEADS, HEAD_DIM)
        return jnp.concatenate([t[:, :-2], t[:, 1:-1], t[:, 2:]], axis=2)

    kb, vb = band(k), band(v)
    qi = jnp.arange(BLOCK)[:, None]
    kj = jnp.arange(KV_SPAN)[None, :]
    dist = jnp.abs(qi + BLOCK - kj)
    key_pos = jnp.arange(nblk)[:, None] * BLOCK - BLOCK + kj
    valid = (dist <= WINDOW)[None] & ((key_pos >= 0) & (key_pos < seq))[:, None, :]
    slopes = jnp.exp2(-8.0 * jnp.arange(1, N_Q_HEADS + 1, dtype=jnp.float32) / N_Q_HEADS)
    slopes = slopes.reshape(N_KV_HEADS, GQA_GROUP)
    bias = -slopes[:, :, None, None] * dist.astype(jnp.float32)
    scores = jnp.einsum('bnqkgd,bnskd->bnkgqs', qb, kb).astype(jnp.float32) + bias
    scores = jnp.where(valid[None, :, None, None], scores, NEG_INF)
    sink_l = sink.astype(jnp.float32).reshape(1, 1, N_KV_HEADS, GQA_GROUP, 1, 1)
    m = jnp.maximum(jnp.max(scores, axis=-1, keepdims=True), sink_l)
    p = jnp.exp(scores - m)
    p = p / (jnp.sum(p, axis=-1, keepdims=True) + jnp.exp(sink_l - m))
    out = jnp.einsum('bnkgqs,bnskd->bnqkgd', p.astype(v.dtype), vb)
    return out.reshape(bsz, seq, ATT_WIDTH)


def dwconv3(h, w, b):
    hp = jnp.pad(h, ((0, 0), (1, 1), (0, 0)))
    return hp[:, :-2] * w[0] + hp[:, 1:-1] * w[1] + hp[:, 2:] * w[2] + b


def setup_inputs(seed: int = 0) -> dict:
    key = jax.random.key(seed)
    ks = iter(jax.random.split(key, 32))
    nrm = lambda shape, scale: scale * jax.random.normal(next(ks), shape, jnp.float32)
    L, D, G, P = DEPTH, D_MODEL, SSM_GROUPS, SSM_STATE
    lam_im_base = jnp.pi * jnp.arange(P, dtype=jnp.float32)
    return {
        'x': nrm((BATCH, SEQ, D), 1.0),
        'c': nrm((BATCH, D), 1.0),
        'w_ada': nrm((L, D, 6 * D), 0.5 * D ** -0.5),
        'b_ada': nrm((L, 6 * D), 0.01),
        'g_pre_mix': 1.0 + nrm((L, D), 0.05),
        'g_post_mix': 1.0 + nrm((L, D), 0.05),
        'g_pre_ffn': 1.0 + nrm((L, D), 0.05),
        'g_post_ffn': 1.0 + nrm((L, D), 0.05),
        'w_in': nrm((L, D, IN_WIDTH), D ** -0.5),
        'lam_re': -0.5 + nrm((L, 2, G, P), 0.01),
        'lam_im': lam_im_base + nrm((L, 2, G, P), 0.01),
        'log_dt': jax.random.uniform(next(ks), (L, 2, G), jnp.float32,
                                     minval=math.log(DT_MIN), maxval=math.log(DT_MAX)),
        'b_re': nrm((L, 2, G, P, SSM_GROUP), (2 * SSM_GROUP) ** -0.5),
        'b_im': nrm((L, 2, G, P, SSM_GROUP), (2 * SSM_GROUP) ** -0.5),
        'c_re': nrm((L, G, SSM_GROUP, P), (2 * P) ** -0.5),
        'c_im': nrm((L, G, SSM_GROUP, P), (2 * P) ** -0.5),
        'd_skip': nrm((L, SSM_WIDTH), 1.0),
        'w_glu': nrm((L, SSM_WIDTH, SSM_WIDTH), SSM_WIDTH ** -0.5),
        'w_fft': nrm((L, FFT_HEADS, HEAD_DIM, HEAD_DIM), HEAD_DIM ** -0.5),
        'sink': nrm((L, N_Q_HEADS), 1.0),
        'w_out': nrm((L, D, D), D ** -0.5),
        'w_up': nrm((L, D, 2 * D_FF), D ** -0.5),
        'conv_w': nrm((L, 3, 2 * D_FF), 3 ** -0.5),
        'conv_b': nrm((L, 2 * D_FF), 0.01),
        'w_down': nrm((L, D_FF, D), D_FF ** -0.5),
    }


def reference(x, c, w_ada, b_ada, g_pre_mix, g_post_mix, g_pre_ffn, g_post_ffn, w_in,
              lam_re, lam_im, log_dt, b_re, b_im, c_re, c_im, d_skip, w_glu, w_fft, sink,
              w_out, w_up, conv_w, conv_b, w_down):
    splits = [SSM_WIDTH, SSM_WIDTH + FFT_WIDTH, SSM_WIDTH + FFT_WIDTH + ATT_WIDTH,
              SSM_WIDTH + FFT_WIDTH + ATT_WIDTH + KV_WIDTH]
    for l in range(DEPTH):
        mod = (c @ w_ada[l] + b_ada[l])[:, None, :]
        sh_m, sc_m, gt_m, sh_f, sc_f, gt_f = jnp.split(mod, 6, axis=-1)

        h = rms_norm(x, g_pre_mix[l]) * (1.0 + sc_m) + sh_m
        z = h @ w_in[l]
        u, f, q, k, v = jnp.split(z, splits, axis=-1)
        y_ssm = s5_mixer(u, lam_re[l], lam_im[l], log_dt[l], b_re[l], b_im[l],
                         c_re[l], c_im[l], d_skip[l], w_glu[l])
        y_fft = fourier_mixer(f, w_fft[l])
        y_att = window_attention(q, k, v, sink[l])
        y = jnp.concatenate([y_ssm.astype(x.dtype), y_fft.astype(x.dtype), y_att.astype(x.dtype)], axis=-1) @ w_out[l]
        x = x + gt_m * rms_norm(y, g_post_mix[l])

        h = rms_norm(x, g_pre_ffn[l]) * (1.0 + sc_f) + sh_f
        up = dwconv3(h @ w_up[l], conv_w[l], conv_b[l])
        gate, val = jnp.split(up, 2, axis=-1)
        y = (jax.nn.gelu(gate) * val) @ w_down[l]
        x = x + gt_f * rms_norm(y, g_post_ffn[l])
    return x
```

```python
import math
import threading
import numpy as np
import ml_dtypes
import concourse.bass as bass
import concourse.mybir as mybir
from concourse.bass_utils import run_bass_kernel_spmd

F32 = mybir.dt.float32
BF16 = mybir.dt.bfloat16
AF = mybir.ActivationFunctionType
ALU = mybir.AluOpType
AX = mybir.AxisListType

ENGS = ("pe", "act", "dve", "pool", "sp")
N_DMA_SEMS = 12


class Dep:
    __slots__ = ("w", "r", "name")

    def __init__(self, name=""):
        self.w = None
        self.r = {}
        self.name = name


class KB:
    def __init__(self, nc):
        self.nc = nc
        self.ops = {e: [] for e in ENGS}
        self.cnt = {e: 0 for e in ENGS}
        self.seen = {e: {} for e in ENGS}
        self.sems = {}
        self.dma_cnt = [0] * N_DMA_SEMS
        self.dma_rr = 0
        self.ninst = 0
        self._tick = {}

    def _need(self, reads, writes):
        need = {}

        def add(tok):
            if tok is None:
                return
            k, v = tok
            if need.get(k, 0) < v:
                need[k] = v

        for d in reads:
            add(d.w)
        for d in writes:
            add(d.w)
            for k, v in d.r.items():
                add((k, v))
        return need

    def _emit_waits(self, eng, need):
        seen = self.seen[eng]
        waits = []
        for k, v in need.items():
            if k == "pe" and eng == "pe":
                continue
            if seen.get(k, 0) >= v:
                continue
            seen[k] = v
            waits.append((k, v))
        return waits

    def _update(self, tok, reads, writes):
        k, v = tok
        for d in reads:
            if d.r.get(k, 0) < v:
                d.r[k] = v
        for d in writes:
            d.w = tok
            d.r = {}

    def op(self, eng, fn, reads=(), writes=()):
        need = self._need(reads, writes)
        waits = self._emit_waits(eng, need)
        self.cnt[eng] += 1
        tok = (eng, self.cnt[eng])
        self.seen[eng][eng] = max(self.seen[eng].get(eng, 0), 0)
        sems = self.sems

        def run(e, waits=waits, fn=fn, eng=eng):
            for k, v in waits:
                e.wait_ge(sems[k], v)
            ins = fn(e)
            ins.then_inc(sems[eng], 1)

        self.ops[eng].append(run)
        self._update(tok, reads, writes)
        self.ninst += 1
        f = self._tick.get(threading.get_ident())
        if f:
            f()
        return tok

    def dma(self, q, out, in_, reads=(), writes=(), **kw):
        i = self.dma_rr
        self.dma_rr = (i + 1) % N_DMA_SEMS
        key = "d%d" % i
        need = self._need(reads, writes)
        if self.dma_cnt[i] > 0:
            if need.get(key, 0) < self.dma_cnt[i]:
                need[key] = self.dma_cnt[i]
        waits = self._emit_waits(q, need)
        self.dma_cnt[i] += 16
        tok = (key, self.dma_cnt[i])
        sems = self.sems

        def run(e, waits=waits, out=out, in_=in_, key=key, kw=kw):
            for k, v in waits:
                e.wait_ge(sems[k], v)
            e.dma_start(out=out, in_=in_, **kw).then_inc(sems[key], 16)

        self.ops[q].append(run)
        self._update(tok, reads, writes)
        self.ninst += 1
        f = self._tick.get(threading.get_ident())
        if f:
            f()
        return tok

    def interleave(self, funcs, quanta):
        n = len(funcs)
        if n == 1:
            funcs[0]()
            return
        cond = threading.Condition()
        st = {"turn": 0, "alive": [True] * n, "cnt": 0, "err": None}

        def nxt():
            t = st["turn"]
            for k in range(1, n + 1):
                c = (t + k) % n
                if st["alive"][c]:
                    st["turn"] = c
                    return
            st["turn"] = -1

        def tick(i):
            st["cnt"] += 1
            if st["cnt"] >= quanta[i]:
                st["cnt"] = 0
                with cond:
                    nxt()
                    cond.notify_all()
                    while st["turn"] != i:
                        cond.wait()

        def worker(i):
            with cond:
                while st["turn"] != i:
                    cond.wait()
            self._tick[threading.get_ident()] = (lambda i=i: tick(i))
            try:
                funcs[i]()
            except BaseException as e:
                st["err"] = e
            finally:
                self._tick.pop(threading.get_ident(), None)
                with cond:
                    st["alive"][i] = False
                    st["cnt"] = 0
                    if st["turn"] == i:
                        nxt()
                    cond.notify_all()

        ths = [threading.Thread(target=worker, args=(i,)) for i in range(n)]
        for t in ths:
            t.start()
        for t in ths:
            t.join()
        if st["err"] is not None:
            raise st["err"]

    def wait_all(self, eng, deps):
        need = self._need((), deps)
        waits = self._emit_waits(eng, need)
        sems = self.sems

        def run(e, waits=waits):
            for k, v in waits:
                e.wait_ge(sems[k], v)

        self.ops[eng].append(run)

    def barrier(self):
        need = {e: self.cnt[e] for e in ENGS if e != "sp" and self.cnt[e] > 0}
        for i in range(N_DMA_SEMS):
            if self.dma_cnt[i] > 0:
                need["d%d" % i] = self.dma_cnt[i]
        sems = self.sems
        for eng in ENGS:
            waits = self._emit_waits(eng, dict(need))

            def run(e, waits=waits):
                for k, v in waits:
                    e.wait_ge(sems[k], v)

            self.ops[eng].append(run)

    def emit(self, stack):
        nc = self.nc
        for e in ("pe", "act", "dve", "pool"):
            self.sems[e] = stack.enter_context(nc.semaphore("sem_" + e))
        for i in range(N_DMA_SEMS):
            self.sems["d%d" % i] = stack.enter_context(nc.semaphore("sem_d%d" % i))
        block = stack.enter_context(nc.Block())
        ops = self.ops

        @block.tensor
        def _(e):
            for f in ops["pe"]:
                f(e)

        @block.scalar
        def _(e):
            for f in ops["act"]:
                f(e)

        @block.vector
        def _(e):
            for f in ops["dve"]:
                f(e)

        @block.gpsimd
        def _(e):
            for f in ops["pool"]:
                f(e)

        @block.sync
        def _(e):
            for f in ops["sp"]:
                f(e)


def MM(out, lhsT, rhs, start=True, stop=True):
    return lambda e: e.matmul(out, lhsT=lhsT, rhs=rhs, start=start, stop=stop)


def MMZ(out, lhsT, rhs):
    return lambda e: e.matmul(out, lhsT=lhsT, rhs=rhs, start=False, stop=False, skip_group_check=True)


def TR(out, in_, ident):
    return lambda e: e.transpose(out, in_, ident)


def SEQ(fs):
    def run(e):
        r = None
        for f in fs:
            r = f(e)
        return r
    return run


def ACTF(out, in_, func, bias=0.0, scale=1.0, accum=None):
    if accum is None:
        return lambda e: e.activation(out=out, in_=in_, func=func, bias=bias, scale=scale)
    return lambda e: e.activation(out=out, in_=in_, func=func, bias=bias, scale=scale, accum_out=accum)


def TT(out, a, b, op):
    return lambda e: e.tensor_tensor(out=out, in0=a, in1=b, op=op)


def TS(out, a, s1, s2=None, op0=ALU.mult, op1=None):
    if s2 is None:
        return lambda e: e.tensor_scalar(out=out, in0=a, scalar1=s1, scalar2=None, op0=op0)
    return lambda e: e.tensor_scalar(out=out, in0=a, scalar1=s1, scalar2=s2, op0=op0, op1=op1)


def STT(out, in0, scalar, in1, op0, op1):
    return lambda e: e.scalar_tensor_tensor(out=out, in0=in0, scalar=scalar, in1=in1, op0=op0, op1=op1)


def CP(out, in_):
    return lambda e: e.tensor_copy(out=out, in_=in_)


def MS(out, val):
    return lambda e: e.memset(out, val)


def RCP(out, in_):
    return lambda e: e.reciprocal(out=out, in_=in_)


class Tl:
    def __init__(self, h, base, shape):
        self.h = h
        self.base = base
        self.shape = list(shape)
        self.row = h.shape[1]
        n = 1
        for d in shape[1:]:
            n *= d
        self.n = n
        ap = h[0:shape[0], base:base + n]
        if len(shape) > 2:
            names = "abcdefg"[:len(shape) - 1]
            kw = {names[i]: shape[i + 1] for i in range(1, len(names))}
            ap = ap.rearrange("p (%s) -> p %s" % (" ".join(names), " ".join(names)), **kw)
        self.full = ap

    def __getitem__(self, idx):
        return self.full[idx]

    def cu(self, p0, npart, off, dims):
        return bass.AP(tensor=self.h, offset=p0 * self.row + self.base + off,
                       ap=[[self.row, npart]] + [list(d) for d in dims])


class Arena:
    def __init__(self, h32, nbytes):
        self.h = {F32: h32, BF16: h32.bitcast(BF16)}
        self.nbytes = nbytes
        self.top = 0
        self.peak = 0

    def alloc(self, shape, dt=F32):
        esz = 4 if dt == F32 else 2
        n = 1
        for d in shape[1:]:
            n *= d
        nb = (n * esz + 31) // 32 * 32
        off = self.top
        self.top += nb
        self.peak = max(self.peak, self.top)
        assert self.top <= self.nbytes, ("SBUF arena overflow", self.top, self.nbytes)
        return Tl(self.h[dt], off // esz, shape)

    def mark(self):
        return self.top

    def release(self, m):
        self.top = m


D = 1024
NCH = 8
DFF = 2816
NFC = 22
L = 2
WIN_COLS = 1408
PI = math.pi


def build(S, parts=("ssm", "fft", "att", "ffn"), nlayers=L):
    import contextlib
    import os
    nc = bass.Bass("TRN2", target_bir_lowering=False)
    kb = KB(nc)
    NT = S // 128
    N5 = S // 512
    K8 = S // 8
    ins = {}

    def din(name, shape, dt=F32):
        ins[name] = nc.dram_tensor(name, list(shape), dt, kind="ExternalInput").ap()
        return ins[name]

    xT_d = din("xT", [2, 128, NCH, S])
    cT_d = din("cT", [128, NCH, 2])
    w_ada_d = din("w_ada", [L, NCH, 128, 6 * D])
    b_ada_d = din("b_adaT", [128, L, 48])
    g_d = din("gT", [128, L, 4, NCH])
    w_in_d = din("w_in", [L, 128, NCH, WIN_COLS])
    lam_d = din("lamT", [128, L, 2, 24])
    ldt_d = din("ldt", [128, L, 24])
    b_d = din("bT", [128, L, 2, 24, 16])
    c_d = din("cTs", [128, L, 2, 24, 16])
    dsk_d = din("dsk", [128, L, 24])
    w_glu_d = din("w_glu", [L, 128, 3, 384])
    w_fft_d = din("w_fft", [L, 64, 4, 64])
    sink_d = din("sinkT", [128, L, 3])
    w_out_d = din("w_out", [L, 128, NCH, D])
    w_up_d = din("w_up", [L, NFC, 128, NCH, 256])
    cw_d = din("conv_w", [128, L, 3, 2 * NFC])
    cb_d = din("conv_b", [128, L, 2 * NFC])
    w_down_d = din("w_down", [L, NFC, 128, D])
    ident_d = din("ident", [128, 128])
    maskf_d = din("maskf", [128, 128])
    maskb_d = din("maskb", [128, 128])
    selW_d = din("selW", [128, 8, 240], BF16)
    unselW_d = din("unselW", [128, 8, 240], BF16)
    biasT_d = din("biasT", [128, 6, 3, 128], BF16)
    c64_d = din("c64d", [64, 128])
    s64_d = din("s64d", [64, 128])
    mv_d = din("mvals", [128, 9, 24])
    dft_d = din("dft", [2, S, S], BF16)
    outT_d = nc.dram_tensor("outT", [2, 128, NCH, S], F32, kind="ExternalOutput").ap()
    DBG = os.environ.get('KDBG') == 'ssmdbg'
    if DBG:
        dbg_ug = nc.dram_tensor("dbg_ug", [128, 24 * (S // 8)], BF16, kind="ExternalOutput").ap()
        dbg_x0 = nc.dram_tensor("dbg_x0", [128, (S // 8) * 48], BF16, kind="ExternalOutput").ap()
        dbg_xs = nc.dram_tensor("dbg_xs", [128, (S // 8) * 48], BF16, kind="ExternalOutput").ap()
        dbg_yg = nc.dram_tensor("dbg_yg", [128, 3 * S], BF16, kind="ExternalOutput").ap()
        dbg_u = nc.dram_tensor("dbg_u", [128, 3 * S], BF16, kind="ExternalOutput").ap()
    ssm_scr = nc.dram_tensor("ssm_scr", [L, 128, 24 * 7 * 128], BF16, kind="Internal").ap()

    def dscr(name, shape):
        return nc.dram_tensor(name, list(shape), BF16, kind="Internal").ap()

    win_b = dscr("win_b", [L, 128, NCH, WIN_COLS])
    wout_b = dscr("wout_b", [L, 128, NCH, D])
    wup_b = dscr("wup_b", [L, NFC, 128, NCH * 256])
    wdn_b = dscr("wdn_b", [L, NFC, 128, D])
    wglu_b = dscr("wglu_b", [L, 128, 3 * 384])
    st = contextlib.ExitStack()
    ARENA_BYTES = 212800
    arena_h = st.enter_context(nc.sbuf_tensor("arena", [128, ARENA_BYTES // 4], F32))
    ar = Arena(arena_h, ARENA_BYTES)
    psum = []
    for i in range(8):
        psum.append((st.enter_context(nc.psum_tensor("ps%d" % i, [128, 512], F32)), Dep()))
    prr = [0]

    held = set()

    def P(hold=False):
        while prr[0] in held:
            prr[0] = (prr[0] + 1) % 8
        i = prr[0]
        prr[0] = (i + 1) % 8
        if hold:
            held.add(i)
        return psum[i]

    def Prelease(pp):
        for i in range(8):
            if psum[i] is pp:
                held.discard(i)

    def ld(q, tile_ap, dram_ap, dep, **kw):
        kb.dma(q, tile_ap, dram_ap, writes=[dep], **kw)

    ident = ar.alloc([128, 128]); d_ident = Dep()
    ld("sp", ident[:], ident_d[:, :], d_ident)
    ident_bf = ar.alloc([128, 128], BF16)
    kb.op("dve", CP(ident_bf[:], ident[:]), [d_ident], [d_ident])
    ones_bf = ar.alloc([128, 128], BF16); d_ones = Dep()
    kb.op("dve", MS(ones_bf[:], 1.0), [], [d_ones])
    cT = ar.alloc([128, NCH, 2]); d_cT = Dep()
    ld("sp", cT[:], cT_d[:, :, :], d_cT)
    b_adaT = ar.alloc([128, L, 48]); d_bada = Dep()
    ld("sp", b_adaT[:], b_ada_d[:, :, :], d_bada)
    gT = ar.alloc([128, L, 4, NCH]); d_gT = Dep()
    ld("sp", gT[:], g_d[:, :, :, :], d_gT)
    modT = ar.alloc([128, L, 48, 2]); d_mod = Dep()
    Am = ar.alloc([128, L, 2, NCH, 2]); d_Am = Dep()
    Gm = ar.alloc([128, L, 2, NCH, 2]); d_Gm = Dep()
    cw = ar.alloc([128, L, 3, 2 * NFC]); d_cw = Dep()
    ld("sp", cw[:], cw_d[:, :, :, :], d_cw)
    cb = ar.alloc([128, L, 2 * NFC]); d_cb = Dep()
    ld("sp", cb[:], cb_d[:, :, :], d_cb)
    sinkT = ar.alloc([128, L, 3]); d_sink = Dep()
    ld("sp", sinkT[:], sink_d[:, :, :], d_sink)
    esink = ar.alloc([128, L, 3])
    kb.op("act", ACTF(esink[:], sinkT[:], AF.Exp), [d_sink], [d_sink])
    tmp2 = ar.alloc([128, 512]); d_tmp2 = Dep()
    eps_t = ar.alloc([128, 1]); d_eps = Dep()
    kb.op("dve", MS(eps_t[:], 1e-6), [], [d_eps])

    A8 = ar.alloc([128, L, 2, 24]); d_A8 = Dep()
    d_wscr = Dep()
    mC = ar.mark()
    NSTG = 3
    stg = [ar.alloc([128, 2048]) for _ in range(NSTG)]
    stb = [ar.alloc([128, 2048], BF16) for _ in range(NSTG)]
    d_stg = [Dep() for _ in range(NSTG)]
    d_stb = [Dep() for _ in range(NSTG)]
    stg_i = [0]

    def stream_precast():
        chunks = []
        for l in range(nlayers):
            for kt in range(NCH):
                chunks.append((win_b[l, :, kt, :], w_in_d[l, :, kt, :], WIN_COLS))
                chunks.append((wout_b[l, :, kt, :], w_out_d[l, :, kt, :], D))
            chunks.append((wglu_b[l], w_glu_d[l].rearrange("p k c -> p (k c)"), 3 * 384))
            if "ffn" in parts:
                for cc_ in range(NFC):
                    chunks.append((wup_b[l, cc_], w_up_d[l, cc_].rearrange("p k c -> p (k c)"), NCH * 256))
                    chunks.append((wdn_b[l, cc_], w_down_d[l, cc_], D))

        def load(ci):
            dst_ap, src_ap, n = chunks[ci]
            kb.dma("sp", stg[ci % NSTG][:, 0:n], src_ap, writes=[d_stg[ci % NSTG]])

        for ci in range(min(2, len(chunks))):
            load(ci)
        for ci, (dst_ap, src_ap, n) in enumerate(chunks):
            i = ci % NSTG
            if ci % 2:
                kb.op("act", ACTF(stb[i][:, 0:n], stg[i][:, 0:n], AF.Copy), [d_stg[i]], [d_stb[i]])
            else:
                kb.op("dve", CP(stb[i][:, 0:n], stg[i][:, 0:n]), [d_stg[i]], [d_stb[i]])
            kb.dma("sp", dst_ap, stb[i][:, 0:n], reads=[d_stb[i]], writes=[Dep()])
            if ci + 2 < len(chunks):
                load(ci + 2)

    m0 = ar.mark()
    wa = [ar.alloc([128, 3072]) for _ in range(2)]
    d_wa = [Dep(), Dep()]

    def stream_adaln():
        wi = 0
        for l in range(nlayers):
            acc_ = P(hold=True)
            pt, pd = acc_
            kb.op("dve", MS(pt[:, 0:96], 0.0), [], [pd])
            for kt in range(NCH):
                for hf in range(2):
                    b = wi % 2
                    wi += 1
                    ld("sp", wa[b][:], w_ada_d[l, kt, :, hf * 3072:(hf + 1) * 3072], d_wa[b])
                    fs = []
                    for m in range(24):
                        mm = hf * 24 + m
                        fs.append(MMZ(pt[:, mm * 2:mm * 2 + 2], wa[b][:, m * 128:(m + 1) * 128], cT[:, kt, :]))
                    kb.op("pe", SEQ(fs), [d_wa[b], d_cT], [pd])
            kb.op("dve", TT(modT[:, l], pt[:, 0:96].rearrange("p (m b) -> p m b", b=2),
                            b_adaT[:, l, :].unsqueeze(2).to_broadcast([128, 48, 2]), ALU.add),
                  [pd, d_bada], [d_mod])
            Prelease(acc_)
            for i, (sc0, gt0, gpre, gpost) in enumerate(((8, 16, 0, 1), (32, 40, 2, 3))):
                kb.op("dve", STT(Am[:, l, i], modT[:, l, sc0:sc0 + 8, :], 1.0,
                                 gT[:, l, gpre, :].unsqueeze(2).to_broadcast([128, NCH, 2]), ALU.add, ALU.mult),
                      [d_mod, d_gT], [d_Am])
                kb.op("dve", TT(Gm[:, l, i], modT[:, l, gt0:gt0 + 8, :],
                                gT[:, l, gpost, :].unsqueeze(2).to_broadcast([128, NCH, 2]), ALU.mult),
                      [d_mod, d_gT], [d_Gm])


    MAGIC = 12582912.0
    if "ssm" in parts:
        maskf = ar.alloc([128, 128]); maskb = ar.alloc([128, 128]); d_mask = Dep()
        ld("sp", maskf[:], maskf_d[:, :], d_mask)
        ld("sp", maskb[:], maskb_d[:, :], d_mask)

    def stream_gen():
        if "ssm" not in parts:
            return
        for l in range(nlayers):
            m1 = ar.mark()
            dg = Dep()

            def G(fn, eng="dve", extra=()):
                kb.op(eng, fn, list(extra), [dg])

            lam = ar.alloc([128, 2, 24]); ldt = ar.alloc([128, 24]); Bt = ar.alloc([128, 2, 24, 16])
            Ct = ar.alloc([128, 2, 24, 16]); mv = ar.alloc([128, 9, 24]); dsk = ar.alloc([128, 24])
            ld("sp", lam[:], lam_d[:, l], dg); ld("sp", ldt[:], ldt_d[:, l], dg)
            ld("sp", Bt[:], b_d[:, l], dg); ld("sp", Ct[:], c_d[:, l], dg)
            ld("sp", mv[:], mv_d[:, :, :], dg); ld("sp", dsk[:], dsk_d[:, l], dg)
            dtt = ar.alloc([128, 24]); zr = ar.alloc([128, 24]); zi = ar.alloc([128, 24])
            G(ACTF(dtt[:], ldt[:], AF.Exp), "act")
            G(TT(zr[:], lam[:, 0], dtt[:], ALU.mult)); G(TT(zi[:], lam[:, 1], dtt[:], ALU.mult))
            sh = [128, 9, 24]
            ang = ar.alloc(sh); mzr = ar.alloc(sh); E = ar.alloc(sh); Ei = ar.alloc(sh)
            t1 = ar.alloc(sh); r1 = ar.alloc(sh); sn = ar.alloc(sh); cs = ar.alloc(sh)
            PWr = ar.alloc(sh); PWi = ar.alloc(sh); NWr = ar.alloc(sh); NWi = ar.alloc(sh)
            zb = lambda z: z[:].unsqueeze(1).to_broadcast(sh)
            G(TT(ang[:], mv[:], zb(zi), ALU.mult)); G(TT(mzr[:], mv[:], zb(zr), ALU.mult))
            G(ACTF(E[:], mzr[:], AF.Exp), "act"); G(ACTF(Ei[:], mzr[:], AF.Exp, scale=-1.0), "act")
            a2 = ar.alloc(sh)
            for (dst, shift) in ((sn, 0.0), (cs, PI / 2)):
                G(TS(a2[:], ang[:], shift, None, ALU.add))
                G(TS(t1[:], a2[:], 1.0 / (2 * PI), MAGIC, ALU.mult, ALU.add))
                G(TS(t1[:], t1[:], -MAGIC, None, ALU.add))
                G(STT(r1[:], t1[:], -2 * PI, a2[:], ALU.mult, ALU.add))
                G(TS(r1[:], r1[:], -3.14159, 3.14159, ALU.max, ALU.min))
                G(ACTF(dst[:], r1[:], AF.Sin), "act")
            G(TT(PWr[:], E[:], cs[:], ALU.mult)); G(TT(PWi[:], E[:], sn[:], ALU.mult))
            G(TT(NWr[:], Ei[:], cs[:], ALU.mult))
            kb.op("dve", CP(A8[:, l, 0, :], PWr[:, 8, :]), [dg], [d_A8])
            kb.op("dve", CP(A8[:, l, 1, :], PWi[:, 8, :]), [dg], [d_A8])
            G(STT(NWi[:], Ei[:], -1.0, sn[:], ALU.mult, ALU.mult))
            s24 = [128, 24]
            nr = ar.alloc(s24); den = ar.alloc(s24); ta = ar.alloc(s24); tb = ar.alloc(s24)
            kr = ar.alloc(s24); ki = ar.alloc(s24)
            G(TS(nr[:], PWr[:, 1, :], -1.0, None, ALU.add))
            G(TT(den[:], lam[:, 0], lam[:, 0], ALU.mult)); G(TT(ta[:], lam[:, 1], lam[:, 1], ALU.mult))
            G(TT(den[:], den[:], ta[:], ALU.add)); G(RCP(den[:], den[:]))
            G(TT(ta[:], nr[:], lam[:, 0], ALU.mult)); G(TT(tb[:], PWi[:, 1, :], lam[:, 1], ALU.mult))
            G(TT(ta[:], ta[:], tb[:], ALU.add)); G(TT(kr[:], ta[:], den[:], ALU.mult))
            G(TT(ta[:], PWi[:, 1, :], lam[:, 0], ALU.mult)); G(TT(tb[:], nr[:], lam[:, 1], ALU.mult))
            G(TT(ta[:], ta[:], tb[:], ALU.subtract)); G(TT(ki[:], ta[:], den[:], ALU.mult))
            sB = [128, 24, 16]
            Bbr = ar.alloc(sB); Bbi = ar.alloc(sB); u1 = ar.alloc(sB); u2 = ar.alloc(sB)
            kbc = lambda k: k[:].unsqueeze(2).to_broadcast(sB)
            G(TT(u1[:], Bt[:, 0], kbc(kr), ALU.mult)); G(TT(u2[:], Bt[:, 1], kbc(ki), ALU.mult))
            G(TT(Bbr[:], u1[:], u2[:], ALU.subtract))
            G(TT(u1[:], Bt[:, 1], kbc(kr), ALU.mult)); G(TT(u2[:], Bt[:, 0], kbc(ki), ALU.mult))
            G(TT(Bbi[:], u1[:], u2[:], ALU.add))
            s8 = [128, 24, 8, 16]
            MBT = ar.alloc([128, 2, 24, 8, 16], BF16); QN = ar.alloc([128, 2, 24, 8, 16], BF16)
            MC = ar.alloc([128, 2, 24, 8, 16], BF16)
            w1 = ar.alloc(s8); w2 = ar.alloc(s8)

            def pw(tile, p0, m0_, mstep):
                return tile.cu(p0, 64, m0_ * 24, [[1, 24], [mstep * 24, 8], [0, 16]])

            def bc8(tile3, ri, p0):
                base = ri * 24 * 16 if ri is not None else 0
                return tile3.cu(p0, 64, base, [[16, 24], [0, 8], [1, 16]])

            def cmul(out, ar_, ai_, br_, bi_, p0, neg_im=False):
                o_r = out.cu(p0, 64, 0, [[128, 24], [16, 8], [1, 16]])
                o_i = out.cu(p0, 64, 24 * 128, [[128, 24], [16, 8], [1, 16]])
                a = w1.cu(p0, 64, 0, [[128, 24], [16, 8], [1, 16]])
                b = w2.cu(p0, 64, 0, [[128, 24], [16, 8], [1, 16]])
                G(TT(a, ar_, br_, ALU.mult)); G(TT(b, ai_, bi_, ALU.mult)); G(TT(o_r, a, b, ALU.subtract))
                G(TT(a, ar_, bi_, ALU.mult)); G(TT(b, ai_, br_, ALU.mult))
                if neg_im:
                    G(STT(o_i, a, -1.0, b, ALU.mult, ALU.subtract))
                else:
                    G(TT(o_i, a, b, ALU.add))

            for p0, (mb0, mbs), (qn0, qns), (mc0, mcs) in ((0, (7, -1), (7, -1), (1, 1)), (64, (0, 1), (0, 1), (8, -1))):
                cmul(MBT, pw(PWr, p0, mb0, mbs), pw(PWi, p0, mb0, mbs), bc8(Bbr, None, p0), bc8(Bbi, None, p0), p0)
                cmul(QN, pw(NWr, p0, qn0, qns), pw(NWi, p0, qn0, qns), bc8(Ct, 0, p0), bc8(Ct, 1, p0), p0, neg_im=True)
                cmul(MC, pw(PWr, p0, mc0, mcs), pw(PWi, p0, mc0, mcs), bc8(Ct, 0, p0), bc8(Ct, 1, p0), p0, neg_im=True)
            mats = ar.alloc([128, 24, 7, 128], BF16); d_mats = Dep()
            tz1 = ar.alloc([128, 128]); tz2 = ar.alloc([128, 128]); d_tz = Dep()
            kb.op("dve", MS(mats[:, :, 3:7, :], 0.0), [], [d_mats])
            for di in range(2):
                kb.op("dve", CP(mats.cu(64 * di, 64, (3 + di) * 128, [[2 * 128, 2], [7 * 128, 24], [1, 128]]),
                                MC.cu(64 * di, 64, 0, [[24 * 128, 2], [128, 24], [1, 128]])), [dg], [d_mats])
            KSKIP = os.environ.get('KSKIP', '')
            for g in range(24 if 'grp' not in KSKIP else 0):
                pt, pd = P()
                pt2, pd2 = P()
                fs = []
                for di in range(2):
                    for ri in range(2):
                        fs.append(MM((pt if di == 0 else pt2)[:, 0:128],
                                     MBT.cu(di * 64, 64, (ri * 24 + g) * 128, [[1, 128]]),
                                     QN.cu(di * 64, 64, (ri * 24 + g) * 128, [[1, 128]]),
                                     start=(ri == 0), stop=(ri == 1)))
                for ri in range(2):
                    fs.append(MM(pt[:, 256 + ri * 128:256 + (ri + 1) * 128],
                                 MBT.cu(0, 128, (ri * 24 + g) * 128, [[1, 128]]), ident_bf[:]))
                if 'gpe' not in KSKIP:
                    kb.op("pe", SEQ(fs), [dg, d_ident], [pd, pd2])
                if 'gdve' in KSKIP:
                    continue
                kb.op("dve", TT(tz1[:], pt[:, 0:128], maskf[:], ALU.mult), [pd, d_mask], [d_tz])
                kb.op("dve", TT(tz2[:], pt2[:, 0:128], maskb[:], ALU.mult), [pd2, d_mask], [d_tz])
                kb.op("dve", TT(tz1[:], tz1[:], tz2[:], ALU.add), [d_tz], [d_tz])
                kb.op("dve", STT(mats[:, g, 0, :], ident[:], dsk[:, g:g + 1], tz1[:], ALU.mult, ALU.add),
                      [d_tz, dg, d_ident], [d_mats])
                kb.op("dve", CP(mats[:, g, 1:3, :], pt[:, 256:512].rearrange("p (r x) -> p r x", r=2)),
                      [pd], [d_mats])
            if 'dma' not in KSKIP:
                kb.dma("sp", ssm_scr[l], mats[:].rearrange("p g r x -> p (g r x)"), reads=[d_mats], writes=[Dep()])
            kb.barrier()
            ar.release(m1)


    kb.interleave([stream_precast, stream_adaln, stream_gen], [6, 2, 10])
    kb.barrier()
    ar.release(mC)

    xT = ar.alloc([128, NCH, S]); d_x = [Dep() for _ in range(N5)]
    hT = ar.alloc([128, NCH, S], BF16); d_h = [Dep() for _ in range(N5)]
    ccT = hT
    d_out = []
    j5 = lambda j: slice(j * 512, (j + 1) * 512)
    evq = [0]

    def evac(out, in_, reads, writes, func=AF.Copy):
        evq[0] += 1
        if evq[0] % 2:
            kb.op("act", ACTF(out, in_, func), reads, writes)
        else:
            kb.op("dve", CP(out, in_), reads, writes)

    def rms_stats(src_sq_fn, j, tmp_sq, d_sq, rstd, d_rstd):
        pt, pd = P()
        kb.op("pe", SEQ([MM(pt[:, :], ones_bf[:], tmp_sq[:, c_, :], start=(c_ == 0), stop=(c_ == NCH - 1))
                         for c_ in range(NCH)]), [d_sq, d_ones], [pd])
        kb.op("act", ACTF(rstd[:], pt[:, :], AF.Sqrt, bias=eps_t[:], scale=1.0 / D), [pd, d_eps], [d_rstd])
        kb.op("dve", RCP(rstd[:], rstd[:]), [d_rstd], [d_rstd])

    def pre_norm(l, i, b, tmps, js=None):
        sq, d_sq, rstd, d_rstd, tmp, d_tmp = tmps
        sh0 = 0 if i == 0 else 24
        for j in (range(N5) if js is None else js):
            kb.op("act", ACTF(sq[:], xT[:, :, j5(j)], AF.Square), [d_x[j]], [d_sq])
            rms_stats(None, j, sq, d_sq, rstd, d_rstd)
            for c_ in range(NCH):
                kb.op("dve", STT(tmp[:], xT[:, c_, j5(j)], Am[:, l, i, c_, b:b + 1], rstd[:], ALU.mult, ALU.mult),
                      [d_x[j], d_Am, d_rstd], [d_tmp])
                kb.op("act", ACTF(hT[:, c_, j5(j)], tmp[:], AF.Identity, bias=modT[:, l, sh0 + c_, b:b + 1]),
                      [d_tmp, d_mod], [d_h[j]])

    def post_norm_residual(l, i, b, j, y_sb, d_y, sq, d_sq, rstd, d_rstd, tmp, d_tmp):
        rms_stats(None, j, sq, d_sq, rstd, d_rstd)
        for c_ in range(NCH):
            tb_, dtb_ = (tmp, d_tmp) if c_ % 2 == 0 else (tmp2, d_tmp2)
            kb.op("dve", STT(tb_[:], y_sb[:, c_, :], Gm[:, l, i, c_, b:b + 1], rstd[:], ALU.mult, ALU.mult),
                  [d_y, d_Gm, d_rstd], [dtb_])
            kb.op("pool", TT(xT[:, c_, j5(j)], xT[:, c_, j5(j)], tb_[:], ALU.add), [dtb_], [d_x[j]])

    def proj_post(l, i, b, w_sb, d_w, nk, rhs_fn, rhs_deps_fn):
        y_sb = ar.alloc([128, NCH, 512]); d_y = Dep()
        sq = ar.alloc([128, NCH, 512], BF16); d_sq = Dep()
        rstd = ar.alloc([128, 512]); d_rstd = Dep()
        tmp = ar.alloc([128, 512]); d_tmp = Dep()
        for j in range(N5):
            import os
            for oc in range(NCH if os.environ.get('KDBG2') != 'nomm' else 0):
                pt, pd = P()
                kb.op("pe", SEQ([MM(pt[:, :], w_sb(kt, oc), rhs_fn(kt, j), start=(kt == 0), stop=(kt == nk - 1))
                                 for kt in range(nk)]), [d_w] + rhs_deps_fn(j), [pd])
                kb.op("dve", CP(y_sb[:, oc, :], pt[:, :]), [pd], [d_y])
                kb.op("act", ACTF(sq[:, oc, :], y_sb[:, oc, :], AF.Square), [d_y], [d_sq])
            import os
            if os.environ.get('KDBG') != 'wo1':
                post_norm_residual(l, i, b, j, y_sb, d_y, sq, d_sq, rstd, d_rstd, tmp, d_tmp)

    for s in range(2):
        for c_ in range(NCH):
            kb.dma("sp", xT[:, c_, :], xT_d[s, :, c_, :], writes=d_x)
        import os
        for l in range(nlayers if os.environ.get('KDBG') != 'ada' else 0):
            mk0 = ar.mark()
            uT = ar.alloc([128, 3, S], BF16); fT = ar.alloc([128, 2, S], BF16)
            mkA = ar.mark()
            qT = ar.alloc([128, 5, S], BF16)
            v_sb = ar.alloc([128, NT, 2, 2, 128], BF16)
            mkW = ar.mark()
            w_in_sb = ar.alloc([128, NCH, WIN_COLS], BF16); d_win = Dep()
            d_z = Dep(); d_v = Dep()
            pn_t = (ar.alloc([128, NCH, 512], BF16), Dep(), ar.alloc([128, 512]), Dep(), ar.alloc([128, 512]), Dep())
            for kt in range(NCH):
                kb.dma("sp", w_in_sb[:, kt, :], win_b[l, :, kt, :], writes=[d_win])
            kb.op("pool", MS(v_sb[:], 0.0), [], [d_v])
            for j in range(N5):
                pre_norm(l, 0, s, pn_t, [j])
                for oc in range(10):
                    dst = uT[:, oc] if oc < 3 else (fT[:, oc - 3] if oc < 5 else qT[:, oc - 5])
                    pt, pd = P()
                    kb.op("pe", SEQ([MM(pt[:, :], w_in_sb[:, kt, oc * 128:(oc + 1) * 128], hT[:, kt, j5(j)],
                                        start=(kt == 0), stop=(kt == NCH - 1)) for kt in range(NCH)]),
                          [d_win, d_h[j]], [pd])
                    if 5 <= oc < 8:
                        kb.op("act", ACTF(dst[:, j5(j)], pt[:, :], AF.Identity, scale=0.125), [pd], [d_z])
                    else:
                        evac(dst[:, j5(j)], pt[:, :], [pd], [d_z])
                for tb in range(4 * j, 4 * j + 4):
                    pt, pd = P()
                    kb.op("pe", SEQ([MM(pt[:, 0:128], hT[:, kt, tb * 128:(tb + 1) * 128], w_in_sb[:, kt, 1280:1408],
                                        start=(kt == 0), stop=(kt == NCH - 1)) for kt in range(NCH)]),
                          [d_win, d_h[tb // 4]], [pd])
                    pv = pt[:, 0:128].rearrange("p (k d) -> p k d", k=2)
                    kb.op("act", ACTF(v_sb[:, tb, :, 0, 0:64], pv, AF.Copy), [pd], [d_v])
                    kb.op("dve", CP(v_sb[:, tb, :, 1, 64:128], pv), [pd], [d_v])
            kb.barrier()
            ar.release(mkW)
            if os.environ.get('KDBG') == 'win':
                ar.release(mk0)
                continue
            d_cc = d_h
            if "att" in parts:
                biasT = ar.alloc([128, 6, 3, 128], BF16); d_bias = Dep()
                ld("sp", biasT[:], biasT_d[:, :, :, :], d_bias)
                onesLR = ar.alloc([128, 2, 128], BF16); d_olr = Dep()
                kb.op("pool", MS(onesLR[:], 0.0), [], [d_olr])
                kb.op("pool", MS(onesLR[:, 0, 0:64], 1.0), [], [d_olr])
                kb.op("pool", MS(onesLR[:, 1, 64:128], 1.0), [], [d_olr])
                NAB = 3
                scb = [ar.alloc([128, 3, 128]) for _ in range(2 * NAB)]; d_scb = [Dep() for _ in range(2 * NAB)]
                pTb = [ar.alloc([128, 2, 3, 128], BF16) for _ in range(NAB)]; d_pT = [Dep() for _ in range(NAB)]
                dnb = [ar.alloc([128, 128]) for _ in range(2)]; d_dnb = [Dep(), Dep()]
                blocks = [(jp, n) for jp in range(3) for n in range(NT)]

                def att_stage1(it, jp, n):
                    kbs = [k_ for k_ in range(3) if 0 <= n + k_ - 1 < NT]
                    k0, k1 = kbs[0], kbs[-1] + 1
                    pb = it % NAB
                    for hh in range(2):
                        h = 2 * jp + hh
                        kv = h // 3
                        pt, pd = P()
                        fs = []
                        for k_ in kbs:
                            fs.append(MM(pt[:, k_ * 128:(k_ + 1) * 128],
                                         qT[64 * hh:64 * hh + 64, 3 + kv, (n + k_ - 1) * 128:(n + k_) * 128],
                                         qT[64 * hh:64 * hh + 64, jp, n * 128:(n + 1) * 128], start=True, stop=False))
                            fs.append(MM(pt[:, k_ * 128:(k_ + 1) * 128], ident_bf[:], biasT[:, h, k_, :], start=False, stop=True))
                        kb.op("pe", SEQ(fs), [d_z, d_bias, d_ident], [pd])
                        kb.op("act", ACTF(pTb[pb][:, hh, k0:k1, :],
                                          pt[:, k0 * 128:k1 * 128].rearrange("p (k q) -> p k q", q=128), AF.Exp),
                              [pd], [d_pT[pb]])

                def att_stage2(it, jp, n):
                    kbs = [k_ for k_ in range(3) if 0 <= n + k_ - 1 < NT]
                    pb = it % NAB
                    dn, d_dn = dnb[it % 2], d_dnb[it % 2]
                    pt, pd = P()
                    fs = []
                    for gi in range(2):
                        pairs = [(hh, k_) for hh in range(2) for k_ in kbs]
                        for ii, (hh, k_) in enumerate(pairs):
                            lhs = v_sb[:, n + k_ - 1, (2 * jp + hh) // 3, hh, :] if gi == 0 else onesLR[:, hh, :]
                            fs.append(MM(pt[:, gi * 128:(gi + 1) * 128], lhs, pTb[pb][:, hh, k_, :],
                                         start=(ii == 0), stop=(ii == len(pairs) - 1)))
                    kb.op("pe", SEQ(fs), [d_pT[pb], d_v, d_olr], [pd])
                    kb.op("dve", TS(dn[:], pt[:, 128:256], esink[:, l, jp:jp + 1], None, ALU.add), [pd, d_sink], [d_dn])
                    kb.op("dve", RCP(dn[:], dn[:]), [d_dn], [d_dn])
                    kb.op("dve", TT(ccT[:, 5 + jp, n * 128:(n + 1) * 128], pt[:, 0:128], dn[:], ALU.mult),
                          [pd, d_dn], [d_cc[n // 4]])

                for it in range(len(blocks) + 1):
                    if it < len(blocks):
                        att_stage1(it, *blocks[it])
                    if it >= 1:
                        att_stage2(it - 1, *blocks[it - 1])
            else:
                for j in range(N5):
                    kb.op("dve", MS(ccT[:, 5:8, j5(j)], 0.0), [], [d_cc[j]])
            kb.barrier()
            ar.release(mkA)
            use_fft = "fft" in parts
            use_ssm = "ssm" in parts and os.environ.get('KDBG') != 'ssmgen'
            if use_ssm:
                mkS = ar.mark()
                selW = ar.alloc([128, 8, 240], BF16); d_sel = Dep()
                ld("sp", selW[:], selW_d[:, :, :], d_sel)
                wglu = ar.alloc([128, 3, 384], BF16); d_wg = Dep()
                kb.dma("sp", wglu[:].rearrange("p k c -> p (k c)"), wglu_b[l], writes=[d_wg])
                Ug = ar.alloc([128, 24, K8], BF16); d_Ug = Dep()
                Xs = ar.alloc([128, K8, 2, 24], BF16); d_Xs = Dep()
                mb = [ar.alloc([128, 7, 128], BF16) for _ in range(2)]; d_mb = [Dep() for _ in range(2)]
                mi_ = [0]
                AA = ar.alloc([128, 2, 24]); AB = ar.alloc([128, 2, 24]); d_A = Dep()
                BL = 32
                NBk = K8 // BL
                shp = [128, NBk, 2, 24]
                Fp = [ar.alloc(shp) for _ in range(2)]; d_Fp = [Dep(), Dep()]
                d_Xo = Dep()
                s1 = ar.alloc(shp); s2_ = ar.alloc(shp); d_st = Dep()
                tB0 = ar.alloc(shp); tB1 = ar.alloc(shp); Cc = ar.alloc(shp)
                Pq = [ar.alloc([128, 2, 24]) for _ in range(2)]; Pw = [ar.alloc([128, 2, 24]) for _ in range(2)]
                w_a = ar.alloc([128, 2, 24]); w_b = ar.alloc([128, 2, 24])
                AAb = ar.alloc([128, 2, 24]); ABb = ar.alloc([128, 2, 24])

            def fft_section():
                mkF = ar.mark()
                c64 = ar.alloc([64, 128]); s64 = ar.alloc([64, 128]); wf = ar.alloc([64, 4, 64]); d_fc = Dep()
                ld("sp", c64[:], c64_d[:, :], d_fc); ld("sp", s64[:], s64_d[:, :], d_fc)
                ld("sp", wf[:], w_fft_d[l], d_fc)
                W2 = ar.alloc([128, 2, 2, 2, 64], BF16); d_W2 = Dep()
                kb.op("pool", MS(W2[:], 0.0), [], [d_W2])
                for h in range(4):
                    pt, pd = P()
                    kb.op("pe", SEQ([MM(pt[:, 0:64], c64[:], wf[:, h, :]), MM(pt[:, 64:128], s64[:], wf[:, h, :])]),
                          [d_fc], [pd])
                    hh = h % 2
                    kb.op("dve", CP(W2[64 * hh:64 * hh + 64, h // 2, :, hh, :],
                                    pt[64 * hh:64 * hh + 64, 0:128].rearrange("p (c e) -> p c e", c=2)), [pd], [d_W2])
                G_sb = ar.alloc([128, NT, 2, 256], BF16); d_G = Dep()
                for tb in range(NT):
                    pt, pd = P()
                    kb.op("pe", SEQ([MM(pt[:, jj * 256:(jj + 1) * 256], fT[:, jj, tb * 128:(tb + 1) * 128],
                                        W2[:, jj].rearrange("p c h e -> p (c h e)")) for jj in range(2)]),
                          [d_z, d_W2], [pd])
                    evac(G_sb[:, tb].rearrange("p j x -> p (j x)"), pt[:, :], [pd], [d_G])
                PG = min(4, NT)
                dbuf = [ar.alloc([128, PG, 512], BF16) for _ in range(2)]; d_db = [Dep() for _ in range(2)]
                di = 0
                for j in range(N5):
                    acc = [P(hold=True), P(hold=True)]
                    first = True
                    for cs_ in range(2):
                        for pg in range(NT // PG):
                            bi = di % 2
                            di += 1
                            ld("sp", dbuf[bi][:], dft_d[cs_, pg * PG * 128:(pg + 1) * PG * 128, j5(j)]
                               .rearrange("(a p) x -> p a x", p=128), d_db[bi])
                            last = (cs_ == 1 and pg == NT // PG - 1)
                            for jj in range(2):
                                kb.op("pe", SEQ([MM(acc[jj][0][:, :], G_sb[:, pg * PG + a, jj, cs_ * 128:(cs_ + 1) * 128],
                                                    dbuf[bi][:, a, :], start=(first and a == 0),
                                                    stop=(last and a == PG - 1)) for a in range(PG)]),
                                      [d_G, d_db[bi]], [acc[jj][1]])
                            first = False
                    for jj in range(2):
                        evac(ccT[:, 3 + jj, j5(j)], acc[jj][0][:, :], [acc[jj][1]], [d_cc[j]])
                        Prelease(acc[jj])
                kb.barrier()
                ar.release(mkF)

            def ssmA_section():
                for g in range(24):
                    ch, g8 = g // 8, g % 8
                    pt, pd = P()
                    kb.op("pe", SEQ([MM(pt[:, 0:K8], selW[:, g8, 112 - 16 * s2:112 - 16 * s2 + 128],
                                        uT.cu(0, 128, ch * S + s2, [[8, K8]]), start=(s2 == 0), stop=(s2 == 7))
                                     for s2 in range(8)]), [d_sel, d_z], [pd])
                    evac(Ug[:, g, :], pt[:, 0:K8], [pd], [d_Ug])
                    bi = mi_[0] % 2
                    mi_[0] += 1
                    ld("sp", mb[bi][:], ssm_scr[l, :, g * 896:(g + 1) * 896].rearrange("p (r x) -> p r x", r=7), d_mb[bi])
                    pt, pd = P()
                    kb.op("pe", SEQ([MM(pt[:, ri * K8:(ri + 1) * K8], mb[bi][:, 1 + ri, :], Ug[:, g, :]) for ri in range(2)]),
                          [d_mb[bi], d_Ug], [pd])
                    if g % 2:
                        kb.op("act", ACTF(Xs.cu(0, 64, g, [[24, 2], [48, K8]]),
                                          pt[0:64, 0:2 * K8].rearrange("p (r k) -> p r k", r=2), AF.Copy), [pd], [d_Xs])
                        kb.op("act", ACTF(Xs.cu(64, 64, (K8 - 1) * 48 + g, [[24, 2], [-48, K8]]),
                                          pt[64:128, 0:2 * K8].rearrange("p (r k) -> p r k", r=2), AF.Copy), [pd], [d_Xs])
                    else:
                        kb.op("dve", CP(Xs.cu(0, 64, g, [[24, 2], [48, K8]]),
                                        pt[0:64, 0:2 * K8].rearrange("p (r k) -> p r k", r=2)), [pd], [d_Xs])
                        kb.op("dve", CP(Xs.cu(64, 64, (K8 - 1) * 48 + g, [[24, 2], [-48, K8]]),
                                        pt[64:128, 0:2 * K8].rearrange("p (r k) -> p r k", r=2)), [pd], [d_Xs])
                if DBG and s == 0 and l == 0:
                    kb.dma("sp", dbg_ug[:, :], Ug[:].rearrange("p g k -> p (g k)"), reads=[d_Ug], writes=[Dep()])
                    kb.dma("sp", dbg_x0[:, :], Xs[:].rearrange("p k r g -> p (k r g)"), reads=[d_Xs], writes=[Dep()])
                    kb.dma("sp", dbg_u[:, :], uT[:].rearrange("p c t -> p (c t)"), reads=[d_z], writes=[Dep()])
                    kb.barrier()
                kb.op("dve", CP(AA[:, 0, :], A8[:, l, 0, :]), [d_A8], [d_A]); kb.op("dve", CP(AA[:, 1, :], A8[:, l, 0, :]), [d_A8], [d_A])
                kb.op("dve", TS(AB[:, 0, :], A8[:, l, 1, :], -1.0, None, ALU.mult), [d_A8], [d_A])
                kb.op("dve", CP(AB[:, 1, :], A8[:, l, 1, :]), [d_A8], [d_A])
                def Xv(i, b0=0):
                    return Xs.cu(0, 128, (b0 * BL + i) * 48, [[BL * 48, NBk - b0], [24, 2], [1, 24]])

                def bc(t, nb):
                    return t.cu(0, 128, 0, [[0, nb], [24, 2], [1, 24]])

                def bcsw(t, nb):
                    return t.cu(0, 128, 24, [[0, nb], [-24, 2], [1, 24]])

                def sw4(t, nb):
                    return t.cu(0, 128, 24, [[48, nb], [-24, 2], [1, 24]])

                kb.op("dve", CP(Fp[0][:], Xv(0)), [d_Xs], [d_Fp[0]])
                for i_ in range(1, BL):
                    Fo, Fn = Fp[(i_ - 1) % 2], Fp[i_ % 2]
                    do, dn_ = d_Fp[(i_ - 1) % 2], d_Fp[i_ % 2]
                    kb.op("dve", TT(s1[:], bc(AA, NBk), Fo[:], ALU.mult), [d_A, do], [d_st])
                    kb.op("dve", TT(s2_[:], bc(AB, NBk), sw4(Fo, NBk), ALU.mult), [do], [d_st])
                    kb.op("dve", TT(s1[:], s1[:], s2_[:], ALU.add), [d_st], [d_st])
                    kb.op("dve", TT(Fn[:], s1[:], Xv(i_), ALU.add), [d_st, d_Xs], [dn_])
                    kb.op("act", ACTF(Xv(i_), Fn[:], AF.Copy), [dn_], [d_Xo])
                if NBk > 1:
                    Fl, d_Fl = Fp[(BL - 1) % 2], d_Fp[(BL - 1) % 2]
                    d_pq = Dep()
                    kb.op("dve", CP(Pq[0][:], A8[:, l]), [d_A8], [d_pq])
                    nsq = BL.bit_length() - 1
                    for q_ in range(nsq):
                        po, pn = Pq[q_ % 2], Pq[(q_ + 1) % 2]
                        kb.op("dve", TT(w_a[:, 0, :], po[:, 0, :], po[:, 0, :], ALU.mult), [d_pq], [d_pq])
                        kb.op("dve", TT(w_a[:, 1, :], po[:, 1, :], po[:, 1, :], ALU.mult), [d_pq], [d_pq])
                        kb.op("dve", TT(pn[:, 0, :], w_a[:, 0, :], w_a[:, 1, :], ALU.subtract), [d_pq], [d_pq])
                        kb.op("dve", STT(pn[:, 1, :], po[:, 0, :], 2.0, po[:, 1, :], ALU.mult, ALU.mult), [d_pq], [d_pq])
                    pBL = Pq[nsq % 2]
                    kb.op("dve", CP(AAb[:, 0, :], pBL[:, 0, :]), [d_pq], [d_pq]); kb.op("dve", CP(AAb[:, 1, :], pBL[:, 0, :]), [d_pq], [d_pq])
                    kb.op("dve", TS(ABb[:, 0, :], pBL[:, 1, :], -1.0, None, ALU.mult), [d_pq], [d_pq])
                    kb.op("dve", CP(ABb[:, 1, :], pBL[:, 1, :]), [d_pq], [d_pq])
                    d_Cc = Dep()
                    kb.op("dve", CP(Cc[:, 0], Fl[:, 0]), [d_Fl], [d_Cc])
                    for b_ in range(1, NBk):
                        cprev_sw = Cc.cu(0, 128, (b_ - 1) * 48 + 24, [[-24, 2], [1, 24]])
                        kb.op("dve", TT(w_a[:], AAb[:], Cc[:, b_ - 1], ALU.mult), [d_pq, d_Cc], [d_pq])
                        kb.op("dve", TT(w_b[:], ABb[:], cprev_sw, ALU.mult), [d_Cc], [d_pq])
                        kb.op("dve", TT(w_a[:], w_a[:], w_b[:], ALU.add), [d_pq], [d_pq])
                        kb.op("dve", TT(Cc[:, b_], w_a[:], Fl[:, b_], ALU.add), [d_pq, d_Fl], [d_Cc])
                    nb1 = NBk - 1
                    CA, CB = s1, s2_
                    d_CAB = Dep()
                    cp_r = Cc.cu(0, 128, 0, [[48, nb1], [0, 2], [1, 24]])
                    kb.op("dve", CP(CA.cu(0, 128, 0, [[48, nb1], [24, 2], [1, 24]]), cp_r), [d_Cc, d_st], [d_CAB])
                    kb.op("dve", TS(CB.cu(0, 128, 0, [[48, nb1], [1, 24]]), Cc.cu(0, 128, 24, [[48, nb1], [1, 24]]), -1.0, None, ALU.mult),
                          [d_Cc, d_st], [d_CAB])
                    kb.op("dve", CP(CB.cu(0, 128, 24, [[48, nb1], [1, 24]]), Cc.cu(0, 128, 24, [[48, nb1], [1, 24]])), [d_Cc], [d_CAB])
                    d_pw = Dep()
                    kb.op("dve", CP(Pw[0][:], A8[:, l]), [d_A8], [d_pw])
                    tA = [Fp[0], Fp[1]]; tB = [tB0, tB1]; d_tA = [Dep(), Dep()]; d_tB = [Dep(), Dep()]
                    for i_ in range(BL):
                        pw = Pw[i_ % 2]
                        ta, tb_ = tA[i_ % 2], tB[i_ % 2]
                        tav = ta.cu(0, 128, 0, [[48, nb1], [24, 2], [1, 24]])
                        tbv = tb_.cu(0, 128, 0, [[48, nb1], [24, 2], [1, 24]])
                        kb.op("dve", TT(tav, CA.cu(0, 128, 0, [[48, nb1], [24, 2], [1, 24]]), bc(pw, nb1), ALU.mult),
                              [d_CAB, d_pw], [d_tA[i_ % 2], d_Fp[i_ % 2]])
                        kb.op("pool", TT(tbv, CB.cu(0, 128, 0, [[48, nb1], [24, 2], [1, 24]]), bcsw(pw, nb1), ALU.mult),
                              [d_CAB, d_pw], [d_tB[i_ % 2]])
                        kb.op("dve", TT(tav, tav, tbv, ALU.add), [d_tB[i_ % 2]], [d_tA[i_ % 2]])
                        kb.op("dve", TT(Xv(i_, 1), Xv(i_, 1), tav, ALU.add), [d_tA[i_ % 2], d_Xs, d_Xo], [d_Xo])
                        if i_ + 1 < BL:
                            pn = Pw[(i_ + 1) % 2]
                            pw_sw = pw.cu(0, 128, 24, [[-24, 2], [1, 24]])
                            kb.op("dve", TT(w_a[:], AA[:], pw[:], ALU.mult), [d_A, d_pw], [d_pq])
                            kb.op("dve", TT(w_b[:], AB[:], pw_sw, ALU.mult), [d_pw], [d_pq])
                            kb.op("dve", TT(pn[:], w_a[:], w_b[:], ALU.add), [d_pq], [d_pw])

            streams, quanta = [], []
            if use_fft:
                streams.append(fft_section); quanta.append(1)
            else:
                for j in range(N5):
                    kb.op("dve", MS(ccT[:, 3:5, j5(j)], 0.0), [], [d_cc[j]])

            if use_ssm:
                streams.append(ssmA_section); quanta.append(8)
            if streams:
                kb.interleave(streams, quanta)
            if use_ssm:
                unselW = ar.alloc([128, 8, 240], BF16)
                ld("sp", unselW[:], unselW_d[:, :, :], d_sel)

                Yb = [ar.alloc([128, K8], BF16) for _ in range(2)]; d_Yb = [Dep(), Dep()]
                yg = ar.alloc([128, 3, S], BF16); d_yg = Dep()
                for ch in range(3):
                    accs = [P(hold=True) for _ in range(N5)]
                    for j in range(N5):
                        kb.op("dve", MS(accs[j][0][:, :], 0.0), [], [accs[j][1]])
                    for g8 in range(8):
                        g = ch * 8 + g8
                        bi = mi_[0] % 2
                        mi_[0] += 1
                        ld("sp", mb[bi][:], ssm_scr[l, :, g * 896:(g + 1) * 896].rearrange("p (r x) -> p r x", r=7), d_mb[bi])
                        pt, pd = P()
                        fs = [MM(pt[:, 0:K8], mb[bi][:, 0, :], Ug[:, g, :], start=True, stop=False)]
                        for ri in range(2):
                            fs.append(MM(pt[:, 1:K8], mb[bi][:, 3 + 2 * ri, :], Xs.cu(0, 128, ri * 24 + g, [[48, K8 - 1]]),
                                         start=False, stop=False))
                            fs.append(MM(pt[:, 0:K8 - 1], mb[bi][:, 4 + 2 * ri, :],
                                         Xs.cu(0, 128, (K8 - 2) * 48 + ri * 24 + g, [[-48, K8 - 1]]),
                                         start=False, stop=(ri == 1)))
                        kb.op("pe", SEQ(fs), [d_mb[bi], d_Ug, d_Xs, d_Xo], [pd])
                        yb = g % 2
                        evac(Yb[yb][:], pt[:, 0:K8], [pd], [d_Yb[yb]])
                        for j in range(N5):
                            kb.op("pe", SEQ([MMZ(bass.AP(tensor=accs[j][0], offset=t2, ap=[[512, 128], [8, 64]]),
                                                 unselW[:, t2, 112 - 16 * g8:112 - 16 * g8 + 128],
                                                 Yb[yb][:, j * 64:(j + 1) * 64]) for t2 in range(8)]),
                                  [d_Yb[yb], d_sel], [accs[j][1]])
                    for j in range(N5):
                        kb.op("act", ACTF(yg[:, ch, j5(j)], accs[j][0][:, :], AF.Gelu_apprx_tanh), [accs[j][1]], [d_yg])
                        Prelease(accs[j])
                if DBG and s == 0 and l == 0:
                    kb.dma("sp", dbg_xs[:, :], Xs[:].rearrange("p k r g -> p (k r g)"), reads=[d_Xs, d_Xo], writes=[Dep()])
                    kb.dma("sp", dbg_yg[:, :], yg[:].rearrange("p c t -> p (c t)"), reads=[d_yg], writes=[Dep()])
                    kb.barrier()
                sg = ar.alloc([128, 512], BF16); d_sg = Dep()
                for oc in range(3):
                    for j in range(N5):
                        pt, pd = P()
                        kb.op("pe", SEQ([MM(pt[:, :], wglu[:, kt, oc * 128:(oc + 1) * 128], yg[:, kt, j5(j)],
                                            start=(kt == 0), stop=(kt == 2)) for kt in range(3)]), [d_wg, d_yg], [pd])
                        kb.op("act", ACTF(sg[:], pt[:, :], AF.Sigmoid), [pd], [d_sg])
                        kb.op("dve", TT(ccT[:, oc, j5(j)], yg[:, oc, j5(j)], sg[:], ALU.mult), [d_sg, d_yg], [d_cc[j]])
                ar.release(mkS)
            else:
                for j in range(N5):
                    kb.op("dve", MS(ccT[:, 0:3, j5(j)], 0.0), [], [d_cc[j]])
            kb.barrier()
            ar.release(mk0)
            mkO = ar.mark()
            w_out_sb = ar.alloc([128, NCH, D], BF16); d_wo = Dep()
            for kt in range(NCH):
                kb.dma("sp", w_out_sb[:, kt, :], wout_b[l, :, kt, :], writes=[d_wo])
            proj_post(l, 0, s, lambda kt, oc: w_out_sb[:, kt, oc * 128:(oc + 1) * 128], d_wo, NCH,
                      lambda kt, j: ccT[:, kt, j5(j)], lambda j: [d_cc[j]])
            kb.barrier()
            ar.release(mkO)
            if "ffn" in parts:
                mkN = ar.mark()
                actT = ar.alloc([128, NFC, 512], BF16)
                NB = 3
                upr = [ar.alloc([128, 2, 514]) for _ in range(NB)]
                d_upr = [[Dep(), Dep()] for _ in range(NB)]
                cen = [ar.alloc([128, 2, 512]) for _ in range(NB)]
                d_cen = [[Dep(), Dep()] for _ in range(NB)]
                gl = [ar.alloc([128, 512]) for _ in range(2)]; d_gl = [Dep(), Dep()]
                wu = [ar.alloc([128, NCH, 256], BF16) for _ in range(3)]; d_wu = [Dep() for _ in range(3)]
                wd = [ar.alloc([128, D], BF16) for _ in range(3)]; d_wd = [Dep() for _ in range(3)]
                y_sb = ar.alloc([128, NCH, 512]); d_y = Dep()
                sq = ar.alloc([128, NCH, 512], BF16); d_sq = Dep()
                rstd = ar.alloc([128, 512]); d_rstd = Dep()
                tmp = ar.alloc([128, 512]); d_tmp = Dep()
                pn_f = (sq, d_sq, rstd, d_rstd, tmp, d_tmp)
                d_act = [Dep() for _ in range(NFC)]
                wi_ = 0
                ci_ = 0
                gcnt = [0]

                def ffn_tail(cc_, ub):
                    gb = gcnt[0] % 2
                    gcnt[0] += 1
                    ce = cen[ub]
                    kb.op("act", ACTF(gl[gb][:], ce[:, 0, :], AF.Gelu_apprx_tanh), [d_cen[ub][0]], [d_gl[gb]])
                    kb.op("pool", TT(actT[:, cc_, :], gl[gb][:], ce[:, 1, :], ALU.mult), [d_gl[gb], d_cen[ub][1]], [d_act[cc_]])

                for j in range(N5):
                    t0 = j * 512
                    pend = []
                    pre_norm(l, 1, s, pn_f, [0, 1][:N5] if j == 0 else ([j + 1] if j + 1 < N5 else []))
                    for cc_ in range(NFC):
                        bi = wi_ % 3
                        wi_ += 1
                        ub = ci_ % NB
                        ci_ += 1
                        kb.dma("sp", wu[bi][:].rearrange("p k c -> p (k c)"), wup_b[l, cc_], writes=[d_wu[bi]])
                        u_ = upr[ub]
                        for hv in range(2):
                            pt, pd = P()
                            kb.op("pe", SEQ([MM(pt[:, :], wu[bi][:, kt, hv * 128:(hv + 1) * 128], hT[:, kt, j5(j)],
                                                start=(kt == 0), stop=(kt == NCH - 1)) for kt in range(NCH)]),
                                  [d_wu[bi], d_h[j]], [pd])
                            kb.op("act", ACTF(u_[:, hv, 1:513], pt[:, :], AF.Copy), [pd], [d_upr[ub][hv]])
                        toks = [t_ for t_ in (t0 - 1, t0 + 512) if 0 <= t_ < S]
                        if t0 - 1 < 0:
                            kb.op("pool", MS(u_[:, :, 0:1], 0.0), [], d_upr[ub])
                        if t0 + 512 >= S:
                            kb.op("pool", MS(u_[:, :, 513:514], 0.0), [], d_upr[ub])
                        if toks:
                            nh = len(toks)
                            pt, pd = P()
                            fs = []
                            for hv in range(2):
                                for kt in range(NCH):
                                    rhs = hT.cu(0, 128, kt * S + toks[0], [[513, nh]])
                                    fs.append(MM(pt[:, hv * 2:hv * 2 + nh], wu[bi][:, kt, hv * 128:(hv + 1) * 128], rhs,
                                                 start=(kt == 0), stop=(kt == NCH - 1)))
                            kb.op("pe", SEQ(fs), [d_wu[bi]] + d_h, [pd])
                            for ti, t_ in enumerate(toks):
                                col = 0 if t_ == t0 - 1 else 513
                                kb.op("dve", CP(u_[:, :, col:col + 1], bass.AP(tensor=pt, offset=ti, ap=[[512, 128], [2, 2], [1, 1]])),
                                      [pd], d_upr[ub])
                        ce = cen[ub]
                        for hv in range(2):
                            ci = hv * NFC + cc_
                            dd = [d_cen[ub][hv]]
                            kb.op("act", ACTF(ce[:, hv, :], u_[:, hv, 1:513], AF.Identity, bias=cb[:, l, ci:ci + 1],
                                              scale=cw[:, l, 1, ci:ci + 1]), [d_upr[ub][hv], d_cw, d_cb], dd)
                            kb.op("dve", STT(ce[:, hv, :], u_[:, hv, 0:512], cw[:, l, 0, ci:ci + 1], ce[:, hv, :], ALU.mult, ALU.add),
                                  [d_upr[ub][hv], d_cw], dd)
                            kb.op("dve", STT(ce[:, hv, :], u_[:, hv, 2:514], cw[:, l, 2, ci:ci + 1], ce[:, hv, :], ALU.mult, ALU.add),
                                  [d_upr[ub][hv], d_cw], dd)
                        pend.append((cc_, ub))
                        if len(pend) > 1:
                            ffn_tail(*pend.pop(0))
                    while pend:
                        ffn_tail(*pend.pop(0))
                    accs = [P(hold=True) for _ in range(NCH)]
                    for cc_ in range(NFC):
                        bi = wi_ % 3
                        wi_ += 1
                        kb.dma("sp", wd[bi][:], wdn_b[l, cc_], writes=[d_wd[bi]])
                        for oc in range(NCH):
                            kb.op("pe", MM(accs[oc][0][:, :], wd[bi][:, oc * 128:(oc + 1) * 128], actT[:, cc_, :],
                                           start=(cc_ == 0), stop=(cc_ == NFC - 1)), [d_wd[bi], d_act[cc_]], [accs[oc][1]])
                    for oc in range(NCH):
                        kb.op("dve", CP(y_sb[:, oc, :], accs[oc][0][:, :]), [accs[oc][1]], [d_y])
                        kb.op("act", ACTF(sq[:, oc, :], y_sb[:, oc, :], AF.Square), [d_y], [d_sq])
                    for a_ in accs:
                        Prelease(a_)
                    post_norm_residual(l, 1, s, j, y_sb, d_y, sq, d_sq, rstd, d_rstd, tmp, d_tmp)
                kb.barrier()
                ar.release(mkN)
        for c_ in range(NCH):
            dd = Dep()
            d_out.append(dd)
            kb.dma("sp", outT_d[s, :, c_, :], xT[:, c_, :], reads=d_x, writes=[dd])
    kb.wait_all("sp", d_out)
    kb.emit(st)
    st.close()
    print("built: %d ops, arena peak %d KiB" % (kb.ninst, ar.peak // 1024))
    return nc


def _consts(S):
    bf = ml_dtypes.bfloat16
    c = {}
    c["ident"] = np.eye(128, dtype=np.float32)
    sp = np.arange(128) // 16
    c["maskf"] = (sp[None, :] >= sp[:, None]).astype(np.float32)
    c["maskb"] = (sp[None, :] <= sp[:, None]).astype(np.float32)
    selW = np.zeros((128, 8, 240), np.float32)
    for g8 in range(8):
        for cc in range(16):
            selW[16 * g8 + cc, g8, cc + 112] = 1.0
    c["selW"] = selW.astype(bf)
    unselW = np.zeros((128, 8, 240), np.float32)
    for t2 in range(8):
        for cc in range(16):
            unselW[16 * t2 + cc, t2, cc + 112] = 1.0
    c["unselW"] = unselW.astype(bf)
    slopes = np.exp2(-8.0 * np.arange(1, 7, dtype=np.float64) / 6)
    j = np.arange(128)[:, None, None]
    kb_ = np.arange(3)[None, :, None]
    i = np.arange(128)[None, None, :]
    dist = np.abs(i + 128 - (kb_ * 128 + j))
    bias = np.where(dist[:, None] <= 128, -slopes[None, :, None, None] * dist[:, None], -1e30)
    c["biasT"] = bias.astype(np.float32).astype(bf)
    a = np.arange(64)
    th = 2 * np.pi * np.outer(a, a) / 64
    c64 = np.cos(th) / 8.0
    s64 = -np.sin(th) / 8.0
    c["c64d"] = np.concatenate([c64, c64], 1).astype(np.float32)
    c["s64d"] = np.concatenate([s64, s64], 1).astype(np.float32)
    c["mvals"] = np.broadcast_to(np.arange(9, dtype=np.float32)[None, :, None], (128, 9, 24)).copy()
    p = np.arange(S, dtype=np.int64)
    th = 2 * np.pi * ((p[:, None] * p[None, :]) % S) / S
    c["dft"] = np.stack([np.cos(th), np.sin(th)]).astype(np.float32) / np.sqrt(S)
    c["dft"] = c["dft"].astype(bf)
    return c


def _prep_shared(inp, S):
    f = lambda a: np.ascontiguousarray(np.asarray(a, dtype=np.float32))
    d = {}
    d["w_ada"] = f(inp["w_ada"]).reshape(L, NCH, 128, 6 * D)
    d["b_adaT"] = f(f(inp["b_ada"]).reshape(L, 48, 128).transpose(2, 0, 1))
    g = np.stack([f(inp[k]) for k in ("g_pre_mix", "g_post_mix", "g_pre_ffn", "g_post_ffn")])
    d["gT"] = f(g.reshape(4, L, NCH, 128).transpose(3, 1, 0, 2))
    w = f(inp["w_in"])
    wn = np.concatenate([w[:, :, 0:1024], w[:, :, 1024:1088], w[:, :, 1024:1088], w[:, :, 1088:1152],
                         w[:, :, 1088:1152], w[:, :, 1152:1280]], axis=2)
    d["w_in"] = f(wn.reshape(L, NCH, 128, WIN_COLS).transpose(0, 2, 1, 3))
    lam = np.stack([f(inp["lam_re"]), f(inp["lam_im"])])
    d["lamT"] = f(lam.transpose(2, 4, 1, 0, 3).reshape(128, L, 2, 24))
    ldt = f(inp["log_dt"])
    d["ldt"] = f(np.broadcast_to(ldt.transpose(1, 0, 2)[:, None], (2, 64, L, 24)).reshape(128, L, 24))
    b = np.stack([f(inp["b_re"]), f(inp["b_im"])])
    d["bT"] = f(b.transpose(2, 4, 1, 0, 3, 5).reshape(128, L, 2, 24, 16))
    cc = np.stack([f(inp["c_re"]), f(inp["c_im"])])
    cc = cc.transpose(4, 1, 0, 2, 3)
    d["cTs"] = f(np.concatenate([cc, cc], 0))
    dsk = f(inp["d_skip"]).reshape(L, 24, 16)
    d["dsk"] = f(np.tile(dsk.transpose(2, 0, 1), (8, 1, 1)))
    d["w_glu"] = f(f(inp["w_glu"]).reshape(L, 3, 128, 384).transpose(0, 2, 1, 3))
    d["w_fft"] = f(f(inp["w_fft"]).transpose(0, 2, 1, 3))
    sk = f(inp["sink"]).reshape(L, 3, 2)
    d["sinkT"] = f(np.repeat(sk.transpose(2, 0, 1), 64, axis=0))
    d["w_out"] = f(f(inp["w_out"]).reshape(L, NCH, 128, D).transpose(0, 2, 1, 3))
    wu = f(inp["w_up"]).reshape(L, NCH, 128, 2, NFC, 128)
    d["w_up"] = f(wu.transpose(0, 4, 2, 1, 3, 5).reshape(L, NFC, 128, NCH, 256))
    d["conv_w"] = f(f(inp["conv_w"]).reshape(L, 3, 2, NFC, 128).transpose(4, 0, 1, 2, 3).reshape(128, L, 3, 2 * NFC))
    d["conv_b"] = f(f(inp["conv_b"]).reshape(L, 2, NFC, 128).transpose(3, 0, 1, 2).reshape(128, L, 2 * NFC))
    d["w_down"] = f(inp["w_down"]).reshape(L, NFC, 128, D)
    d.update(_consts(S))
    return d


_NC_CACHE = {}


def run_cores(inp, S, n_cores, parts=("ssm", "fft", "att", "ffn"), nlayers=L):
    key = (S, tuple(parts), nlayers)
    if key not in _NC_CACHE:
        _NC_CACHE[key] = build(S, parts, nlayers)
    nc = _NC_CACHE[key]
    shared = _prep_shared(inp, S)
    x = np.asarray(inp["x"], dtype=np.float32)
    c = np.asarray(inp["c"], dtype=np.float32)
    in_maps = []
    for i in range(n_cores):
        xb = x[2 * i:2 * i + 2]
        m = dict(shared)
        m["xT"] = np.ascontiguousarray(xb.transpose(0, 2, 1).reshape(2, NCH, 128, S).transpose(0, 2, 1, 3))
        m["cT"] = np.ascontiguousarray(c[2 * i:2 * i + 2].reshape(2, NCH, 128).transpose(2, 1, 0))
        in_maps.append(m)
    res = run_bass_kernel_spmd(nc, in_maps, core_ids=list(range(n_cores)))
    outs = []
    for r in res.results:
        o = np.asarray(r["outT"])
        outs.append(o.transpose(0, 3, 2, 1).reshape(2, S, D))
    return np.concatenate(outs, 0).astype(np.float32)


def kernel(**inputs):
    S = inputs["x"].shape[1]
    return run_cores(inputs, S, 8)
```

```python
import math
import threading
import numpy as np
import ml_dtypes
import concourse.bass as bass
import concourse.mybir as mybir
from concourse.bass_utils import run_bass_kernel_spmd

F32 = mybir.dt.float32
BF16 = mybir.dt.bfloat16
AF = mybir.ActivationFunctionType
ALU = mybir.AluOpType
AX = mybir.AxisListType

ENGS = ("pe", "act", "dve", "pool", "sp")
N_DMA_SEMS = 12


class Dep:
    __slots__ = ("w", "r", "name")

    def __init__(self, name=""):
        self.w = None
        self.r = {}
        self.name = name


class KB:
    def __init__(self, nc):
        self.nc = nc
        self.ops = {e: [] for e in ENGS}
        self.cnt = {e: 0 for e in ENGS}
        self.seen = {e: {} for e in ENGS}
        self.sems = {}
        self.dma_cnt = [0] * N_DMA_SEMS
        self.dma_rr = 0
        self.ninst = 0
        self._tick = {}

    def _need(self, reads, writes):
        need = {}

        def add(tok):
            if tok is None:
                return
            k, v = tok
            if need.get(k, 0) < v:
                need[k] = v

        for d in reads:
            add(d.w)
        for d in writes:
            add(d.w)
            for k, v in d.r.items():
                add((k, v))
        return need

    def _emit_waits(self, eng, need):
        seen = self.seen[eng]
        waits = []
        for k, v in need.items():
            if k == "pe" and eng == "pe":
                continue
            if seen.get(k, 0) >= v:
                continue
            seen[k] = v
            waits.append((k, v))
        return waits

    def _update(self, tok, reads, writes):
        k, v = tok
        for d in reads:
            if d.r.get(k, 0) < v:
                d.r[k] = v
        for d in writes:
            d.w = tok
            d.r = {}

    def op(self, eng, fn, reads=(), writes=()):
        need = self._need(reads, writes)
        waits = self._emit_waits(eng, need)
        self.cnt[eng] += 1
        tok = (eng, self.cnt[eng])
        self.seen[eng][eng] = max(self.seen[eng].get(eng, 0), 0)
        sems = self.sems

        def run(e, waits=waits, fn=fn, eng=eng):
            for k, v in waits:
                e.wait_ge(sems[k], v)
            ins = fn(e)
            ins.then_inc(sems[eng], 1)

        self.ops[eng].append(run)
        self._update(tok, reads, writes)
        self.ninst += 1
        f = self._tick.get(threading.get_ident())
        if f:
            f()
        return tok

    def dma(self, q, out, in_, reads=(), writes=(), **kw):
        i = self.dma_rr
        self.dma_rr = (i + 1) % N_DMA_SEMS
        key = "d%d" % i
        need = self._need(reads, writes)
        if self.dma_cnt[i] > 0:
            if need.get(key, 0) < self.dma_cnt[i]:
                need[key] = self.dma_cnt[i]
        waits = self._emit_waits(q, need)
        self.dma_cnt[i] += 16
        tok = (key, self.dma_cnt[i])
        sems = self.sems

        def run(e, waits=waits, out=out, in_=in_, key=key, kw=kw):
            for k, v in waits:
                e.wait_ge(sems[k], v)
            e.dma_start(out=out, in_=in_, **kw).then_inc(sems[key], 16)

        self.ops[q].append(run)
        self._update(tok, reads, writes)
        self.ninst += 1
        f = self._tick.get(threading.get_ident())
        if f:
            f()
        return tok

    def interleave(self, funcs, quanta):
        n = len(funcs)
        if n == 1:
            funcs[0]()
            return
        cond = threading.Condition()
        st = {"turn": 0, "alive": [True] * n, "cnt": 0, "err": None}

        def nxt():
            t = st["turn"]
            for k in range(1, n + 1):
                c = (t + k) % n
                if st["alive"][c]:
                    st["turn"] = c
                    return
            st["turn"] = -1

        def tick(i):
            st["cnt"] += 1
            if st["cnt"] >= quanta[i]:
                st["cnt"] = 0
                with cond:
                    nxt()
                    cond.notify_all()
                    while st["turn"] != i:
                        cond.wait()

        def worker(i):
            with cond:
                while st["turn"] != i:
                    cond.wait()
            self._tick[threading.get_ident()] = (lambda i=i: tick(i))
            try:
                funcs[i]()
            except BaseException as e:
                st["err"] = e
            finally:
                self._tick.pop(threading.get_ident(), None)
                with cond:
                    st["alive"][i] = False
                    st["cnt"] = 0
                    if st["turn"] == i:
                        nxt()
                    cond.notify_all()

        ths = [threading.Thread(target=worker, args=(i,)) for i in range(n)]
        for t in ths:
            t.start()
        for t in ths:
            t.join()
        if st["err"] is not None:
            raise st["err"]

    def wait_all(self, eng, deps):
        need = self._need((), deps)
        waits = self._emit_waits(eng, need)
        sems = self.sems

        def run(e, waits=waits):
            for k, v in waits:
                e.wait_ge(sems[k], v)

        self.ops[eng].append(run)

    def barrier(self):
        need = {e: self.cnt[e] for e in ENGS if e != "sp" and self.cnt[e] > 0}
        for i in range(N_DMA_SEMS):
            if self.dma_cnt[i] > 0:
                need["d%d" % i] = self.dma_cnt[i]
        sems = self.sems
        for eng in ENGS:
            waits = self._emit_waits(eng, dict(need))

            def run(e, waits=waits):
                for k, v in waits:
                    e.wait_ge(sems[k], v)

            self.ops[eng].append(run)

    def emit(self, stack):
        nc = self.nc
        for e in ("pe", "act", "dve", "pool"):
            self.sems[e] = stack.enter_context(nc.semaphore("sem_" + e))
        for i in range(N_DMA_SEMS):
            self.sems["d%d" % i] = stack.enter_context(nc.semaphore("sem_d%d" % i))
        block = stack.enter_context(nc.Block())
        ops = self.ops

        @block.tensor
        def _(e):
            for f in ops["pe"]:
                f(e)

        @block.scalar
        def _(e):
            for f in ops["act"]:
                f(e)

        @block.vector
        def _(e):
            for f in ops["dve"]:
                f(e)

        @block.gpsimd
        def _(e):
            for f in ops["pool"]:
                f(e)

        @block.sync
        def _(e):
            for f in ops["sp"]:
                f(e)


def MM(out, lhsT, rhs, start=True, stop=True):
    return lambda e: e.matmul(out, lhsT=lhsT, rhs=rhs, start=start, stop=stop)


def MMZ(out, lhsT, rhs):
    return lambda e: e.matmul(out, lhsT=lhsT, rhs=rhs, start=False, stop=False, skip_group_check=True)


def TR(out, in_, ident):
    return lambda e: e.transpose(out, in_, ident)


def SEQ(fs):
    def run(e):
        r = None
        for f in fs:
            r = f(e)
        return r
    return run


def ACTF(out, in_, func, bias=0.0, scale=1.0, accum=None):
    if accum is None:
        return lambda e: e.activation(out=out, in_=in_, func=func, bias=bias, scale=scale)
    return lambda e: e.activation(out=out, in_=in_, func=func, bias=bias, scale=scale, accum_out=accum)


def TT(out, a, b, op):
    return lambda e: e.tensor_tensor(out=out, in0=a, in1=b, op=op)


def TS(out, a, s1, s2=None, op0=ALU.mult, op1=None):
    if s2 is None:
        return lambda e: e.tensor_scalar(out=out, in0=a, scalar1=s1, scalar2=None, op0=op0)
    return lambda e: e.tensor_scalar(out=out, in0=a, scalar1=s1, scalar2=s2, op0=op0, op1=op1)


def STT(out, in0, scalar, in1, op0, op1):
    return lambda e: e.scalar_tensor_tensor(out=out, in0=in0, scalar=scalar, in1=in1, op0=op0, op1=op1)


def CP(out, in_):
    return lambda e: e.tensor_copy(out=out, in_=in_)


def MS(out, val):
    return lambda e: e.memset(out, val)


def RCP(out, in_):
    return lambda e: e.reciprocal(out=out, in_=in_)


class Tl:
    def __init__(self, h, base, shape):
        self.h = h
        self.base = base
        self.shape = list(shape)
        self.row = h.shape[1]
        n = 1
        for d in shape[1:]:
            n *= d
        self.n = n
        ap = h[0:shape[0], base:base + n]
        if len(shape) > 2:
            names = "abcdefg"[:len(shape) - 1]
            kw = {names[i]: shape[i + 1] for i in range(1, len(names))}
            ap = ap.rearrange("p (%s) -> p %s" % (" ".join(names), " ".join(names)), **kw)
        self.full = ap

    def __getitem__(self, idx):
        return self.full[idx]

    def cu(self, p0, npart, off, dims):
        return bass.AP(tensor=self.h, offset=p0 * self.row + self.base + off,
                       ap=[[self.row, npart]] + [list(d) for d in dims])


class Arena:
    def __init__(self, h32, nbytes):
        self.h = {F32: h32, BF16: h32.bitcast(BF16)}
        self.nbytes = nbytes
        self.top = 0
        self.peak = 0

    def alloc(self, shape, dt=F32):
        esz = 4 if dt == F32 else 2
        n = 1
        for d in shape[1:]:
            n *= d
        nb = (n * esz + 31) // 32 * 32
        off = self.top
        self.top += nb
        self.peak = max(self.peak, self.top)
        assert self.top <= self.nbytes, ("SBUF arena overflow", self.top, self.nbytes)
        return Tl(self.h[dt], off // esz, shape)

    def mark(self):
        return self.top

    def release(self, m):
        self.top = m


D = 1024
NCH = 8
DFF = 2816
NFC = 22
L = 2
WIN_COLS = 1408
PI = math.pi


def build(S, parts=("ssm", "fft", "att", "ffn"), nlayers=L):
    import contextlib
    import os
    nc = bass.Bass("TRN2", target_bir_lowering=False)
    kb = KB(nc)
    NT = S // 128
    N5 = S // 512
    K8 = S // 8
    ins = {}

    def din(name, shape, dt=F32):
        ins[name] = nc.dram_tensor(name, list(shape), dt, kind="ExternalInput").ap()
        return ins[name]

    xT_d = din("xT", [2, 128, NCH, S])
    cT_d = din("cT", [128, NCH, 2])
    w_ada_d = din("w_ada", [L, NCH, 128, 6 * D])
    b_ada_d = din("b_adaT", [128, L, 48])
    g_d = din("gT", [128, L, 4, NCH])
    w_in_d = din("w_in", [L, 128, NCH, WIN_COLS])
    lam_d = din("lamT", [128, L, 2, 24])
    ldt_d = din("ldt", [128, L, 24])
    b_d = din("bT", [128, L, 2, 24, 16])
    c_d = din("cTs", [128, L, 2, 24, 16])
    dsk_d = din("dsk", [128, L, 24])
    w_glu_d = din("w_glu", [L, 128, 3, 384])
    w_fft_d = din("w_fft", [L, 64, 4, 64])
    sink_d = din("sinkT", [128, L, 3])
    w_out_d = din("w_out", [L, 128, NCH, D])
    w_up_d = din("w_up", [L, NFC, 128, NCH, 256])
    cw_d = din("conv_w", [128, L, 3, 2 * NFC])
    cb_d = din("conv_b", [128, L, 2 * NFC])
    w_down_d = din("w_down", [L, NFC, 128, D])
    ident_d = din("ident", [128, 128])
    maskf_d = din("maskf", [128, 128])
    maskb_d = din("maskb", [128, 128])
    selW_d = din("selW", [128, 8, 240], BF16)
    unselW_d = din("unselW", [128, 8, 240], BF16)
    biasT_d = din("biasT", [128, 6, 3, 128], BF16)
    c64_d = din("c64d", [64, 128])
    s64_d = din("s64d", [64, 128])
    mv_d = din("mvals", [128, 9, 24])
    dft_d = din("dft", [2, S, S], BF16)
    outT_d = nc.dram_tensor("outT", [2, 128, NCH, S], F32, kind="ExternalOutput").ap()
    DBG = os.environ.get('KDBG') == 'ssmdbg'
    if DBG:
        dbg_ug = nc.dram_tensor("dbg_ug", [128, 24 * (S // 8)], BF16, kind="ExternalOutput").ap()
        dbg_x0 = nc.dram_tensor("dbg_x0", [128, (S // 8) * 48], BF16, kind="ExternalOutput").ap()
        dbg_xs = nc.dram_tensor("dbg_xs", [128, (S // 8) * 48], BF16, kind="ExternalOutput").ap()
        dbg_yg = nc.dram_tensor("dbg_yg", [128, 3 * S], BF16, kind="ExternalOutput").ap()
        dbg_u = nc.dram_tensor("dbg_u", [128, 3 * S], BF16, kind="ExternalOutput").ap()
    ssm_scr = nc.dram_tensor("ssm_scr", [L, 128, 24 * 7 * 128], BF16, kind="Internal").ap()

    def dscr(name, shape):
        return nc.dram_tensor(name, list(shape), BF16, kind="Internal").ap()

    win_b = dscr("win_b", [L, 128, NCH, WIN_COLS])
    wout_b = dscr("wout_b", [L, 128, NCH, D])
    wup_b = dscr("wup_b", [L, NFC, 128, NCH * 256])
    wdn_b = dscr("wdn_b", [L, NFC, 128, D])
    wglu_b = dscr("wglu_b", [L, 128, 3 * 384])
    st = contextlib.ExitStack()
    ARENA_BYTES = 212800
    arena_h = st.enter_context(nc.sbuf_tensor("arena", [128, ARENA_BYTES // 4], F32))
    ar = Arena(arena_h, ARENA_BYTES)
    psum = []
    for i in range(8):
        psum.append((st.enter_context(nc.psum_tensor("ps%d" % i, [128, 512], F32)), Dep()))
    prr = [0]

    held = set()

    def P(hold=False):
        while prr[0] in held:
            prr[0] = (prr[0] + 1) % 8
        i = prr[0]
        prr[0] = (i + 1) % 8
        if hold:
            held.add(i)
        return psum[i]

    def Prelease(pp):
        for i in range(8):
            if psum[i] is pp:
                held.discard(i)

    def ld(q, tile_ap, dram_ap, dep, **kw):
        kb.dma(q, tile_ap, dram_ap, writes=[dep], **kw)

    ident = ar.alloc([128, 128]); d_ident = Dep()
    ld("sp", ident[:], ident_d[:, :], d_ident)
    ident_bf = ar.alloc([128, 128], BF16)
    kb.op("dve", CP(ident_bf[:], ident[:]), [d_ident], [d_ident])
    ones_bf = ar.alloc([128, 128], BF16); d_ones = Dep()
    kb.op("dve", MS(ones_bf[:], 1.0), [], [d_ones])
    cT = ar.alloc([128, NCH, 2]); d_cT = Dep()
    ld("sp", cT[:], cT_d[:, :, :], d_cT)
    b_adaT = ar.alloc([128, L, 48]); d_bada = Dep()
    ld("sp", b_adaT[:], b_ada_d[:, :, :], d_bada)
    gT = ar.alloc([128, L, 4, NCH]); d_gT = Dep()
    ld("sp", gT[:], g_d[:, :, :, :], d_gT)
    modT = ar.alloc([128, L, 48, 2]); d_mod = Dep()
    Am = ar.alloc([128, L, 2, NCH, 2]); d_Am = Dep()
    Gm = ar.alloc([128, L, 2, NCH, 2]); d_Gm = Dep()
    cw = ar.alloc([128, L, 3, 2 * NFC]); d_cw = Dep()
    ld("sp", cw[:], cw_d[:, :, :, :], d_cw)
    cb = ar.alloc([128, L, 2 * NFC]); d_cb = Dep()
    ld("sp", cb[:], cb_d[:, :, :], d_cb)
    sinkT = ar.alloc([128, L, 3]); d_sink = Dep()
    ld("sp", sinkT[:], sink_d[:, :, :], d_sink)
    esink = ar.alloc([128, L, 3])
    kb.op("act", ACTF(esink[:], sinkT[:], AF.Exp), [d_sink], [d_sink])
    tmp2 = ar.alloc([128, 512]); d_tmp2 = Dep()
    eps_t = ar.alloc([128, 1]); d_eps = Dep()
    kb.op("dve", MS(eps_t[:], 1e-6), [], [d_eps])

    A8 = ar.alloc([128, L, 2, 24]); d_A8 = Dep()
    d_wscr = Dep()
    mC = ar.mark()
    NSTG = 3
    stg = [ar.alloc([128, 2048]) for _ in range(NSTG)]
    stb = [ar.alloc([128, 2048], BF16) for _ in range(NSTG)]
    d_stg = [Dep() for _ in range(NSTG)]
    d_stb = [Dep() for _ in range(NSTG)]
    stg_i = [0]

    def stream_precast():
        chunks = []
        for l in range(nlayers):
            for kt in range(NCH):
                chunks.append((win_b[l, :, kt, :], w_in_d[l, :, kt, :], WIN_COLS))
                chunks.append((wout_b[l, :, kt, :], w_out_d[l, :, kt, :], D))
            chunks.append((wglu_b[l], w_glu_d[l].rearrange("p k c -> p (k c)"), 3 * 384))
            if "ffn" in parts:
                for cc_ in range(NFC):
                    chunks.append((wup_b[l, cc_], w_up_d[l, cc_].rearrange("p k c -> p (k c)"), NCH * 256))
                    chunks.append((wdn_b[l, cc_], w_down_d[l, cc_], D))

        def load(ci):
            dst_ap, src_ap, n = chunks[ci]
            kb.dma("sp", stg[ci % NSTG][:, 0:n], src_ap, writes=[d_stg[ci % NSTG]])

        for ci in range(min(2, len(chunks))):
            load(ci)
        for ci, (dst_ap, src_ap, n) in enumerate(chunks):
            i = ci % NSTG
            if ci % 2:
                kb.op("act", ACTF(stb[i][:, 0:n], stg[i][:, 0:n], AF.Copy), [d_stg[i]], [d_stb[i]])
            else:
                kb.op("dve", CP(stb[i][:, 0:n], stg[i][:, 0:n]), [d_stg[i]], [d_stb[i]])
            kb.dma("sp", dst_ap, stb[i][:, 0:n], reads=[d_stb[i]], writes=[Dep()])
            if ci + 2 < len(chunks):
                load(ci + 2)

    m0 = ar.mark()
    wa = [ar.alloc([128, 3072]) for _ in range(2)]
    d_wa = [Dep(), Dep()]

    def stream_adaln():
        wi = 0
        for l in range(nlayers):
            acc_ = P(hold=True)
            pt, pd = acc_
            kb.op("dve", MS(pt[:, 0:96], 0.0), [], [pd])
            for kt in range(NCH):
                for hf in range(2):
                    b = wi % 2
                    wi += 1
                    ld("sp", wa[b][:], w_ada_d[l, kt, :, hf * 3072:(hf + 1) * 3072], d_wa[b])
                    fs = []
                    for m in range(24):
                        mm = hf * 24 + m
                        fs.append(MMZ(pt[:, mm * 2:mm * 2 + 2], wa[b][:, m * 128:(m + 1) * 128], cT[:, kt, :]))
                    kb.op("pe", SEQ(fs), [d_wa[b], d_cT], [pd])
            kb.op("dve", TT(modT[:, l], pt[:, 0:96].rearrange("p (m b) -> p m b", b=2),
                            b_adaT[:, l, :].unsqueeze(2).to_broadcast([128, 48, 2]), ALU.add),
                  [pd, d_bada], [d_mod])
            Prelease(acc_)
            for i, (sc0, gt0, gpre, gpost) in enumerate(((8, 16, 0, 1), (32, 40, 2, 3))):
                kb.op("dve", STT(Am[:, l, i], modT[:, l, sc0:sc0 + 8, :], 1.0,
                                 gT[:, l, gpre, :].unsqueeze(2).to_broadcast([128, NCH, 2]), ALU.add, ALU.mult),
                      [d_mod, d_gT], [d_Am])
                kb.op("dve", TT(Gm[:, l, i], modT[:, l, gt0:gt0 + 8, :],
                                gT[:, l, gpost, :].unsqueeze(2).to_broadcast([128, NCH, 2]), ALU.mult),
                      [d_mod, d_gT], [d_Gm])


    MAGIC = 12582912.0
    if "ssm" in parts:
        maskf = ar.alloc([128, 128]); maskb = ar.alloc([128, 128]); d_mask = Dep()
        ld("sp", maskf[:], maskf_d[:, :], d_mask)
        ld("sp", maskb[:], maskb_d[:, :], d_mask)

    def stream_gen():
        if "ssm" not in parts:
            return
        for l in range(nlayers):
            m1 = ar.mark()
            dg = Dep()

            def G(fn, eng="dve", extra=()):
                kb.op(eng, fn, list(extra), [dg])

            lam = ar.alloc([128, 2, 24]); ldt = ar.alloc([128, 24]); Bt = ar.alloc([128, 2, 24, 16])
            Ct = ar.alloc([128, 2, 24, 16]); mv = ar.alloc([128, 9, 24]); dsk = ar.alloc([128, 24])
            ld("sp", lam[:], lam_d[:, l], dg); ld("sp", ldt[:], ldt_d[:, l], dg)
            ld("sp", Bt[:], b_d[:, l], dg); ld("sp", Ct[:], c_d[:, l], dg)
            ld("sp", mv[:], mv_d[:, :, :], dg); ld("sp", dsk[:], dsk_d[:, l], dg)
            dtt = ar.alloc([128, 24]); zr = ar.alloc([128, 24]); zi = ar.alloc([128, 24])
            G(ACTF(dtt[:], ldt[:], AF.Exp), "act")
            G(TT(zr[:], lam[:, 0], dtt[:], ALU.mult)); G(TT(zi[:], lam[:, 1], dtt[:], ALU.mult))
            sh = [128, 9, 24]
            ang = ar.alloc(sh); mzr = ar.alloc(sh); E = ar.alloc(sh); Ei = ar.alloc(sh)
            t1 = ar.alloc(sh); r1 = ar.alloc(sh); sn = ar.alloc(sh); cs = ar.alloc(sh)
            PWr = ar.alloc(sh); PWi = ar.alloc(sh); NWr = ar.alloc(sh); NWi = ar.alloc(sh)
            zb = lambda z: z[:].unsqueeze(1).to_broadcast(sh)
            G(TT(ang[:], mv[:], zb(zi), ALU.mult)); G(TT(mzr[:], mv[:], zb(zr), ALU.mult))
            G(ACTF(E[:], mzr[:], AF.Exp), "act"); G(ACTF(Ei[:], mzr[:], AF.Exp, scale=-1.0), "act")
            a2 = ar.alloc(sh)
            for (dst, shift) in ((sn, 0.0), (cs, PI / 2)):
                G(TS(a2[:], ang[:], shift, None, ALU.add))
                G(TS(t1[:], a2[:], 1.0 / (2 * PI), MAGIC, ALU.mult, ALU.add))
                G(TS(t1[:], t1[:], -MAGIC, None, ALU.add))
                G(STT(r1[:], t1[:], -2 * PI, a2[:], ALU.mult, ALU.add))
                G(TS(r1[:], r1[:], -3.14159, 3.14159, ALU.max, ALU.min))
                G(ACTF(dst[:], r1[:], AF.Sin), "act")
            G(TT(PWr[:], E[:], cs[:], ALU.mult)); G(TT(PWi[:], E[:], sn[:], ALU.mult))
            G(TT(NWr[:], Ei[:], cs[:], ALU.mult))
            kb.op("dve", CP(A8[:, l, 0, :], PWr[:, 8, :]), [dg], [d_A8])
            kb.op("dve", CP(A8[:, l, 1, :], PWi[:, 8, :]), [dg], [d_A8])
            G(STT(NWi[:], Ei[:], -1.0, sn[:], ALU.mult, ALU.mult))
            s24 = [128, 24]
            nr = ar.alloc(s24); den = ar.alloc(s24); ta = ar.alloc(s24); tb = ar.alloc(s24)
            kr = ar.alloc(s24); ki = ar.alloc(s24)
            G(TS(nr[:], PWr[:, 1, :], -1.0, None, ALU.add))
            G(TT(den[:], lam[:, 0], lam[:, 0], ALU.mult)); G(TT(ta[:], lam[:, 1], lam[:, 1], ALU.mult))
            G(TT(den[:], den[:], ta[:], ALU.add)); G(RCP(den[:], den[:]))
            G(TT(ta[:], nr[:], lam[:, 0], ALU.mult)); G(TT(tb[:], PWi[:, 1, :], lam[:, 1], ALU.mult))
            G(TT(ta[:], ta[:], tb[:], ALU.add)); G(TT(kr[:], ta[:], den[:], ALU.mult))
            G(TT(ta[:], PWi[:, 1, :], lam[:, 0], ALU.mult)); G(TT(tb[:], nr[:], lam[:, 1], ALU.mult))
            G(TT(ta[:], ta[:], tb[:], ALU.subtract)); G(TT(ki[:], ta[:], den[:], ALU.mult))
            sB = [128, 24, 16]
            Bbr = ar.alloc(sB); Bbi = ar.alloc(sB); u1 = ar.alloc(sB); u2 = ar.alloc(sB)
            kbc = lambda k: k[:].unsqueeze(2).to_broadcast(sB)
            G(TT(u1[:], Bt[:, 0], kbc(kr), ALU.mult)); G(TT(u2[:], Bt[:, 1], kbc(ki), ALU.mult))
            G(TT(Bbr[:], u1[:], u2[:], ALU.subtract))
            G(TT(u1[:], Bt[:, 1], kbc(kr), ALU.mult)); G(TT(u2[:], Bt[:, 0], kbc(ki), ALU.mult))
            G(TT(Bbi[:], u1[:], u2[:], ALU.add))
            s8 = [128, 24, 8, 16]
            MBT = ar.alloc([128, 2, 24, 8, 16], BF16); QN = ar.alloc([128, 2, 24, 8, 16], BF16)
            MC = ar.alloc([128, 2, 24, 8, 16], BF16)
            w1 = ar.alloc(s8); w2 = ar.alloc(s8)

            def pw(tile, p0, m0_, mstep):
                return tile.cu(p0, 64, m0_ * 24, [[1, 24], [mstep * 24, 8], [0, 16]])

            def bc8(tile3, ri, p0):
                base = ri * 24 * 16 if ri is not None else 0
                return tile3.cu(p0, 64, base, [[16, 24], [0, 8], [1, 16]])

            def cmul(out, ar_, ai_, br_, bi_, p0, neg_im=False):
                o_r = out.cu(p0, 64, 0, [[128, 24], [16, 8], [1, 16]])
                o_i = out.cu(p0, 64, 24 * 128, [[128, 24], [16, 8], [1, 16]])
                a = w1.cu(p0, 64, 0, [[128, 24], [16, 8], [1, 16]])
                b = w2.cu(p0, 64, 0, [[128, 24], [16, 8], [1, 16]])
                G(TT(a, ar_, br_, ALU.mult)); G(TT(b, ai_, bi_, ALU.mult)); G(TT(o_r, a, b, ALU.subtract))
                G(TT(a, ar_, bi_, ALU.mult)); G(TT(b, ai_, br_, ALU.mult))
                if neg_im:
                    G(STT(o_i, a, -1.0, b, ALU.mult, ALU.subtract))
                else:
                    G(TT(o_i, a, b, ALU.add))

            for p0, (mb0, mbs), (qn0, qns), (mc0, mcs) in ((0, (7, -1), (7, -1), (1, 1)), (64, (0, 1), (0, 1), (8, -1))):
                cmul(MBT, pw(PWr, p0, mb0, mbs), pw(PWi, p0, mb0, mbs), bc8(Bbr, None, p0), bc8(Bbi, None, p0), p0)
                cmul(QN, pw(NWr, p0, qn0, qns), pw(NWi, p0, qn0, qns), bc8(Ct, 0, p0), bc8(Ct, 1, p0), p0, neg_im=True)
                cmul(MC, pw(PWr, p0, mc0, mcs), pw(PWi, p0, mc0, mcs), bc8(Ct, 0, p0), bc8(Ct, 1, p0), p0, neg_im=True)
            mats = ar.alloc([128, 24, 7, 128], BF16); d_mats = Dep()
            tz1 = ar.alloc([128, 128]); tz2 = ar.alloc([128, 128]); d_tz = Dep()
            kb.op("dve", MS(mats[:, :, 3:7, :], 0.0), [], [d_mats])
            for di in range(2):
                kb.op("dve", CP(mats.cu(64 * di, 64, (3 + di) * 128, [[2 * 128, 2], [7 * 128, 24], [1, 128]]),
                                MC.cu(64 * di, 64, 0, [[24 * 128, 2], [128, 24], [1, 128]])), [dg], [d_mats])
            KSKIP = os.environ.get('KSKIP', '')
            for g in range(24 if 'grp' not in KSKIP else 0):
                pt, pd = P()
                pt2, pd2 = P()
                fs = []
                for di in range(2):
                    for ri in range(2):
                        fs.append(MM((pt if di == 0 else pt2)[:, 0:128],
                                     MBT.cu(di * 64, 64, (ri * 24 + g) * 128, [[1, 128]]),
                                     QN.cu(di * 64, 64, (ri * 24 + g) * 128, [[1, 128]]),
                                     start=(ri == 0), stop=(ri == 1)))
                for ri in range(2):
                    fs.append(MM(pt[:, 256 + ri * 128:256 + (ri + 1) * 128],
                                 MBT.cu(0, 128, (ri * 24 + g) * 128, [[1, 128]]), ident_bf[:]))
                if 'gpe' not in KSKIP:
                    kb.op("pe", SEQ(fs), [dg, d_ident], [pd, pd2])
                if 'gdve' in KSKIP:
                    continue
                kb.op("dve", TT(tz1[:], pt[:, 0:128], maskf[:], ALU.mult), [pd, d_mask], [d_tz])
                kb.op("dve", TT(tz2[:], pt2[:, 0:128], maskb[:], ALU.mult), [pd2, d_mask], [d_tz])
                kb.op("dve", TT(tz1[:], tz1[:], tz2[:], ALU.add), [d_tz], [d_tz])
                kb.op("dve", STT(mats[:, g, 0, :], ident[:], dsk[:, g:g + 1], tz1[:], ALU.mult, ALU.add),
                      [d_tz, dg, d_ident], [d_mats])
                kb.op("dve", CP(mats[:, g, 1:3, :], pt[:, 256:512].rearrange("p (r x) -> p r x", r=2)),
                      [pd], [d_mats])
            if 'dma' not in KSKIP:
                kb.dma("sp", ssm_scr[l], mats[:].rearrange("p g r x -> p (g r x)"), reads=[d_mats], writes=[Dep()])
            kb.barrier()
            ar.release(m1)


    kb.interleave([stream_precast, stream_adaln, stream_gen], [6, 2, 10])
    kb.barrier()
    ar.release(mC)

    xT = ar.alloc([128, NCH, S]); d_x = [Dep() for _ in range(N5)]
    hT = ar.alloc([128, NCH, S], BF16); d_h = [Dep() for _ in range(N5)]
    ccT = hT
    d_out = []
    j5 = lambda j: slice(j * 512, (j + 1) * 512)
    evq = [0]

    def evac(out, in_, reads, writes, func=AF.Copy):
        evq[0] += 1
        if evq[0] % 2:
            kb.op("act", ACTF(out, in_, func), reads, writes)
        else:
            kb.op("dve", CP(out, in_), reads, writes)

    def rms_stats(src_sq_fn, j, tmp_sq, d_sq, rstd, d_rstd):
        pt, pd = P()
        kb.op("pe", SEQ([MM(pt[:, :], ones_bf[:], tmp_sq[:, c_, :], start=(c_ == 0), stop=(c_ == NCH - 1))
                         for c_ in range(NCH)]), list(d_sq) + [d_ones], [pd])
        kb.op("act", ACTF(rstd[:], pt[:, :], AF.Sqrt, bias=eps_t[:], scale=1.0 / D), [pd, d_eps], [d_rstd])
        kb.op("dve", RCP(rstd[:], rstd[:]), [d_rstd], [d_rstd])

    def pre_norm(l, i, b, tmps, js=None):
        sq, d_sq, rstd, d_rstd, tmp, d_tmp = tmps
        sh0 = 0 if i == 0 else 24
        for j in (range(N5) if js is None else js):
            kb.op("act", ACTF(sq[:], xT[:, :, j5(j)], AF.Square), [d_x[j]], list(d_sq))
            rms_stats(None, j, sq, d_sq, rstd, d_rstd)
            for c_ in range(NCH):
                kb.op("dve", STT(tmp[:], xT[:, c_, j5(j)], Am[:, l, i, c_, b:b + 1], rstd[:], ALU.mult, ALU.mult),
                      [d_x[j], d_Am, d_rstd], [d_tmp])
                kb.op("act", ACTF(hT[:, c_, j5(j)], tmp[:], AF.Identity, bias=modT[:, l, sh0 + c_, b:b + 1]),
                      [d_tmp, d_mod], [d_h[j]])

    def post_norm_residual(l, i, b, j, y_sb, d_y, sq, d_sq, rstd, d_rstd, tmp, d_tmp):
        rms_stats(None, j, sq, d_sq, rstd, d_rstd)
        for c_ in range(NCH):
            tb_, dtb_ = (tmp, d_tmp) if c_ % 2 == 0 else (tmp2, d_tmp2)
            kb.op("dve", STT(tb_[:], y_sb[:, c_, :], Gm[:, l, i, c_, b:b + 1], rstd[:], ALU.mult, ALU.mult),
                  [d_y[c_], d_Gm, d_rstd], [dtb_])
            kb.op("pool", TT(xT[:, c_, j5(j)], xT[:, c_, j5(j)], tb_[:], ALU.add), [dtb_], [d_x[j]])

    def proj_post(l, i, b, w_sb, d_w, nk, rhs_fn, rhs_deps_fn):
        y_sb = ar.alloc([128, NCH, 512]); d_y = [Dep() for _ in range(NCH)]
        sq = ar.alloc([128, NCH, 512], BF16); d_sq = [Dep() for _ in range(NCH)]
        rstd = ar.alloc([128, 512]); d_rstd = Dep()
        tmp = ar.alloc([128, 512]); d_tmp = Dep()
        for j in range(N5):
            import os
            for oc in range(NCH if os.environ.get('KDBG2') != 'nomm' else 0):
                pt, pd = P()
                kb.op("pe", SEQ([MM(pt[:, :], w_sb(kt, oc), rhs_fn(kt, j), start=(kt == 0), stop=(kt == nk - 1))
                                 for kt in range(nk)]), [d_w] + rhs_deps_fn(j), [pd])
                kb.op("dve", CP(y_sb[:, oc, :], pt[:, :]), [pd], [d_y[oc]])
                kb.op("act", ACTF(sq[:, oc, :], y_sb[:, oc, :], AF.Square), [d_y[oc]], [d_sq[oc]])
            import os
            if os.environ.get('KDBG') != 'wo1':
                post_norm_residual(l, i, b, j, y_sb, d_y, sq, d_sq, rstd, d_rstd, tmp, d_tmp)

    for s in range(2):
        for c_ in range(NCH):
            kb.dma("sp", xT[:, c_, :], xT_d[s, :, c_, :], writes=d_x)
        import os
        for l in range(nlayers if os.environ.get('KDBG') != 'ada' else 0):
            mk0 = ar.mark()
            uT = ar.alloc([128, 3, S], BF16); fT = ar.alloc([128, 2, S], BF16)
            mkA = ar.mark()
            qT = ar.alloc([128, 5, S], BF16)
            v_sb = ar.alloc([128, NT, 2, 2, 128], BF16)
            mkW = ar.mark()
            w_in_sb = ar.alloc([128, NCH, WIN_COLS], BF16); d_win = Dep()
            d_z = Dep(); d_v = Dep()
            pn_t = (ar.alloc([128, NCH, 512], BF16), [Dep() for _ in range(NCH)], ar.alloc([128, 512]), Dep(), ar.alloc([128, 512]), Dep())
            for kt in range(NCH):
                kb.dma("sp", w_in_sb[:, kt, :], win_b[l, :, kt, :], writes=[d_win])
            kb.op("pool", MS(v_sb[:], 0.0), [], [d_v])
            for j in range(N5):
                pre_norm(l, 0, s, pn_t, [j])
                for oc in range(10):
                    dst = uT[:, oc] if oc < 3 else (fT[:, oc - 3] if oc < 5 else qT[:, oc - 5])
                    pt, pd = P()
                    kb.op("pe", SEQ([MM(pt[:, :], w_in_sb[:, kt, oc * 128:(oc + 1) * 128], hT[:, kt, j5(j)],
                                        start=(kt == 0), stop=(kt == NCH - 1)) for kt in range(NCH)]),
                          [d_win, d_h[j]], [pd])
                    if 5 <= oc < 8:
                        kb.op("act", ACTF(dst[:, j5(j)], pt[:, :], AF.Identity, scale=0.125), [pd], [d_z])
                    else:
                        evac(dst[:, j5(j)], pt[:, :], [pd], [d_z])
                for tb in range(4 * j, 4 * j + 4):
                    pt, pd = P()
                    kb.op("pe", SEQ([MM(pt[:, 0:128], hT[:, kt, tb * 128:(tb + 1) * 128], w_in_sb[:, kt, 1280:1408],
                                        start=(kt == 0), stop=(kt == NCH - 1)) for kt in range(NCH)]),
                          [d_win, d_h[tb // 4]], [pd])
                    pv = pt[:, 0:128].rearrange("p (k d) -> p k d", k=2)
                    kb.op("act", ACTF(v_sb[:, tb, :, 0, 0:64], pv, AF.Copy), [pd], [d_v])
                    kb.op("dve", CP(v_sb[:, tb, :, 1, 64:128], pv), [pd], [d_v])
            kb.barrier()
            ar.release(mkW)
            if os.environ.get('KDBG') == 'win':
                ar.release(mk0)
                continue
            d_cc = d_h
            if "att" in parts:
                biasT = ar.alloc([128, 6, 3, 128], BF16); d_bias = Dep()
                ld("sp", biasT[:], biasT_d[:, :, :, :], d_bias)
                onesLR = ar.alloc([128, 2, 128], BF16); d_olr = Dep()
                kb.op("pool", MS(onesLR[:], 0.0), [], [d_olr])
                kb.op("pool", MS(onesLR[:, 0, 0:64], 1.0), [], [d_olr])
                kb.op("pool", MS(onesLR[:, 1, 64:128], 1.0), [], [d_olr])
                NAB = 3
                scb = [ar.alloc([128, 3, 128]) for _ in range(2 * NAB)]; d_scb = [Dep() for _ in range(2 * NAB)]
                pTb = [ar.alloc([128, 2, 3, 128], BF16) for _ in range(NAB)]; d_pT = [Dep() for _ in range(NAB)]
                dnb = [ar.alloc([128, 128]) for _ in range(2)]; d_dnb = [Dep(), Dep()]
                blocks = [(jp, n) for jp in range(3) for n in range(NT)]

                def att_stage1(it, jp, n):
                    kbs = [k_ for k_ in range(3) if 0 <= n + k_ - 1 < NT]
                    k0, k1 = kbs[0], kbs[-1] + 1
                    pb = it % NAB
                    for hh in range(2):
                        h = 2 * jp + hh
                        kv = h // 3
                        pt, pd = P()
                        pe_bias = (hh == 0)
                        fs = []
                        for k_ in kbs:
                            fs.append(MM(pt[:, k_ * 128:(k_ + 1) * 128],
                                         qT[64 * hh:64 * hh + 64, 3 + kv, (n + k_ - 1) * 128:(n + k_) * 128],
                                         qT[64 * hh:64 * hh + 64, jp, n * 128:(n + 1) * 128], start=True, stop=not pe_bias))
                            if pe_bias:
                                fs.append(MM(pt[:, k_ * 128:(k_ + 1) * 128], ident_bf[:], biasT[:, h, k_, :], start=False, stop=True))
                        kb.op("pe", SEQ(fs), [d_z, d_bias, d_ident], [pd])
                        psv = pt[:, k0 * 128:k1 * 128].rearrange("p (k q) -> p k q", q=128)
                        if pe_bias:
                            kb.op("act", ACTF(pTb[pb][:, hh, k0:k1, :], psv, AF.Exp), [pd], [d_pT[pb]])
                        else:
                            sb_ = scb[pb]
                            kb.op("dve", TT(sb_[:, k0:k1, :], psv, biasT[:, h, k0:k1, :], ALU.add), [pd, d_bias], [d_scb[pb]])
                            kb.op("act", ACTF(pTb[pb][:, hh, k0:k1, :], sb_[:, k0:k1, :], AF.Exp), [d_scb[pb]], [d_pT[pb]])

                def att_stage2(it, jp, n):
                    kbs = [k_ for k_ in range(3) if 0 <= n + k_ - 1 < NT]
                    pb = it % NAB
                    dn, d_dn = dnb[it % 2], d_dnb[it % 2]
                    pt, pd = P()
                    fs = []
                    for gi in range(2):
                        pairs = [(hh, k_) for hh in range(2) for k_ in kbs]
                        for ii, (hh, k_) in enumerate(pairs):
                            lhs = v_sb[:, n + k_ - 1, (2 * jp + hh) // 3, hh, :] if gi == 0 else onesLR[:, hh, :]
                            fs.append(MM(pt[:, gi * 128:(gi + 1) * 128], lhs, pTb[pb][:, hh, k_, :],
                                         start=(ii == 0), stop=(ii == len(pairs) - 1)))
                    kb.op("pe", SEQ(fs), [d_pT[pb], d_v, d_olr], [pd])
                    kb.op("dve", TS(dn[:], pt[:, 128:256], esink[:, l, jp:jp + 1], None, ALU.add), [pd, d_sink], [d_dn])
                    kb.op("dve", RCP(dn[:], dn[:]), [d_dn], [d_dn])
                    kb.op("dve", TT(ccT[:, 5 + jp, n * 128:(n + 1) * 128], pt[:, 0:128], dn[:], ALU.mult),
                          [pd, d_dn], [d_cc[n // 4]])

                for it in range(len(blocks) + 1):
                    if it < len(blocks):
                        att_stage1(it, *blocks[it])
                    if it >= 1:
                        att_stage2(it - 1, *blocks[it - 1])
            else:
                for j in range(N5):
                    kb.op("dve", MS(ccT[:, 5:8, j5(j)], 0.0), [], [d_cc[j]])
            kb.barrier()
            ar.release(mkA)
            use_fft = "fft" in parts
            use_ssm = "ssm" in parts and os.environ.get('KDBG') != 'ssmgen'
            if use_ssm:
                mkS = ar.mark()
                selW = ar.alloc([128, 8, 240], BF16); d_sel = Dep()
                ld("sp", selW[:], selW_d[:, :, :], d_sel)
                wglu = ar.alloc([128, 3, 384], BF16); d_wg = Dep()
                kb.dma("sp", wglu[:].rearrange("p k c -> p (k c)"), wglu_b[l], writes=[d_wg])
                Ug = ar.alloc([128, 24, K8], BF16); d_Ug = Dep()
                Xs = ar.alloc([128, K8, 2, 24], BF16); d_Xs = Dep()
                mb = [ar.alloc([128, 7, 128], BF16) for _ in range(2)]; d_mb = [Dep() for _ in range(2)]
                mi_ = [0]
                AA = ar.alloc([128, 2, 24]); AB = ar.alloc([128, 2, 24]); d_A = Dep()
                BL = 32
                NBk = K8 // BL
                shp = [128, NBk, 2, 24]
                Fp = [ar.alloc(shp) for _ in range(2)]; d_Fp = [Dep(), Dep()]
                d_Xo = Dep()
                s1 = ar.alloc(shp); s2_ = ar.alloc(shp); d_st = Dep()
                tB0 = ar.alloc(shp); tB1 = ar.alloc(shp); Cc = ar.alloc(shp)
                Pq = [ar.alloc([128, 2, 24]) for _ in range(2)]; Pw = [ar.alloc([128, 2, 24]) for _ in range(2)]
                w_a = ar.alloc([128, 2, 24]); w_b = ar.alloc([128, 2, 24])
                AAb = ar.alloc([128, 2, 24]); ABb = ar.alloc([128, 2, 24])

            def fft_section():
                mkF = ar.mark()
                c64 = ar.alloc([64, 128]); s64 = ar.alloc([64, 128]); wf = ar.alloc([64, 4, 64]); d_fc = Dep()
                ld("sp", c64[:], c64_d[:, :], d_fc); ld("sp", s64[:], s64_d[:, :], d_fc)
                ld("sp", wf[:], w_fft_d[l], d_fc)
                W2 = ar.alloc([128, 2, 2, 2, 64], BF16); d_W2 = Dep()
                kb.op("pool", MS(W2[:], 0.0), [], [d_W2])
                for h in range(4):
                    pt, pd = P()
                    kb.op("pe", SEQ([MM(pt[:, 0:64], c64[:], wf[:, h, :]), MM(pt[:, 64:128], s64[:], wf[:, h, :])]),
                          [d_fc], [pd])
                    hh = h % 2
                    kb.op("dve", CP(W2[64 * hh:64 * hh + 64, h // 2, :, hh, :],
                                    pt[64 * hh:64 * hh + 64, 0:128].rearrange("p (c e) -> p c e", c=2)), [pd], [d_W2])
                G_sb = ar.alloc([128, NT, 2, 256], BF16); d_G = Dep()
                for tb in range(NT):
                    pt, pd = P()
                    kb.op("pe", SEQ([MM(pt[:, jj * 256:(jj + 1) * 256], fT[:, jj, tb * 128:(tb + 1) * 128],
                                        W2[:, jj].rearrange("p c h e -> p (c h e)")) for jj in range(2)]),
                          [d_z, d_W2], [pd])
                    evac(G_sb[:, tb].rearrange("p j x -> p (j x)"), pt[:, :], [pd], [d_G])
                PG = min(4, NT)
                dbuf = [ar.alloc([128, PG, 512], BF16) for _ in range(2)]; d_db = [Dep() for _ in range(2)]
                di = 0
                for j in range(N5):
                    acc = [P(hold=True), P(hold=True)]
                    first = True
                    for cs_ in range(2):
                        for pg in range(NT // PG):
                            bi = di % 2
                            di += 1
                            ld("sp", dbuf[bi][:], dft_d[cs_, pg * PG * 128:(pg + 1) * PG * 128, j5(j)]
                               .rearrange("(a p) x -> p a x", p=128), d_db[bi])
                            last = (cs_ == 1 and pg == NT // PG - 1)
                            for jj in range(2):
                                kb.op("pe", SEQ([MM(acc[jj][0][:, :], G_sb[:, pg * PG + a, jj, cs_ * 128:(cs_ + 1) * 128],
                                                    dbuf[bi][:, a, :], start=(first and a == 0),
                                                    stop=(last and a == PG - 1)) for a in range(PG)]),
                                      [d_G, d_db[bi]], [acc[jj][1]])
                            first = False
                    for jj in range(2):
                        evac(ccT[:, 3 + jj, j5(j)], acc[jj][0][:, :], [acc[jj][1]], [d_cc[j]])
                        Prelease(acc[jj])
                kb.barrier()
                ar.release(mkF)

            def ssmA_section():
                for g in range(24):
                    ch, g8 = g // 8, g % 8
                    pt, pd = P()
                    kb.op("pe", SEQ([MM(pt[:, 0:K8], selW[:, g8, 112 - 16 * s2:112 - 16 * s2 + 128],
                                        uT.cu(0, 128, ch * S + s2, [[8, K8]]), start=(s2 == 0), stop=(s2 == 7))
                                     for s2 in range(8)]), [d_sel, d_z], [pd])
                    evac(Ug[:, g, :], pt[:, 0:K8], [pd], [d_Ug])
                    bi = mi_[0] % 2
                    mi_[0] += 1
                    ld("sp", mb[bi][:], ssm_scr[l, :, g * 896:(g + 1) * 896].rearrange("p (r x) -> p r x", r=7), d_mb[bi])
                    pt, pd = P()
                    kb.op("pe", SEQ([MM(pt[:, ri * K8:(ri + 1) * K8], mb[bi][:, 1 + ri, :], Ug[:, g, :]) for ri in range(2)]),
                          [d_mb[bi], d_Ug], [pd])
                    if g % 2:
                        kb.op("act", ACTF(Xs.cu(0, 64, g, [[24, 2], [48, K8]]),
                                          pt[0:64, 0:2 * K8].rearrange("p (r k) -> p r k", r=2), AF.Copy), [pd], [d_Xs])
                        kb.op("act", ACTF(Xs.cu(64, 64, (K8 - 1) * 48 + g, [[24, 2], [-48, K8]]),
                                          pt[64:128, 0:2 * K8].rearrange("p (r k) -> p r k", r=2), AF.Copy), [pd], [d_Xs])
                    else:
                        kb.op("dve", CP(Xs.cu(0, 64, g, [[24, 2], [48, K8]]),
                                        pt[0:64, 0:2 * K8].rearrange("p (r k) -> p r k", r=2)), [pd], [d_Xs])
                        kb.op("dve", CP(Xs.cu(64, 64, (K8 - 1) * 48 + g, [[24, 2], [-48, K8]]),
                                        pt[64:128, 0:2 * K8].rearrange("p (r k) -> p r k", r=2)), [pd], [d_Xs])
                if DBG and s == 0 and l == 0:
                    kb.dma("sp", dbg_ug[:, :], Ug[:].rearrange("p g k -> p (g k)"), reads=[d_Ug], writes=[Dep()])
                    kb.dma("sp", dbg_x0[:, :], Xs[:].rearrange("p k r g -> p (k r g)"), reads=[d_Xs], writes=[Dep()])
                    kb.dma("sp", dbg_u[:, :], uT[:].rearrange("p c t -> p (c t)"), reads=[d_z], writes=[Dep()])
                    kb.barrier()
                kb.op("dve", CP(AA[:, 0, :], A8[:, l, 0, :]), [d_A8], [d_A]); kb.op("dve", CP(AA[:, 1, :], A8[:, l, 0, :]), [d_A8], [d_A])
                kb.op("dve", TS(AB[:, 0, :], A8[:, l, 1, :], -1.0, None, ALU.mult), [d_A8], [d_A])
                kb.op("dve", CP(AB[:, 1, :], A8[:, l, 1, :]), [d_A8], [d_A])
                def Xv(i, b0=0):
                    return Xs.cu(0, 128, (b0 * BL + i) * 48, [[BL * 48, NBk - b0], [24, 2], [1, 24]])

                def bc(t, nb):
                    return t.cu(0, 128, 0, [[0, nb], [24, 2], [1, 24]])

                def bcsw(t, nb):
                    return t.cu(0, 128, 24, [[0, nb], [-24, 2], [1, 24]])

                def sw4(t, nb):
                    return t.cu(0, 128, 24, [[48, nb], [-24, 2], [1, 24]])

                kb.op("dve", CP(Fp[0][:], Xv(0)), [d_Xs], [d_Fp[0]])
                for i_ in range(1, BL):
                    Fo, Fn = Fp[(i_ - 1) % 2], Fp[i_ % 2]
                    do, dn_ = d_Fp[(i_ - 1) % 2], d_Fp[i_ % 2]
                    kb.op("dve", TT(s1[:], bc(AA, NBk), Fo[:], ALU.mult), [d_A, do], [d_st])
                    kb.op("dve", TT(s2_[:], bc(AB, NBk), sw4(Fo, NBk), ALU.mult), [do], [d_st])
                    kb.op("dve", TT(s1[:], s1[:], s2_[:], ALU.add), [d_st], [d_st])
                    kb.op("dve", TT(Fn[:], s1[:], Xv(i_), ALU.add), [d_st, d_Xs], [dn_])
                    kb.op("act", ACTF(Xv(i_), Fn[:], AF.Copy), [dn_], [d_Xo])
                if NBk > 1:
                    Fl, d_Fl = Fp[(BL - 1) % 2], d_Fp[(BL - 1) % 2]
                    d_pq = Dep()
                    kb.op("dve", CP(Pq[0][:], A8[:, l]), [d_A8], [d_pq])
                    nsq = BL.bit_length() - 1
                    for q_ in range(nsq):
                        po, pn = Pq[q_ % 2], Pq[(q_ + 1) % 2]
                        kb.op("dve", TT(w_a[:, 0, :], po[:, 0, :], po[:, 0, :], ALU.mult), [d_pq], [d_pq])
                        kb.op("dve", TT(w_a[:, 1, :], po[:, 1, :], po[:, 1, :], ALU.mult), [d_pq], [d_pq])
                        kb.op("dve", TT(pn[:, 0, :], w_a[:, 0, :], w_a[:, 1, :], ALU.subtract), [d_pq], [d_pq])
                        kb.op("dve", STT(pn[:, 1, :], po[:, 0, :], 2.0, po[:, 1, :], ALU.mult, ALU.mult), [d_pq], [d_pq])
                    pBL = Pq[nsq % 2]
                    kb.op("dve", CP(AAb[:, 0, :], pBL[:, 0, :]), [d_pq], [d_pq]); kb.op("dve", CP(AAb[:, 1, :], pBL[:, 0, :]), [d_pq], [d_pq])
                    kb.op("dve", TS(ABb[:, 0, :], pBL[:, 1, :], -1.0, None, ALU.mult), [d_pq], [d_pq])
                    kb.op("dve", CP(ABb[:, 1, :], pBL[:, 1, :]), [d_pq], [d_pq])
                    d_Cc = Dep()
                    kb.op("dve", CP(Cc[:, 0], Fl[:, 0]), [d_Fl], [d_Cc])
                    for b_ in range(1, NBk):
                        cprev_sw = Cc.cu(0, 128, (b_ - 1) * 48 + 24, [[-24, 2], [1, 24]])
                        kb.op("dve", TT(w_a[:], AAb[:], Cc[:, b_ - 1], ALU.mult), [d_pq, d_Cc], [d_pq])
                        kb.op("dve", TT(w_b[:], ABb[:], cprev_sw, ALU.mult), [d_Cc], [d_pq])
                        kb.op("dve", TT(w_a[:], w_a[:], w_b[:], ALU.add), [d_pq], [d_pq])
                        kb.op("dve", TT(Cc[:, b_], w_a[:], Fl[:, b_], ALU.add), [d_pq, d_Fl], [d_Cc])
                    nb1 = NBk - 1
                    CA, CB = s1, s2_
                    d_CAB = Dep()
                    cp_r = Cc.cu(0, 128, 0, [[48, nb1], [0, 2], [1, 24]])
                    kb.op("dve", CP(CA.cu(0, 128, 0, [[48, nb1], [24, 2], [1, 24]]), cp_r), [d_Cc, d_st], [d_CAB])
                    kb.op("dve", TS(CB.cu(0, 128, 0, [[48, nb1], [1, 24]]), Cc.cu(0, 128, 24, [[48, nb1], [1, 24]]), -1.0, None, ALU.mult),
                          [d_Cc, d_st], [d_CAB])
                    kb.op("dve", CP(CB.cu(0, 128, 24, [[48, nb1], [1, 24]]), Cc.cu(0, 128, 24, [[48, nb1], [1, 24]])), [d_Cc], [d_CAB])
                    d_pw = Dep()
                    kb.op("dve", CP(Pw[0][:], A8[:, l]), [d_A8], [d_pw])
                    tA = [Fp[0], Fp[1]]; tB = [tB0, tB1]; d_tA = [Dep(), Dep()]; d_tB = [Dep(), Dep()]
                    for i_ in range(BL):
                        pw = Pw[i_ % 2]
                        ta, tb_ = tA[i_ % 2], tB[i_ % 2]
                        tav = ta.cu(0, 128, 0, [[48, nb1], [24, 2], [1, 24]])
                        tbv = tb_.cu(0, 128, 0, [[48, nb1], [24, 2], [1, 24]])
                        kb.op("dve", TT(tav, CA.cu(0, 128, 0, [[48, nb1], [24, 2], [1, 24]]), bc(pw, nb1), ALU.mult),
                              [d_CAB, d_pw], [d_tA[i_ % 2], d_Fp[i_ % 2]])
                        kb.op("pool", TT(tbv, CB.cu(0, 128, 0, [[48, nb1], [24, 2], [1, 24]]), bcsw(pw, nb1), ALU.mult),
                              [d_CAB, d_pw], [d_tB[i_ % 2]])
                        kb.op("dve", TT(tav, tav, tbv, ALU.add), [d_tB[i_ % 2]], [d_tA[i_ % 2]])
                        kb.op("dve", TT(Xv(i_, 1), Xv(i_, 1), tav, ALU.add), [d_tA[i_ % 2], d_Xs, d_Xo], [d_Xo])
                        if i_ + 1 < BL:
                            pn = Pw[(i_ + 1) % 2]
                            pw_sw = pw.cu(0, 128, 24, [[-24, 2], [1, 24]])
                            kb.op("dve", TT(w_a[:], AA[:], pw[:], ALU.mult), [d_A, d_pw], [d_pq])
                            kb.op("dve", TT(w_b[:], AB[:], pw_sw, ALU.mult), [d_pw], [d_pq])
                            kb.op("dve", TT(pn[:], w_a[:], w_b[:], ALU.add), [d_pq], [d_pw])

            streams, quanta = [], []
            if use_fft:
                streams.append(fft_section); quanta.append(1)
            else:
                for j in range(N5):
                    kb.op("dve", MS(ccT[:, 3:5, j5(j)], 0.0), [], [d_cc[j]])

            if use_ssm:
                streams.append(ssmA_section); quanta.append(8)
            if streams:
                kb.interleave(streams, quanta)
            if use_ssm:
                unselW = ar.alloc([128, 8, 240], BF16)
                ld("sp", unselW[:], unselW_d[:, :, :], d_sel)

                Yb = [ar.alloc([128, K8], BF16) for _ in range(2)]; d_Yb = [Dep(), Dep()]
                yg = ar.alloc([128, 3, S], BF16); d_yg = Dep()
                for ch in range(3):
                    accs = [P(hold=True) for _ in range(N5)]
                    for j in range(N5):
                        kb.op("dve", MS(accs[j][0][:, :], 0.0), [], [accs[j][1]])
                    for g8 in range(8):
                        g = ch * 8 + g8
                        bi = mi_[0] % 2
                        mi_[0] += 1
                        ld("sp", mb[bi][:], ssm_scr[l, :, g * 896:(g + 1) * 896].rearrange("p (r x) -> p r x", r=7), d_mb[bi])
                        pt, pd = P()
                        fs = [MM(pt[:, 0:K8], mb[bi][:, 0, :], Ug[:, g, :], start=True, stop=False)]
                        for ri in range(2):
                            fs.append(MM(pt[:, 1:K8], mb[bi][:, 3 + 2 * ri, :], Xs.cu(0, 128, ri * 24 + g, [[48, K8 - 1]]),
                                         start=False, stop=False))
                            fs.append(MM(pt[:, 0:K8 - 1], mb[bi][:, 4 + 2 * ri, :],
                                         Xs.cu(0, 128, (K8 - 2) * 48 + ri * 24 + g, [[-48, K8 - 1]]),
                                         start=False, stop=(ri == 1)))
                        kb.op("pe", SEQ(fs), [d_mb[bi], d_Ug, d_Xs, d_Xo], [pd])
                        yb = g % 2
                        evac(Yb[yb][:], pt[:, 0:K8], [pd], [d_Yb[yb]])
                        for j in range(N5):
                            kb.op("pe", SEQ([MMZ(bass.AP(tensor=accs[j][0], offset=t2, ap=[[512, 128], [8, 64]]),
                                                 unselW[:, t2, 112 - 16 * g8:112 - 16 * g8 + 128],
                                                 Yb[yb][:, j * 64:(j + 1) * 64]) for t2 in range(8)]),
                                  [d_Yb[yb], d_sel], [accs[j][1]])
                    for j in range(N5):
                        kb.op("act", ACTF(yg[:, ch, j5(j)], accs[j][0][:, :], AF.Gelu_apprx_tanh), [accs[j][1]], [d_yg])
                        Prelease(accs[j])
                if DBG and s == 0 and l == 0:
                    kb.dma("sp", dbg_xs[:, :], Xs[:].rearrange("p k r g -> p (k r g)"), reads=[d_Xs, d_Xo], writes=[Dep()])
                    kb.dma("sp", dbg_yg[:, :], yg[:].rearrange("p c t -> p (c t)"), reads=[d_yg], writes=[Dep()])
                    kb.barrier()
                sg = ar.alloc([128, 512], BF16); d_sg = Dep()
                for oc in range(3):
                    for j in range(N5):
                        pt, pd = P()
                        kb.op("pe", SEQ([MM(pt[:, :], wglu[:, kt, oc * 128:(oc + 1) * 128], yg[:, kt, j5(j)],
                                            start=(kt == 0), stop=(kt == 2)) for kt in range(3)]), [d_wg, d_yg], [pd])
                        kb.op("act", ACTF(sg[:], pt[:, :], AF.Sigmoid), [pd], [d_sg])
                        kb.op("dve", TT(ccT[:, oc, j5(j)], yg[:, oc, j5(j)], sg[:], ALU.mult), [d_sg, d_yg], [d_cc[j]])
                ar.release(mkS)
            else:
                for j in range(N5):
                    kb.op("dve", MS(ccT[:, 0:3, j5(j)], 0.0), [], [d_cc[j]])
            kb.barrier()
            ar.release(mk0)
            mkO = ar.mark()
            w_out_sb = ar.alloc([128, NCH, D], BF16); d_wo = Dep()
            for kt in range(NCH):
                kb.dma("sp", w_out_sb[:, kt, :], wout_b[l, :, kt, :], writes=[d_wo])
            proj_post(l, 0, s, lambda kt, oc: w_out_sb[:, kt, oc * 128:(oc + 1) * 128], d_wo, NCH,
                      lambda kt, j: ccT[:, kt, j5(j)], lambda j: [d_cc[j]])
            kb.barrier()
            ar.release(mkO)
            if "ffn" in parts:
                mkN = ar.mark()
                actT = ar.alloc([128, NFC, 512], BF16)
                NB = 3
                upr = [ar.alloc([128, 2, 514]) for _ in range(NB)]
                d_upr = [[Dep(), Dep()] for _ in range(NB)]
                cen = [ar.alloc([128, 2, 512]) for _ in range(NB)]
                d_cen = [[Dep(), Dep()] for _ in range(NB)]
                gl = [ar.alloc([128, 512]) for _ in range(2)]; d_gl = [Dep(), Dep()]
                wu = [ar.alloc([128, NCH, 256], BF16) for _ in range(3)]; d_wu = [Dep() for _ in range(3)]
                wd = [ar.alloc([128, D], BF16) for _ in range(3)]; d_wd = [Dep() for _ in range(3)]
                y_sb = ar.alloc([128, NCH, 512]); d_y = [Dep() for _ in range(NCH)]
                sq = ar.alloc([128, NCH, 512], BF16); d_sq = [Dep() for _ in range(NCH)]
                rstd = ar.alloc([128, 512]); d_rstd = Dep()
                tmp = ar.alloc([128, 512]); d_tmp = Dep()
                pn_f = (sq, d_sq, rstd, d_rstd, tmp, d_tmp)
                d_act = [Dep() for _ in range(NFC)]
                wi_ = 0
                ci_ = 0
                gcnt = [0]

                def ffn_tail(cc_, ub):
                    gb = gcnt[0] % 2
                    gcnt[0] += 1
                    ce = cen[ub]
                    kb.op("act", ACTF(gl[gb][:], ce[:, 0, :], AF.Gelu_apprx_tanh), [d_cen[ub][0]], [d_gl[gb]])
                    kb.op("pool", TT(actT[:, cc_, :], gl[gb][:], ce[:, 1, :], ALU.mult), [d_gl[gb], d_cen[ub][1]], [d_act[cc_]])

                for j in range(N5):
                    t0 = j * 512
                    pend = []
                    pre_norm(l, 1, s, pn_f, [0, 1][:N5] if j == 0 else ([j + 1] if j + 1 < N5 else []))
                    for cc_ in range(NFC):
                        bi = wi_ % 3
                        wi_ += 1
                        ub = ci_ % NB
                        ci_ += 1
                        kb.dma("sp", wu[bi][:].rearrange("p k c -> p (k c)"), wup_b[l, cc_], writes=[d_wu[bi]])
                        u_ = upr[ub]
                        for hv in range(2):
                            pt, pd = P()
                            kb.op("pe", SEQ([MM(pt[:, :], wu[bi][:, kt, hv * 128:(hv + 1) * 128], hT[:, kt, j5(j)],
                                                start=(kt == 0), stop=(kt == NCH - 1)) for kt in range(NCH)]),
                                  [d_wu[bi], d_h[j]], [pd])
                            kb.op("act", ACTF(u_[:, hv, 1:513], pt[:, :], AF.Copy), [pd], [d_upr[ub][hv]])
                        toks = [t_ for t_ in (t0 - 1, t0 + 512) if 0 <= t_ < S]
                        if t0 - 1 < 0:
                            kb.op("pool", MS(u_[:, :, 0:1], 0.0), [], d_upr[ub])
                        if t0 + 512 >= S:
                            kb.op("pool", MS(u_[:, :, 513:514], 0.0), [], d_upr[ub])
                        if toks:
                            nh = len(toks)
                            pt, pd = P()
                            fs = []
                            for hv in range(2):
                                for kt in range(NCH):
                                    rhs = hT.cu(0, 128, kt * S + toks[0], [[513, nh]])
                                    fs.append(MM(pt[:, hv * 2:hv * 2 + nh], wu[bi][:, kt, hv * 128:(hv + 1) * 128], rhs,
                                                 start=(kt == 0), stop=(kt == NCH - 1)))
                            kb.op("pe", SEQ(fs), [d_wu[bi]] + d_h, [pd])
                            for ti, t_ in enumerate(toks):
                                col = 0 if t_ == t0 - 1 else 513
                                kb.op("dve", CP(u_[:, :, col:col + 1], bass.AP(tensor=pt, offset=ti, ap=[[512, 128], [2, 2], [1, 1]])),
                                      [pd], d_upr[ub])
                        ce = cen[ub]
                        for hv in range(2):
                            ci = hv * NFC + cc_
                            dd = [d_cen[ub][hv]]
                            kb.op("act", ACTF(ce[:, hv, :], u_[:, hv, 1:513], AF.Identity, bias=cb[:, l, ci:ci + 1],
                                              scale=cw[:, l, 1, ci:ci + 1]), [d_upr[ub][hv], d_cw, d_cb], dd)
                            kb.op("dve", STT(ce[:, hv, :], u_[:, hv, 0:512], cw[:, l, 0, ci:ci + 1], ce[:, hv, :], ALU.mult, ALU.add),
                                  [d_upr[ub][hv], d_cw], dd)
                            kb.op("dve", STT(ce[:, hv, :], u_[:, hv, 2:514], cw[:, l, 2, ci:ci + 1], ce[:, hv, :], ALU.mult, ALU.add),
                                  [d_upr[ub][hv], d_cw], dd)
                        pend.append((cc_, ub))
                        if len(pend) > 1:
                            ffn_tail(*pend.pop(0))
                    while pend:
                        ffn_tail(*pend.pop(0))
                    accs = [P(hold=True) for _ in range(NCH)]
                    for cc_ in range(NFC):
                        bi = wi_ % 3
                        wi_ += 1
                        kb.dma("sp", wd[bi][:], wdn_b[l, cc_], writes=[d_wd[bi]])
                        for oc in range(NCH):
                            kb.op("pe", MM(accs[oc][0][:, :], wd[bi][:, oc * 128:(oc + 1) * 128], actT[:, cc_, :],
                                           start=(cc_ == 0), stop=(cc_ == NFC - 1)), [d_wd[bi], d_act[cc_]], [accs[oc][1]])
                    for oc in range(NCH):
                        kb.op("dve", CP(y_sb[:, oc, :], accs[oc][0][:, :]), [accs[oc][1]], [d_y[oc]])
                        kb.op("act", ACTF(sq[:, oc, :], y_sb[:, oc, :], AF.Square), [d_y[oc]], [d_sq[oc]])
                    for a_ in accs:
                        Prelease(a_)
                    post_norm_residual(l, 1, s, j, y_sb, d_y, sq, d_sq, rstd, d_rstd, tmp, d_tmp)
                kb.barrier()
                ar.release(mkN)
        for c_ in range(NCH):
            dd = Dep()
            d_out.append(dd)
            kb.dma("sp", outT_d[s, :, c_, :], xT[:, c_, :], reads=d_x, writes=[dd])
    kb.wait_all("sp", d_out)
    kb.emit(st)
    st.close()
    print("built: %d ops, arena peak %d KiB" % (kb.ninst, ar.peak // 1024))
    return nc


def _consts(S):
    bf = ml_dtypes.bfloat16
    c = {}
    c["ident"] = np.eye(128, dtype=np.float32)
    sp = np.arange(128) // 16
    c["maskf"] = (sp[None, :] >= sp[:, None]).astype(np.float32)
    c["maskb"] = (sp[None, :] <= sp[:, None]).astype(np.float32)
    selW = np.zeros((128, 8, 240), np.float32)
    for g8 in range(8):
        for cc in range(16):
            selW[16 * g8 + cc, g8, cc + 112] = 1.0
    c["selW"] = selW.astype(bf)
    unselW = np.zeros((128, 8, 240), np.float32)
    for t2 in range(8):
        for cc in range(16):
            unselW[16 * t2 + cc, t2, cc + 112] = 1.0
    c["unselW"] = unselW.astype(bf)
    slopes = np.exp2(-8.0 * np.arange(1, 7, dtype=np.float64) / 6)
    j = np.arange(128)[:, None, None]
    kb_ = np.arange(3)[None, :, None]
    i = np.arange(128)[None, None, :]
    dist = np.abs(i + 128 - (kb_ * 128 + j))
    bias = np.where(dist[:, None] <= 128, -slopes[None, :, None, None] * dist[:, None], -1e30)
    c["biasT"] = bias.astype(np.float32).astype(bf)
    a = np.arange(64)
    th = 2 * np.pi * np.outer(a, a) / 64
    c64 = np.cos(th) / 8.0
    s64 = -np.sin(th) / 8.0
    c["c64d"] = np.concatenate([c64, c64], 1).astype(np.float32)
    c["s64d"] = np.concatenate([s64, s64], 1).astype(np.float32)
    c["mvals"] = np.broadcast_to(np.arange(9, dtype=np.float32)[None, :, None], (128, 9, 24)).copy()
    p = np.arange(S, dtype=np.int64)
    th = 2 * np.pi * ((p[:, None] * p[None, :]) % S) / S
    c["dft"] = np.stack([np.cos(th), np.sin(th)]).astype(np.float32) / np.sqrt(S)
    c["dft"] = c["dft"].astype(bf)
    return c


def _prep_shared(inp, S):
    f = lambda a: np.ascontiguousarray(np.asarray(a, dtype=np.float32))
    d = {}
    d["w_ada"] = f(inp["w_ada"]).reshape(L, NCH, 128, 6 * D)
    d["b_adaT"] = f(f(inp["b_ada"]).reshape(L, 48, 128).transpose(2, 0, 1))
    g = np.stack([f(inp[k]) for k in ("g_pre_mix", "g_post_mix", "g_pre_ffn", "g_post_ffn")])
    d["gT"] = f(g.reshape(4, L, NCH, 128).transpose(3, 1, 0, 2))
    w = f(inp["w_in"])
    wn = np.concatenate([w[:, :, 0:1024], w[:, :, 1024:1088], w[:, :, 1024:1088], w[:, :, 1088:1152],
                         w[:, :, 1088:1152], w[:, :, 1152:1280]], axis=2)
    d["w_in"] = f(wn.reshape(L, NCH, 128, WIN_COLS).transpose(0, 2, 1, 3))
    lam = np.stack([f(inp["lam_re"]), f(inp["lam_im"])])
    d["lamT"] = f(lam.transpose(2, 4, 1, 0, 3).reshape(128, L, 2, 24))
    ldt = f(inp["log_dt"])
    d["ldt"] = f(np.broadcast_to(ldt.transpose(1, 0, 2)[:, None], (2, 64, L, 24)).reshape(128, L, 24))
    b = np.stack([f(inp["b_re"]), f(inp["b_im"])])
    d["bT"] = f(b.transpose(2, 4, 1, 0, 3, 5).reshape(128, L, 2, 24, 16))
    cc = np.stack([f(inp["c_re"]), f(inp["c_im"])])
    cc = cc.transpose(4, 1, 0, 2, 3)
    d["cTs"] = f(np.concatenate([cc, cc], 0))
    dsk = f(inp["d_skip"]).reshape(L, 24, 16)
    d["dsk"] = f(np.tile(dsk.transpose(2, 0, 1), (8, 1, 1)))
    d["w_glu"] = f(f(inp["w_glu"]).reshape(L, 3, 128, 384).transpose(0, 2, 1, 3))
    d["w_fft"] = f(f(inp["w_fft"]).transpose(0, 2, 1, 3))
    sk = f(inp["sink"]).reshape(L, 3, 2)
    d["sinkT"] = f(np.repeat(sk.transpose(2, 0, 1), 64, axis=0))
    d["w_out"] = f(f(inp["w_out"]).reshape(L, NCH, 128, D).transpose(0, 2, 1, 3))
    wu = f(inp["w_up"]).reshape(L, NCH, 128, 2, NFC, 128)
    d["w_up"] = f(wu.transpose(0, 4, 2, 1, 3, 5).reshape(L, NFC, 128, NCH, 256))
    d["conv_w"] = f(f(inp["conv_w"]).reshape(L, 3, 2, NFC, 128).transpose(4, 0, 1, 2, 3).reshape(128, L, 3, 2 * NFC))
    d["conv_b"] = f(f(inp["conv_b"]).reshape(L, 2, NFC, 128).transpose(3, 0, 1, 2).reshape(128, L, 2 * NFC))
    d["w_down"] = f(inp["w_down"]).reshape(L, NFC, 128, D)
    d.update(_consts(S))
    return d


_NC_CACHE = {}


def run_cores(inp, S, n_cores, parts=("ssm", "fft", "att", "ffn"), nlayers=L):
    key = (S, tuple(parts), nlayers)
    if key not in _NC_CACHE:
        _NC_CACHE[key] = build(S, parts, nlayers)
    nc = _NC_CACHE[key]
    shared = _prep_shared(inp, S)
    x = np.asarray(inp["x"], dtype=np.float32)
    c = np.asarray(inp["c"], dtype=np.float32)
    in_maps = []
    for i in range(n_cores):
        xb = x[2 * i:2 * i + 2]
        m = dict(shared)
        m["xT"] = np.ascontiguousarray(xb.transpose(0, 2, 1).reshape(2, NCH, 128, S).transpose(0, 2, 1, 3))
        m["cT"] = np.ascontiguousarray(c[2 * i:2 * i + 2].reshape(2, NCH, 128).transpose(2, 1, 0))
        in_maps.append(m)
    res = run_bass_kernel_spmd(nc, in_maps, core_ids=list(range(n_cores)))
    outs = []
    for r in res.results:
        o = np.asarray(r["outT"])
        outs.append(o.transpose(0, 3, 2, 1).reshape(2, S, D))
    return np.concatenate(outs, 0).astype(np.float32)


def kernel(**inputs):
    S = inputs["x"].shape[1]
    return run_cores(inputs, S, 8)
```

```python
import math
import threading
import numpy as np
import ml_dtypes
import concourse.bass as bass
import concourse.mybir as mybir
from concourse.bass_utils import run_bass_kernel_spmd

F32 = mybir.dt.float32
BF16 = mybir.dt.bfloat16
AF = mybir.ActivationFunctionType
ALU = mybir.AluOpType
AX = mybir.AxisListType

ENGS = ("pe", "act", "dve", "pool", "sp")
N_DMA_SEMS = 12


class Dep:
    __slots__ = ("w", "r", "name")

    def __init__(self, name=""):
        self.w = None
        self.r = {}
        self.name = name


class KB:
    def __init__(self, nc):
        self.nc = nc
        self.ops = {e: [] for e in ENGS}
        self.cnt = {e: 0 for e in ENGS}
        self.seen = {e: {} for e in ENGS}
        self.sems = {}
        self.dma_cnt = [0] * N_DMA_SEMS
        self.dma_rr = 0
        self.ninst = 0
        self._tick = {}

    def _need(self, reads, writes):
        need = {}

        def add(tok):
            if tok is None:
                return
            k, v = tok
            if need.get(k, 0) < v:
                need[k] = v

        for d in reads:
            add(d.w)
        for d in writes:
            add(d.w)
            for k, v in d.r.items():
                add((k, v))
        return need

    def _emit_waits(self, eng, need):
        seen = self.seen[eng]
        waits = []
        for k, v in need.items():
            if k == "pe" and eng == "pe":
                continue
            if seen.get(k, 0) >= v:
                continue
            seen[k] = v
            waits.append((k, v))
        return waits

    def _update(self, tok, reads, writes):
        k, v = tok
        for d in reads:
            if d.r.get(k, 0) < v:
                d.r[k] = v
        for d in writes:
            d.w = tok
            d.r = {}

    def op(self, eng, fn, reads=(), writes=()):
        need = self._need(reads, writes)
        waits = self._emit_waits(eng, need)
        self.cnt[eng] += 1
        tok = (eng, self.cnt[eng])
        self.seen[eng][eng] = max(self.seen[eng].get(eng, 0), 0)
        sems = self.sems

        def run(e, waits=waits, fn=fn, eng=eng):
            for k, v in waits:
                e.wait_ge(sems[k], v)
            ins = fn(e)
            ins.then_inc(sems[eng], 1)

        self.ops[eng].append(run)
        self._update(tok, reads, writes)
        self.ninst += 1
        f = self._tick.get(threading.get_ident())
        if f:
            f()
        return tok

    def dma(self, q, out, in_, reads=(), writes=(), **kw):
        i = self.dma_rr
        self.dma_rr = (i + 1) % N_DMA_SEMS
        key = "d%d" % i
        need = self._need(reads, writes)
        if self.dma_cnt[i] > 0:
            if need.get(key, 0) < self.dma_cnt[i]:
                need[key] = self.dma_cnt[i]
        waits = self._emit_waits(q, need)
        self.dma_cnt[i] += 16
        tok = (key, self.dma_cnt[i])
        sems = self.sems

        def run(e, waits=waits, out=out, in_=in_, key=key, kw=kw):
            for k, v in waits:
                e.wait_ge(sems[k], v)
            e.dma_start(out=out, in_=in_, **kw).then_inc(sems[key], 16)

        self.ops[q].append(run)
        self._update(tok, reads, writes)
        self.ninst += 1
        f = self._tick.get(threading.get_ident())
        if f:
            f()
        return tok

    def interleave(self, funcs, quanta):
        n = len(funcs)
        if n == 1:
            funcs[0]()
            return
        cond = threading.Condition()
        st = {"turn": 0, "alive": [True] * n, "cnt": 0, "err": None}

        def nxt():
            t = st["turn"]
            for k in range(1, n + 1):
                c = (t + k) % n
                if st["alive"][c]:
                    st["turn"] = c
                    return
            st["turn"] = -1

        def tick(i):
            st["cnt"] += 1
            if st["cnt"] >= quanta[i]:
                st["cnt"] = 0
                with cond:
                    nxt()
                    cond.notify_all()
                    while st["turn"] != i:
                        cond.wait()

        def worker(i):
            with cond:
                while st["turn"] != i:
                    cond.wait()
            self._tick[threading.get_ident()] = (lambda i=i: tick(i))
            try:
                funcs[i]()
            except BaseException as e:
                st["err"] = e
            finally:
                self._tick.pop(threading.get_ident(), None)
                with cond:
                    st["alive"][i] = False
                    st["cnt"] = 0
                    if st["turn"] == i:
                        nxt()
                    cond.notify_all()

        ths = [threading.Thread(target=worker, args=(i,)) for i in range(n)]
        for t in ths:
            t.start()
        for t in ths:
            t.join()
        if st["err"] is not None:
            raise st["err"]

    def wait_all(self, eng, deps):
        need = self._need((), deps)
        waits = self._emit_waits(eng, need)
        sems = self.sems

        def run(e, waits=waits):
            for k, v in waits:
                e.wait_ge(sems[k], v)

        self.ops[eng].append(run)

    def barrier(self):
        need = {e: self.cnt[e] for e in ENGS if e != "sp" and self.cnt[e] > 0}
        for i in range(N_DMA_SEMS):
            if self.dma_cnt[i] > 0:
                need["d%d" % i] = self.dma_cnt[i]
        sems = self.sems
        for eng in ENGS:
            waits = self._emit_waits(eng, dict(need))

            def run(e, waits=waits):
                for k, v in waits:
                    e.wait_ge(sems[k], v)

            self.ops[eng].append(run)

    def emit(self, stack):
        nc = self.nc
        for e in ("pe", "act", "dve", "pool"):
            self.sems[e] = stack.enter_context(nc.semaphore("sem_" + e))
        for i in range(N_DMA_SEMS):
            self.sems["d%d" % i] = stack.enter_context(nc.semaphore("sem_d%d" % i))
        block = stack.enter_context(nc.Block())
        ops = self.ops

        @block.tensor
        def _(e):
            for f in ops["pe"]:
                f(e)

        @block.scalar
        def _(e):
            for f in ops["act"]:
                f(e)

        @block.vector
        def _(e):
            for f in ops["dve"]:
                f(e)

        @block.gpsimd
        def _(e):
            for f in ops["pool"]:
                f(e)

        @block.sync
        def _(e):
            for f in ops["sp"]:
                f(e)


def MM(out, lhsT, rhs, start=True, stop=True):
    return lambda e: e.matmul(out, lhsT=lhsT, rhs=rhs, start=start, stop=stop)


def MMZ(out, lhsT, rhs):
    return lambda e: e.matmul(out, lhsT=lhsT, rhs=rhs, start=False, stop=False, skip_group_check=True)


def TR(out, in_, ident):
    return lambda e: e.transpose(out, in_, ident)


def SEQ(fs):
    def run(e):
        r = None
        for f in fs:
            r = f(e)
        return r
    return run


def ACTF(out, in_, func, bias=0.0, scale=1.0, accum=None):
    if accum is None:
        return lambda e: e.activation(out=out, in_=in_, func=func, bias=bias, scale=scale)
    return lambda e: e.activation(out=out, in_=in_, func=func, bias=bias, scale=scale, accum_out=accum)


def TT(out, a, b, op):
    return lambda e: e.tensor_tensor(out=out, in0=a, in1=b, op=op)


def TS(out, a, s1, s2=None, op0=ALU.mult, op1=None):
    if s2 is None:
        return lambda e: e.tensor_scalar(out=out, in0=a, scalar1=s1, scalar2=None, op0=op0)
    return lambda e: e.tensor_scalar(out=out, in0=a, scalar1=s1, scalar2=s2, op0=op0, op1=op1)


def STT(out, in0, scalar, in1, op0, op1):
    return lambda e: e.scalar_tensor_tensor(out=out, in0=in0, scalar=scalar, in1=in1, op0=op0, op1=op1)


def CP(out, in_):
    return lambda e: e.tensor_copy(out=out, in_=in_)


def MS(out, val):
    return lambda e: e.memset(out, val)


def RCP(out, in_):
    return lambda e: e.reciprocal(out=out, in_=in_)


class Tl:
    def __init__(self, h, base, shape):
        self.h = h
        self.base = base
        self.shape = list(shape)
        self.row = h.shape[1]
        n = 1
        for d in shape[1:]:
            n *= d
        self.n = n
        ap = h[0:shape[0], base:base + n]
        if len(shape) > 2:
            names = "abcdefg"[:len(shape) - 1]
            kw = {names[i]: shape[i + 1] for i in range(1, len(names))}
            ap = ap.rearrange("p (%s) -> p %s" % (" ".join(names), " ".join(names)), **kw)
        self.full = ap

    def __getitem__(self, idx):
        return self.full[idx]

    def cu(self, p0, npart, off, dims):
        return bass.AP(tensor=self.h, offset=p0 * self.row + self.base + off,
                       ap=[[self.row, npart]] + [list(d) for d in dims])


class Arena:
    def __init__(self, h32, nbytes):
        self.h = {F32: h32, BF16: h32.bitcast(BF16)}
        self.nbytes = nbytes
        self.top = 0
        self.peak = 0

    def alloc(self, shape, dt=F32):
        esz = 4 if dt == F32 else 2
        n = 1
        for d in shape[1:]:
            n *= d
        nb = (n * esz + 31) // 32 * 32
        off = self.top
        self.top += nb
        self.peak = max(self.peak, self.top)
        assert self.top <= self.nbytes, ("SBUF arena overflow", self.top, self.nbytes)
        return Tl(self.h[dt], off // esz, shape)

    def mark(self):
        return self.top

    def release(self, m):
        self.top = m


D = 1024
NCH = 8
DFF = 2816
NFC = 22
L = 2
WIN_COLS = 1408
PI = math.pi


def build(S, parts=("ssm", "fft", "att", "ffn"), nlayers=L):
    import contextlib
    import os
    nc = bass.Bass("TRN2", target_bir_lowering=False)
    kb = KB(nc)
    NT = S // 128
    N5 = S // 512
    K8 = S // 8
    ins = {}

    def din(name, shape, dt=F32):
        ins[name] = nc.dram_tensor(name, list(shape), dt, kind="ExternalInput").ap()
        return ins[name]

    xT_d = din("xT", [2, 128, NCH, S])
    cT_d = din("cT", [128, NCH, 2])
    w_ada_d = din("w_ada", [L, NCH, 128, 6 * D])
    b_ada_d = din("b_adaT", [128, L, 48])
    g_d = din("gT", [128, L, 4, NCH])
    w_in_d = din("w_in", [L, 128, NCH, WIN_COLS])
    lam_d = din("lamT", [128, L, 2, 24])
    ldt_d = din("ldt", [128, L, 24])
    b_d = din("bT", [128, L, 2, 24, 16])
    c_d = din("cTs", [128, L, 2, 24, 16])
    dsk_d = din("dsk", [128, L, 24])
    w_glu_d = din("w_glu", [L, 128, 3, 384])
    w_fft_d = din("w_fft", [L, 64, 4, 64])
    sink_d = din("sinkT", [128, L, 3])
    w_out_d = din("w_out", [L, 128, NCH, D])
    w_up_d = din("w_up", [L, NFC, 128, NCH, 256])
    cw_d = din("conv_w", [128, L, 3, 2 * NFC])
    cb_d = din("conv_b", [128, L, 2 * NFC])
    w_down_d = din("w_down", [L, NFC, 128, D])
    ident_d = din("ident", [128, 128])
    maskf_d = din("maskf", [128, 128])
    maskb_d = din("maskb", [128, 128])
    selW_d = din("selW", [128, 8, 240], BF16)
    unselW_d = din("unselW", [128, 8, 240], BF16)
    biasT_d = din("biasT", [128, 6, 3, 128], BF16)
    c64_d = din("c64d", [64, 128])
    s64_d = din("s64d", [64, 128])
    mv_d = din("mvals", [128, 9, 24])
    dft_d = din("dft", [2, S, S], BF16)
    outT_d = nc.dram_tensor("outT", [2, 128, NCH, S], F32, kind="ExternalOutput").ap()
    DBG = os.environ.get('KDBG') == 'ssmdbg'
    if DBG:
        dbg_ug = nc.dram_tensor("dbg_ug", [128, 24 * (S // 8)], BF16, kind="ExternalOutput").ap()
        dbg_x0 = nc.dram_tensor("dbg_x0", [128, (S // 8) * 48], BF16, kind="ExternalOutput").ap()
        dbg_xs = nc.dram_tensor("dbg_xs", [128, (S // 8) * 48], BF16, kind="ExternalOutput").ap()
        dbg_yg = nc.dram_tensor("dbg_yg", [128, 3 * S], BF16, kind="ExternalOutput").ap()
        dbg_u = nc.dram_tensor("dbg_u", [128, 3 * S], BF16, kind="ExternalOutput").ap()
    ssm_scr = nc.dram_tensor("ssm_scr", [L, 128, 24 * 7 * 128], BF16, kind="Internal").ap()

    def dscr(name, shape):
        return nc.dram_tensor(name, list(shape), BF16, kind="Internal").ap()

    win_b = dscr("win_b", [L, 128, NCH, WIN_COLS])
    wout_b = dscr("wout_b", [L, 128, NCH, D])
    wup_b = dscr("wup_b", [L, NFC, 128, NCH * 256])
    wdn_b = dscr("wdn_b", [L, NFC, 128, D])
    wglu_b = dscr("wglu_b", [L, 128, 3 * 384])
    st = contextlib.ExitStack()
    ARENA_BYTES = 212800
    arena_h = st.enter_context(nc.sbuf_tensor("arena", [128, ARENA_BYTES // 4], F32))
    ar = Arena(arena_h, ARENA_BYTES)
    psum = []
    for i in range(8):
        psum.append((st.enter_context(nc.psum_tensor("ps%d" % i, [128, 512], F32)), Dep()))
    prr = [0]

    held = set()

    def P(hold=False):
        while prr[0] in held:
            prr[0] = (prr[0] + 1) % 8
        i = prr[0]
        prr[0] = (i + 1) % 8
        if hold:
            held.add(i)
        return psum[i]

    def Prelease(pp):
        for i in range(8):
            if psum[i] is pp:
                held.discard(i)

    def ld(q, tile_ap, dram_ap, dep, **kw):
        kb.dma(q, tile_ap, dram_ap, writes=[dep], **kw)

    ident = ar.alloc([128, 128]); d_ident = Dep()
    ld("sp", ident[:], ident_d[:, :], d_ident)
    ident_bf = ar.alloc([128, 128], BF16)
    kb.op("dve", CP(ident_bf[:], ident[:]), [d_ident], [d_ident])
    ones_bf = ar.alloc([128, 128], BF16); d_ones = Dep()
    kb.op("dve", MS(ones_bf[:], 1.0), [], [d_ones])
    cT = ar.alloc([128, NCH, 2]); d_cT = Dep()
    ld("sp", cT[:], cT_d[:, :, :], d_cT)
    b_adaT = ar.alloc([128, L, 48]); d_bada = Dep()
    ld("sp", b_adaT[:], b_ada_d[:, :, :], d_bada)
    gT = ar.alloc([128, L, 4, NCH]); d_gT = Dep()
    ld("sp", gT[:], g_d[:, :, :, :], d_gT)
    modT = ar.alloc([128, L, 48, 2]); d_mod = Dep()
    Am = ar.alloc([128, L, 2, NCH, 2]); d_Am = Dep()
    Gm = ar.alloc([128, L, 2, NCH, 2]); d_Gm = Dep()
    cw = ar.alloc([128, L, 3, 2 * NFC]); d_cw = Dep()
    ld("sp", cw[:], cw_d[:, :, :, :], d_cw)
    cb = ar.alloc([128, L, 2 * NFC]); d_cb = Dep()
    ld("sp", cb[:], cb_d[:, :, :], d_cb)
    sinkT = ar.alloc([128, L, 3]); d_sink = Dep()
    ld("sp", sinkT[:], sink_d[:, :, :], d_sink)
    esink = ar.alloc([128, L, 3])
    kb.op("act", ACTF(esink[:], sinkT[:], AF.Exp), [d_sink], [d_sink])
    tmp2 = ar.alloc([128, 512]); d_tmp2 = Dep()
    eps_t = ar.alloc([128, 1]); d_eps = Dep()
    kb.op("dve", MS(eps_t[:], 1e-6), [], [d_eps])

    A8 = ar.alloc([128, L, 2, 24]); d_A8 = Dep()
    d_wscr = Dep()
    mC = ar.mark()
    NSTG = 3
    stg = [ar.alloc([128, 2048]) for _ in range(NSTG)]
    stb = [ar.alloc([128, 2048], BF16) for _ in range(NSTG)]
    d_stg = [Dep() for _ in range(NSTG)]
    d_stb = [Dep() for _ in range(NSTG)]
    stg_i = [0]

    def stream_precast():
        chunks = []
        for l in range(nlayers):
            for kt in range(NCH):
                chunks.append((win_b[l, :, kt, :], w_in_d[l, :, kt, :], WIN_COLS))
                chunks.append((wout_b[l, :, kt, :], w_out_d[l, :, kt, :], D))
            chunks.append((wglu_b[l], w_glu_d[l].rearrange("p k c -> p (k c)"), 3 * 384))
            if "ffn" in parts:
                for cc_ in range(NFC):
                    chunks.append((wup_b[l, cc_], w_up_d[l, cc_].rearrange("p k c -> p (k c)"), NCH * 256))
                    chunks.append((wdn_b[l, cc_], w_down_d[l, cc_], D))

        def load(ci):
            dst_ap, src_ap, n = chunks[ci]
            kb.dma("sp", stg[ci % NSTG][:, 0:n], src_ap, writes=[d_stg[ci % NSTG]])

        for ci in range(min(2, len(chunks))):
            load(ci)
        for ci, (dst_ap, src_ap, n) in enumerate(chunks):
            i = ci % NSTG
            if ci % 2:
                kb.op("act", ACTF(stb[i][:, 0:n], stg[i][:, 0:n], AF.Copy), [d_stg[i]], [d_stb[i]])
            else:
                kb.op("dve", CP(stb[i][:, 0:n], stg[i][:, 0:n]), [d_stg[i]], [d_stb[i]])
            kb.dma("sp", dst_ap, stb[i][:, 0:n], reads=[d_stb[i]], writes=[Dep()])
            if ci + 2 < len(chunks):
                load(ci + 2)

    m0 = ar.mark()
    wa = [ar.alloc([128, 3072]) for _ in range(2)]
    d_wa = [Dep(), Dep()]

    def stream_adaln():
        wi = 0
        for l in range(nlayers):
            acc_ = P(hold=True)
            pt, pd = acc_
            kb.op("dve", MS(pt[:, 0:96], 0.0), [], [pd])
            for kt in range(NCH):
                for hf in range(2):
                    b = wi % 2
                    wi += 1
                    ld("sp", wa[b][:], w_ada_d[l, kt, :, hf * 3072:(hf + 1) * 3072], d_wa[b])
                    fs = []
                    for m in range(24):
                        mm = hf * 24 + m
                        fs.append(MMZ(pt[:, mm * 2:mm * 2 + 2], wa[b][:, m * 128:(m + 1) * 128], cT[:, kt, :]))
                    kb.op("pe", SEQ(fs), [d_wa[b], d_cT], [pd])
            kb.op("dve", TT(modT[:, l], pt[:, 0:96].rearrange("p (m b) -> p m b", b=2),
                            b_adaT[:, l, :].unsqueeze(2).to_broadcast([128, 48, 2]), ALU.add),
                  [pd, d_bada], [d_mod])
            Prelease(acc_)
            for i, (sc0, gt0, gpre, gpost) in enumerate(((8, 16, 0, 1), (32, 40, 2, 3))):
                kb.op("dve", STT(Am[:, l, i], modT[:, l, sc0:sc0 + 8, :], 1.0,
                                 gT[:, l, gpre, :].unsqueeze(2).to_broadcast([128, NCH, 2]), ALU.add, ALU.mult),
                      [d_mod, d_gT], [d_Am])
                kb.op("dve", TT(Gm[:, l, i], modT[:, l, gt0:gt0 + 8, :],
                                gT[:, l, gpost, :].unsqueeze(2).to_broadcast([128, NCH, 2]), ALU.mult),
                      [d_mod, d_gT], [d_Gm])


    MAGIC = 12582912.0
    if "ssm" in parts:
        maskf = ar.alloc([128, 128]); maskb = ar.alloc([128, 128]); d_mask = Dep()
        ld("sp", maskf[:], maskf_d[:, :], d_mask)
        ld("sp", maskb[:], maskb_d[:, :], d_mask)

    def stream_gen():
        if "ssm" not in parts:
            return
        for l in range(nlayers):
            m1 = ar.mark()
            dg = Dep()

            def G(fn, eng="dve", extra=()):
                kb.op(eng, fn, list(extra), [dg])

            lam = ar.alloc([128, 2, 24]); ldt = ar.alloc([128, 24]); Bt = ar.alloc([128, 2, 24, 16])
            Ct = ar.alloc([128, 2, 24, 16]); mv = ar.alloc([128, 9, 24]); dsk = ar.alloc([128, 24])
            ld("sp", lam[:], lam_d[:, l], dg); ld("sp", ldt[:], ldt_d[:, l], dg)
            ld("sp", Bt[:], b_d[:, l], dg); ld("sp", Ct[:], c_d[:, l], dg)
            ld("sp", mv[:], mv_d[:, :, :], dg); ld("sp", dsk[:], dsk_d[:, l], dg)
            dtt = ar.alloc([128, 24]); zr = ar.alloc([128, 24]); zi = ar.alloc([128, 24])
            G(ACTF(dtt[:], ldt[:], AF.Exp), "act")
            G(TT(zr[:], lam[:, 0], dtt[:], ALU.mult)); G(TT(zi[:], lam[:, 1], dtt[:], ALU.mult))
            sh = [128, 9, 24]
            ang = ar.alloc(sh); mzr = ar.alloc(sh); E = ar.alloc(sh); Ei = ar.alloc(sh)
            t1 = ar.alloc(sh); r1 = ar.alloc(sh); sn = ar.alloc(sh); cs = ar.alloc(sh)
            PWr = ar.alloc(sh); PWi = ar.alloc(sh); NWr = ar.alloc(sh); NWi = ar.alloc(sh)
            zb = lambda z: z[:].unsqueeze(1).to_broadcast(sh)
            G(TT(ang[:], mv[:], zb(zi), ALU.mult)); G(TT(mzr[:], mv[:], zb(zr), ALU.mult))
            G(ACTF(E[:], mzr[:], AF.Exp), "act"); G(ACTF(Ei[:], mzr[:], AF.Exp, scale=-1.0), "act")
            a2 = ar.alloc(sh)
            for (dst, shift) in ((sn, 0.0), (cs, PI / 2)):
                G(TS(a2[:], ang[:], shift, None, ALU.add))
                G(TS(t1[:], a2[:], 1.0 / (2 * PI), MAGIC, ALU.mult, ALU.add))
                G(TS(t1[:], t1[:], -MAGIC, None, ALU.add))
                G(STT(r1[:], t1[:], -2 * PI, a2[:], ALU.mult, ALU.add))
                G(TS(r1[:], r1[:], -3.14159, 3.14159, ALU.max, ALU.min))
                G(ACTF(dst[:], r1[:], AF.Sin), "act")
            G(TT(PWr[:], E[:], cs[:], ALU.mult)); G(TT(PWi[:], E[:], sn[:], ALU.mult))
            G(TT(NWr[:], Ei[:], cs[:], ALU.mult))
            kb.op("dve", CP(A8[:, l, 0, :], PWr[:, 8, :]), [dg], [d_A8])
            kb.op("dve", CP(A8[:, l, 1, :], PWi[:, 8, :]), [dg], [d_A8])
            G(STT(NWi[:], Ei[:], -1.0, sn[:], ALU.mult, ALU.mult))
            s24 = [128, 24]
            nr = ar.alloc(s24); den = ar.alloc(s24); ta = ar.alloc(s24); tb = ar.alloc(s24)
            kr = ar.alloc(s24); ki = ar.alloc(s24)
            G(TS(nr[:], PWr[:, 1, :], -1.0, None, ALU.add))
            G(TT(den[:], lam[:, 0], lam[:, 0], ALU.mult)); G(TT(ta[:], lam[:, 1], lam[:, 1], ALU.mult))
            G(TT(den[:], den[:], ta[:], ALU.add)); G(RCP(den[:], den[:]))
            G(TT(ta[:], nr[:], lam[:, 0], ALU.mult)); G(TT(tb[:], PWi[:, 1, :], lam[:, 1], ALU.mult))
            G(TT(ta[:], ta[:], tb[:], ALU.add)); G(TT(kr[:], ta[:], den[:], ALU.mult))
            G(TT(ta[:], PWi[:, 1, :], lam[:, 0], ALU.mult)); G(TT(tb[:], nr[:], lam[:, 1], ALU.mult))
            G(TT(ta[:], ta[:], tb[:], ALU.subtract)); G(TT(ki[:], ta[:], den[:], ALU.mult))
            sB = [128, 24, 16]
            Bbr = ar.alloc(sB); Bbi = ar.alloc(sB); u1 = ar.alloc(sB); u2 = ar.alloc(sB)
            kbc = lambda k: k[:].unsqueeze(2).to_broadcast(sB)
            G(TT(u1[:], Bt[:, 0], kbc(kr), ALU.mult)); G(TT(u2[:], Bt[:, 1], kbc(ki), ALU.mult))
            G(TT(Bbr[:], u1[:], u2[:], ALU.subtract))
            G(TT(u1[:], Bt[:, 1], kbc(kr), ALU.mult)); G(TT(u2[:], Bt[:, 0], kbc(ki), ALU.mult))
            G(TT(Bbi[:], u1[:], u2[:], ALU.add))
            s8 = [128, 24, 8, 16]
            MBT = ar.alloc([128, 2, 24, 8, 16], BF16); QN = ar.alloc([128, 2, 24, 8, 16], BF16)
            MC = ar.alloc([128, 2, 24, 8, 16], BF16)
            w1 = ar.alloc(s8); w2 = ar.alloc(s8)

            def pw(tile, p0, m0_, mstep):
                return tile.cu(p0, 64, m0_ * 24, [[1, 24], [mstep * 24, 8], [0, 16]])

            def bc8(tile3, ri, p0):
                base = ri * 24 * 16 if ri is not None else 0
                return tile3.cu(p0, 64, base, [[16, 24], [0, 8], [1, 16]])

            def cmul(out, ar_, ai_, br_, bi_, p0, neg_im=False):
                o_r = out.cu(p0, 64, 0, [[128, 24], [16, 8], [1, 16]])
                o_i = out.cu(p0, 64, 24 * 128, [[128, 24], [16, 8], [1, 16]])
                a = w1.cu(p0, 64, 0, [[128, 24], [16, 8], [1, 16]])
                b = w2.cu(p0, 64, 0, [[128, 24], [16, 8], [1, 16]])
                G(TT(a, ar_, br_, ALU.mult)); G(TT(b, ai_, bi_, ALU.mult)); G(TT(o_r, a, b, ALU.subtract))
                G(TT(a, ar_, bi_, ALU.mult)); G(TT(b, ai_, br_, ALU.mult))
                if neg_im:
                    G(STT(o_i, a, -1.0, b, ALU.mult, ALU.subtract))
                else:
                    G(TT(o_i, a, b, ALU.add))

            for p0, (mb0, mbs), (qn0, qns), (mc0, mcs) in ((0, (7, -1), (7, -1), (1, 1)), (64, (0, 1), (0, 1), (8, -1))):
                cmul(MBT, pw(PWr, p0, mb0, mbs), pw(PWi, p0, mb0, mbs), bc8(Bbr, None, p0), bc8(Bbi, None, p0), p0)
                cmul(QN, pw(NWr, p0, qn0, qns), pw(NWi, p0, qn0, qns), bc8(Ct, 0, p0), bc8(Ct, 1, p0), p0, neg_im=True)
                cmul(MC, pw(PWr, p0, mc0, mcs), pw(PWi, p0, mc0, mcs), bc8(Ct, 0, p0), bc8(Ct, 1, p0), p0, neg_im=True)
            mats = ar.alloc([128, 24, 7, 128], BF16); d_mats = Dep()
            tz1 = ar.alloc([128, 128]); tz2 = ar.alloc([128, 128]); d_tz = Dep()
            kb.op("dve", MS(mats[:, :, 3:7, :], 0.0), [], [d_mats])
            for di in range(2):
                kb.op("dve", CP(mats.cu(64 * di, 64, (3 + di) * 128, [[2 * 128, 2], [7 * 128, 24], [1, 128]]),
                                MC.cu(64 * di, 64, 0, [[24 * 128, 2], [128, 24], [1, 128]])), [dg], [d_mats])
            KSKIP = os.environ.get('KSKIP', '')
            for g in range(24 if 'grp' not in KSKIP else 0):
                pt, pd = P()
                pt2, pd2 = P()
                fs = []
                for di in range(2):
                    for ri in range(2):
                        fs.append(MM((pt if di == 0 else pt2)[:, 0:128],
                                     MBT.cu(di * 64, 64, (ri * 24 + g) * 128, [[1, 128]]),
                                     QN.cu(di * 64, 64, (ri * 24 + g) * 128, [[1, 128]]),
                                     start=(ri == 0), stop=(ri == 1)))
                for ri in range(2):
                    fs.append(MM(pt[:, 256 + ri * 128:256 + (ri + 1) * 128],
                                 MBT.cu(0, 128, (ri * 24 + g) * 128, [[1, 128]]), ident_bf[:]))
                if 'gpe' not in KSKIP:
                    kb.op("pe", SEQ(fs), [dg, d_ident], [pd, pd2])
                if 'gdve' in KSKIP:
                    continue
                kb.op("dve", TT(tz1[:], pt[:, 0:128], maskf[:], ALU.mult), [pd, d_mask], [d_tz])
                kb.op("dve", TT(tz2[:], pt2[:, 0:128], maskb[:], ALU.mult), [pd2, d_mask], [d_tz])
                kb.op("dve", TT(tz1[:], tz1[:], tz2[:], ALU.add), [d_tz], [d_tz])
                kb.op("dve", STT(mats[:, g, 0, :], ident[:], dsk[:, g:g + 1], tz1[:], ALU.mult, ALU.add),
                      [d_tz, dg, d_ident], [d_mats])
                kb.op("dve", CP(mats[:, g, 1:3, :], pt[:, 256:512].rearrange("p (r x) -> p r x", r=2)),
                      [pd], [d_mats])
            if 'dma' not in KSKIP:
                kb.dma("sp", ssm_scr[l], mats[:].rearrange("p g r x -> p (g r x)"), reads=[d_mats], writes=[Dep()])
            kb.barrier()
            ar.release(m1)


    kb.interleave([stream_precast, stream_adaln, stream_gen], [6, 2, 10])
    kb.barrier()
    ar.release(mC)

    xT = ar.alloc([128, NCH, S]); d_x = [Dep() for _ in range(N5)]
    hT = ar.alloc([128, NCH, S], BF16); d_h = [Dep() for _ in range(N5)]
    ccT = hT
    d_out = []
    j5 = lambda j: slice(j * 512, (j + 1) * 512)
    evq = [0]

    def evac(out, in_, reads, writes, func=AF.Copy):
        evq[0] += 1
        if evq[0] % 2:
            kb.op("act", ACTF(out, in_, func), reads, writes)
        else:
            kb.op("dve", CP(out, in_), reads, writes)

    def rms_stats(src_sq_fn, j, tmp_sq, d_sq, rstd, d_rstd):
        pt, pd = P()
        kb.op("pe", SEQ([MM(pt[:, :], ones_bf[:], tmp_sq[:, c_, :], start=(c_ == 0), stop=(c_ == NCH - 1))
                         for c_ in range(NCH)]), list(d_sq) + [d_ones], [pd])
        kb.op("act", ACTF(rstd[:], pt[:, :], AF.Sqrt, bias=eps_t[:], scale=1.0 / D), [pd, d_eps], [d_rstd])
        kb.op("dve", RCP(rstd[:], rstd[:]), [d_rstd], [d_rstd])

    def pre_norm(l, i, b, tmps, js=None):
        sq, d_sq, rstd, d_rstd, tmp, d_tmp = tmps
        sh0 = 0 if i == 0 else 24
        for j in (range(N5) if js is None else js):
            kb.op("act", ACTF(sq[:], xT[:, :, j5(j)], AF.Square), [d_x[j]], list(d_sq))
            rms_stats(None, j, sq, d_sq, rstd, d_rstd)
            for c_ in range(NCH):
                tb_, dtb_ = (tmp, d_tmp) if c_ % 2 == 0 else (tmp2, d_tmp2)
                kb.op("dve", STT(tb_[:], xT[:, c_, j5(j)], Am[:, l, i, c_, b:b + 1], rstd[:], ALU.mult, ALU.mult),
                      [d_x[j], d_Am, d_rstd], [dtb_])
                kb.op("act", ACTF(hT[:, c_, j5(j)], tb_[:], AF.Identity, bias=modT[:, l, sh0 + c_, b:b + 1]),
                      [dtb_, d_mod], [d_h[j]])

    def post_norm_residual(l, i, b, j, y_sb, d_y, sq, d_sq, rstd, d_rstd, tmp, d_tmp):
        rms_stats(None, j, sq, d_sq, rstd, d_rstd)
        for c_ in range(NCH):
            tb_, dtb_ = (tmp, d_tmp) if c_ % 2 == 0 else (tmp2, d_tmp2)
            kb.op("dve", STT(tb_[:], y_sb[:, c_, :], Gm[:, l, i, c_, b:b + 1], rstd[:], ALU.mult, ALU.mult),
                  [d_y[c_], d_Gm, d_rstd], [dtb_])
            kb.op("pool", TT(xT[:, c_, j5(j)], xT[:, c_, j5(j)], tb_[:], ALU.add), [dtb_], [d_x[j]])

    def proj_post(l, i, b, w_sb, d_w, nk, rhs_fn, rhs_deps_fn):
        y_sb = ar.alloc([128, NCH, 512]); d_y = [Dep() for _ in range(NCH)]
        sq = ar.alloc([128, NCH, 512], BF16); d_sq = [Dep() for _ in range(NCH)]
        rstd = ar.alloc([128, 512]); d_rstd = Dep()
        tmp = ar.alloc([128, 512]); d_tmp = Dep()
        for j in range(N5):
            import os
            for oc in range(NCH if os.environ.get('KDBG2') != 'nomm' else 0):
                pt, pd = P()
                kb.op("pe", SEQ([MM(pt[:, :], w_sb(kt, oc), rhs_fn(kt, j), start=(kt == 0), stop=(kt == nk - 1))
                                 for kt in range(nk)]), [d_w] + rhs_deps_fn(j), [pd])
                kb.op("dve", CP(y_sb[:, oc, :], pt[:, :]), [pd], [d_y[oc]])
                kb.op("act", ACTF(sq[:, oc, :], y_sb[:, oc, :], AF.Square), [d_y[oc]], [d_sq[oc]])
            import os
            if os.environ.get('KDBG') != 'wo1':
                post_norm_residual(l, i, b, j, y_sb, d_y, sq, d_sq, rstd, d_rstd, tmp, d_tmp)

    for s in range(2):
        for c_ in range(NCH):
            kb.dma("sp", xT[:, c_, :], xT_d[s, :, c_, :], writes=d_x)
        import os
        for l in range(nlayers if os.environ.get('KDBG') != 'ada' else 0):
            mk0 = ar.mark()
            uT = ar.alloc([128, 3, S], BF16); fT = ar.alloc([128, 2, S], BF16)
            mkA = ar.mark()
            qT = ar.alloc([128, 5, S], BF16)
            v_sb = ar.alloc([128, NT, 2, 2, 128], BF16)
            mkW = ar.mark()
            w_in_sb = ar.alloc([128, NCH, WIN_COLS], BF16); d_win = Dep()
            d_zl = [Dep() for _ in range(10)]; d_v = Dep()
            pn_t = (ar.alloc([128, NCH, 512], BF16), [Dep() for _ in range(NCH)], ar.alloc([128, 512]), Dep(), ar.alloc([128, 512]), Dep())
            for kt in range(NCH):
                kb.dma("sp", w_in_sb[:, kt, :], win_b[l, :, kt, :], writes=[d_win])
            kb.op("pool", MS(v_sb[:], 0.0), [], [d_v])
            for j in range(N5):
                pre_norm(l, 0, s, pn_t, [j])
                for oc in range(10):
                    dst = uT[:, oc] if oc < 3 else (fT[:, oc - 3] if oc < 5 else qT[:, oc - 5])
                    pt, pd = P()
                    kb.op("pe", SEQ([MM(pt[:, :], w_in_sb[:, kt, oc * 128:(oc + 1) * 128], hT[:, kt, j5(j)],
                                        start=(kt == 0), stop=(kt == NCH - 1)) for kt in range(NCH)]),
                          [d_win, d_h[j]], [pd])
                    if 5 <= oc < 8:
                        kb.op("act", ACTF(dst[:, j5(j)], pt[:, :], AF.Identity, scale=0.125), [pd], [d_zl[oc]])
                    else:
                        evac(dst[:, j5(j)], pt[:, :], [pd], [d_zl[oc]])
                for tb in range(4 * j, 4 * j + 4):
                    pt, pd = P()
                    kb.op("pe", SEQ([MM(pt[:, 0:128], hT[:, kt, tb * 128:(tb + 1) * 128], w_in_sb[:, kt, 1280:1408],
                                        start=(kt == 0), stop=(kt == NCH - 1)) for kt in range(NCH)]),
                          [d_win, d_h[tb // 4]], [pd])
                    pv = pt[:, 0:128].rearrange("p (k d) -> p k d", k=2)
                    kb.op("act", ACTF(v_sb[:, tb, :, 0, 0:64], pv, AF.Copy), [pd], [d_v])
                    kb.op("dve", CP(v_sb[:, tb, :, 1, 64:128], pv), [pd], [d_v])
            kb.barrier()
            ar.release(mkW)
            if os.environ.get('KDBG') == 'win':
                ar.release(mk0)
                continue
            d_cc = d_h
            if "att" in parts:
                biasT = ar.alloc([128, 6, 3, 128], BF16); d_bias = Dep()
                ld("sp", biasT[:], biasT_d[:, :, :, :], d_bias)
                onesLR = ar.alloc([128, 2, 128], BF16); d_olr = Dep()
                kb.op("pool", MS(onesLR[:], 0.0), [], [d_olr])
                kb.op("pool", MS(onesLR[:, 0, 0:64], 1.0), [], [d_olr])
                kb.op("pool", MS(onesLR[:, 1, 64:128], 1.0), [], [d_olr])
                NAB = 3
                scb = [ar.alloc([128, 3, 128]) for _ in range(2 * NAB)]; d_scb = [Dep() for _ in range(2 * NAB)]
                pTb = [ar.alloc([128, 2, 3, 128], BF16) for _ in range(NAB)]; d_pT = [Dep() for _ in range(NAB)]
                dnb = [ar.alloc([128, 128]) for _ in range(2)]; d_dnb = [Dep(), Dep()]
                blocks = [(jp, n) for jp in range(3) for n in range(NT)]

                def att_stage1(it, jp, n):
                    kbs = [k_ for k_ in range(3) if 0 <= n + k_ - 1 < NT]
                    k0, k1 = kbs[0], kbs[-1] + 1
                    pb = it % NAB
                    for hh in range(2):
                        h = 2 * jp + hh
                        kv = h // 3
                        pt, pd = P()
                        pe_bias = (hh == 0)
                        fs = []
                        for k_ in kbs:
                            fs.append(MM(pt[:, k_ * 128:(k_ + 1) * 128],
                                         qT[64 * hh:64 * hh + 64, 3 + kv, (n + k_ - 1) * 128:(n + k_) * 128],
                                         qT[64 * hh:64 * hh + 64, jp, n * 128:(n + 1) * 128], start=True, stop=not pe_bias))
                            if pe_bias:
                                fs.append(MM(pt[:, k_ * 128:(k_ + 1) * 128], ident_bf[:], biasT[:, h, k_, :], start=False, stop=True))
                        kb.op("pe", SEQ(fs), d_zl[5:10] + [d_bias, d_ident], [pd])
                        psv = pt[:, k0 * 128:k1 * 128].rearrange("p (k q) -> p k q", q=128)
                        if pe_bias:
                            kb.op("act", ACTF(pTb[pb][:, hh, k0:k1, :], psv, AF.Exp), [pd], [d_pT[pb]])
                        else:
                            sb_ = scb[pb]
                            kb.op("dve", TT(sb_[:, k0:k1, :], psv, biasT[:, h, k0:k1, :], ALU.add), [pd, d_bias], [d_scb[pb]])
                            kb.op("act", ACTF(pTb[pb][:, hh, k0:k1, :], sb_[:, k0:k1, :], AF.Exp), [d_scb[pb]], [d_pT[pb]])

                def att_stage2(it, jp, n):
                    kbs = [k_ for k_ in range(3) if 0 <= n + k_ - 1 < NT]
                    pb = it % NAB
                    dn, d_dn = dnb[it % 2], d_dnb[it % 2]
                    pt, pd = P()
                    fs = []
                    for gi in range(2):
                        pairs = [(hh, k_) for hh in range(2) for k_ in kbs]
                        for ii, (hh, k_) in enumerate(pairs):
                            lhs = v_sb[:, n + k_ - 1, (2 * jp + hh) // 3, hh, :] if gi == 0 else onesLR[:, hh, :]
                            fs.append(MM(pt[:, gi * 128:(gi + 1) * 128], lhs, pTb[pb][:, hh, k_, :],
                                         start=(ii == 0), stop=(ii == len(pairs) - 1)))
                    kb.op("pe", SEQ(fs), [d_pT[pb], d_v, d_olr], [pd])
                    kb.op("dve", TS(dn[:], pt[:, 128:256], esink[:, l, jp:jp + 1], None, ALU.add), [pd, d_sink], [d_dn])
                    kb.op("dve", RCP(dn[:], dn[:]), [d_dn], [d_dn])
                    kb.op("dve", TT(ccT[:, 5 + jp, n * 128:(n + 1) * 128], pt[:, 0:128], dn[:], ALU.mult),
                          [pd, d_dn], [d_cc[n // 4]])

                for it in range(len(blocks) + 1):
                    if it < len(blocks):
                        att_stage1(it, *blocks[it])
                    if it >= 1:
                        att_stage2(it - 1, *blocks[it - 1])
            else:
                for j in range(N5):
                    kb.op("dve", MS(ccT[:, 5:8, j5(j)], 0.0), [], [d_cc[j]])
            kb.barrier()
            ar.release(mkA)
            use_fft = "fft" in parts
            use_ssm = "ssm" in parts and os.environ.get('KDBG') != 'ssmgen'
            if use_ssm:
                mkS = ar.mark()
                selW = ar.alloc([128, 8, 240], BF16); d_sel = Dep()
                ld("sp", selW[:], selW_d[:, :, :], d_sel)
                wglu = ar.alloc([128, 3, 384], BF16); d_wg = Dep()
                kb.dma("sp", wglu[:].rearrange("p k c -> p (k c)"), wglu_b[l], writes=[d_wg])
                Ug = ar.alloc([128, 24, K8], BF16); d_Ug = Dep()
                Xs = ar.alloc([128, K8, 2, 24], BF16); d_Xs = Dep()
                mb = [ar.alloc([128, 7, 128], BF16) for _ in range(2)]; d_mb = [Dep() for _ in range(2)]
                mi_ = [0]
                AA = ar.alloc([128, 2, 24]); AB = ar.alloc([128, 2, 24]); d_A = Dep()
                BL = 32
                NBk = K8 // BL
                shp = [128, NBk, 2, 24]
                Fp = [ar.alloc(shp) for _ in range(2)]; d_Fp = [Dep(), Dep()]
                d_Xo = Dep()
                s1 = ar.alloc(shp); s2_ = ar.alloc(shp); d_st = Dep()
                tB0 = ar.alloc(shp); tB1 = ar.alloc(shp); Cc = ar.alloc(shp)
                Pq = [ar.alloc([128, 2, 24]) for _ in range(2)]; Pw = [ar.alloc([128, 2, 24]) for _ in range(2)]
                w_a = ar.alloc([128, 2, 24]); w_b = ar.alloc([128, 2, 24])
                AAb = ar.alloc([128, 2, 24]); ABb = ar.alloc([128, 2, 24])

            def fft_section():
                mkF = ar.mark()
                c64 = ar.alloc([64, 128]); s64 = ar.alloc([64, 128]); wf = ar.alloc([64, 4, 64]); d_fc = Dep()
                ld("sp", c64[:], c64_d[:, :], d_fc); ld("sp", s64[:], s64_d[:, :], d_fc)
                ld("sp", wf[:], w_fft_d[l], d_fc)
                W2 = ar.alloc([128, 2, 2, 2, 64], BF16); d_W2 = Dep()
                kb.op("pool", MS(W2[:], 0.0), [], [d_W2])
                for h in range(4):
                    pt, pd = P()
                    kb.op("pe", SEQ([MM(pt[:, 0:64], c64[:], wf[:, h, :]), MM(pt[:, 64:128], s64[:], wf[:, h, :])]),
                          [d_fc], [pd])
                    hh = h % 2
                    kb.op("dve", CP(W2[64 * hh:64 * hh + 64, h // 2, :, hh, :],
                                    pt[64 * hh:64 * hh + 64, 0:128].rearrange("p (c e) -> p c e", c=2)), [pd], [d_W2])
                G_sb = ar.alloc([128, NT, 2, 256], BF16); d_G = Dep()
                for tb in range(NT):
                    pt, pd = P()
                    kb.op("pe", SEQ([MM(pt[:, jj * 256:(jj + 1) * 256], fT[:, jj, tb * 128:(tb + 1) * 128],
                                        W2[:, jj].rearrange("p c h e -> p (c h e)")) for jj in range(2)]),
                          d_zl[3:5] + [d_W2], [pd])
                    evac(G_sb[:, tb].rearrange("p j x -> p (j x)"), pt[:, :], [pd], [d_G])
                PG = min(4, NT)
                dbuf = [ar.alloc([128, PG, 512], BF16) for _ in range(2)]; d_db = [Dep() for _ in range(2)]
                di = 0
                for j in range(N5):
                    acc = [P(hold=True), P(hold=True)]
                    first = True
                    for cs_ in range(2):
                        for pg in range(NT // PG):
                            bi = di % 2
                            di += 1
                            ld("sp", dbuf[bi][:], dft_d[cs_, pg * PG * 128:(pg + 1) * PG * 128, j5(j)]
                               .rearrange("(a p) x -> p a x", p=128), d_db[bi])
                            last = (cs_ == 1 and pg == NT // PG - 1)
                            for jj in range(2):
                                kb.op("pe", SEQ([MM(acc[jj][0][:, :], G_sb[:, pg * PG + a, jj, cs_ * 128:(cs_ + 1) * 128],
                                                    dbuf[bi][:, a, :], start=(first and a == 0),
                                                    stop=(last and a == PG - 1)) for a in range(PG)]),
                                      [d_G, d_db[bi]], [acc[jj][1]])
                            first = False
                    for jj in range(2):
                        evac(ccT[:, 3 + jj, j5(j)], acc[jj][0][:, :], [acc[jj][1]], [d_cc[j]])
                        Prelease(acc[jj])
                kb.barrier()
                ar.release(mkF)

            def ssmA_section():
                for g in range(24):
                    ch, g8 = g // 8, g % 8
                    pt, pd = P()
                    kb.op("pe", SEQ([MM(pt[:, 0:K8], selW[:, g8, 112 - 16 * s2:112 - 16 * s2 + 128],
                                        uT.cu(0, 128, ch * S + s2, [[8, K8]]), start=(s2 == 0), stop=(s2 == 7))
                                     for s2 in range(8)]), [d_sel] + d_zl[0:3], [pd])
                    evac(Ug[:, g, :], pt[:, 0:K8], [pd], [d_Ug])
                    bi = mi_[0] % 2
                    mi_[0] += 1
                    ld("sp", mb[bi][:], ssm_scr[l, :, g * 896:(g + 1) * 896].rearrange("p (r x) -> p r x", r=7), d_mb[bi])
                    pt, pd = P()
                    kb.op("pe", SEQ([MM(pt[:, ri * K8:(ri + 1) * K8], mb[bi][:, 1 + ri, :], Ug[:, g, :]) for ri in range(2)]),
                          [d_mb[bi], d_Ug], [pd])
                    if g % 2:
                        kb.op("act", ACTF(Xs.cu(0, 64, g, [[24, 2], [48, K8]]),
                                          pt[0:64, 0:2 * K8].rearrange("p (r k) -> p r k", r=2), AF.Copy), [pd], [d_Xs])
                        kb.op("act", ACTF(Xs.cu(64, 64, (K8 - 1) * 48 + g, [[24, 2], [-48, K8]]),
                                          pt[64:128, 0:2 * K8].rearrange("p (r k) -> p r k", r=2), AF.Copy), [pd], [d_Xs])
                    else:
                        kb.op("dve", CP(Xs.cu(0, 64, g, [[24, 2], [48, K8]]),
                                        pt[0:64, 0:2 * K8].rearrange("p (r k) -> p r k", r=2)), [pd], [d_Xs])
                        kb.op("dve", CP(Xs.cu(64, 64, (K8 - 1) * 48 + g, [[24, 2], [-48, K8]]),
                                        pt[64:128, 0:2 * K8].rearrange("p (r k) -> p r k", r=2)), [pd], [d_Xs])
                if DBG and s == 0 and l == 0:
                    kb.dma("sp", dbg_ug[:, :], Ug[:].rearrange("p g k -> p (g k)"), reads=[d_Ug], writes=[Dep()])
                    kb.dma("sp", dbg_x0[:, :], Xs[:].rearrange("p k r g -> p (k r g)"), reads=[d_Xs], writes=[Dep()])
                    kb.dma("sp", dbg_u[:, :], uT[:].rearrange("p c t -> p (c t)"), reads=d_zl[0:3], writes=[Dep()])
                    kb.barrier()
                kb.op("dve", CP(AA[:, 0, :], A8[:, l, 0, :]), [d_A8], [d_A]); kb.op("dve", CP(AA[:, 1, :], A8[:, l, 0, :]), [d_A8], [d_A])
                kb.op("dve", TS(AB[:, 0, :], A8[:, l, 1, :], -1.0, None, ALU.mult), [d_A8], [d_A])
                kb.op("dve", CP(AB[:, 1, :], A8[:, l, 1, :]), [d_A8], [d_A])
                def Xv(i, b0=0):
                    return Xs.cu(0, 128, (b0 * BL + i) * 48, [[BL * 48, NBk - b0], [24, 2], [1, 24]])

                def bc(t, nb):
                    return t.cu(0, 128, 0, [[0, nb], [24, 2], [1, 24]])

                def bcsw(t, nb):
                    return t.cu(0, 128, 24, [[0, nb], [-24, 2], [1, 24]])

                def sw4(t, nb):
                    return t.cu(0, 128, 24, [[48, nb], [-24, 2], [1, 24]])

                kb.op("dve", CP(Fp[0][:], Xv(0)), [d_Xs], [d_Fp[0]])
                for i_ in range(1, BL):
                    Fo, Fn = Fp[(i_ - 1) % 2], Fp[i_ % 2]
                    do, dn_ = d_Fp[(i_ - 1) % 2], d_Fp[i_ % 2]
                    kb.op("dve", TT(s1[:], bc(AA, NBk), Fo[:], ALU.mult), [d_A, do], [d_st])
                    kb.op("dve", TT(s2_[:], bc(AB, NBk), sw4(Fo, NBk), ALU.mult), [do], [d_st])
                    kb.op("dve", TT(s1[:], s1[:], s2_[:], ALU.add), [d_st], [d_st])
                    kb.op("dve", TT(Fn[:], s1[:], Xv(i_), ALU.add), [d_st, d_Xs], [dn_])
                    kb.op("act", ACTF(Xv(i_), Fn[:], AF.Copy), [dn_], [d_Xo])
                if NBk > 1:
                    Fl, d_Fl = Fp[(BL - 1) % 2], d_Fp[(BL - 1) % 2]
                    d_pq = Dep()
                    kb.op("dve", CP(Pq[0][:], A8[:, l]), [d_A8], [d_pq])
                    nsq = BL.bit_length() - 1
                    for q_ in range(nsq):
                        po, pn = Pq[q_ % 2], Pq[(q_ + 1) % 2]
                        kb.op("dve", TT(w_a[:, 0, :], po[:, 0, :], po[:, 0, :], ALU.mult), [d_pq], [d_pq])
                        kb.op("dve", TT(w_a[:, 1, :], po[:, 1, :], po[:, 1, :], ALU.mult), [d_pq], [d_pq])
                        kb.op("dve", TT(pn[:, 0, :], w_a[:, 0, :], w_a[:, 1, :], ALU.subtract), [d_pq], [d_pq])
                        kb.op("dve", STT(pn[:, 1, :], po[:, 0, :], 2.0, po[:, 1, :], ALU.mult, ALU.mult), [d_pq], [d_pq])
                    pBL = Pq[nsq % 2]
                    kb.op("dve", CP(AAb[:, 0, :], pBL[:, 0, :]), [d_pq], [d_pq]); kb.op("dve", CP(AAb[:, 1, :], pBL[:, 0, :]), [d_pq], [d_pq])
                    kb.op("dve", TS(ABb[:, 0, :], pBL[:, 1, :], -1.0, None, ALU.mult), [d_pq], [d_pq])
                    kb.op("dve", CP(ABb[:, 1, :], pBL[:, 1, :]), [d_pq], [d_pq])
                    d_Cc = Dep()
                    kb.op("dve", CP(Cc[:, 0], Fl[:, 0]), [d_Fl], [d_Cc])
                    for b_ in range(1, NBk):
                        cprev_sw = Cc.cu(0, 128, (b_ - 1) * 48 + 24, [[-24, 2], [1, 24]])
                        kb.op("dve", TT(w_a[:], AAb[:], Cc[:, b_ - 1], ALU.mult), [d_pq, d_Cc], [d_pq])
                        kb.op("dve", TT(w_b[:], ABb[:], cprev_sw, ALU.mult), [d_Cc], [d_pq])
                        kb.op("dve", TT(w_a[:], w_a[:], w_b[:], ALU.add), [d_pq], [d_pq])
                        kb.op("dve", TT(Cc[:, b_], w_a[:], Fl[:, b_], ALU.add), [d_pq, d_Fl], [d_Cc])
                    nb1 = NBk - 1
                    CA, CB = s1, s2_
                    d_CAB = Dep()
                    cp_r = Cc.cu(0, 128, 0, [[48, nb1], [0, 2], [1, 24]])
                    kb.op("dve", CP(CA.cu(0, 128, 0, [[48, nb1], [24, 2], [1, 24]]), cp_r), [d_Cc, d_st], [d_CAB])
                    kb.op("dve", TS(CB.cu(0, 128, 0, [[48, nb1], [1, 24]]), Cc.cu(0, 128, 24, [[48, nb1], [1, 24]]), -1.0, None, ALU.mult),
                          [d_Cc, d_st], [d_CAB])
                    kb.op("dve", CP(CB.cu(0, 128, 24, [[48, nb1], [1, 24]]), Cc.cu(0, 128, 24, [[48, nb1], [1, 24]])), [d_Cc], [d_CAB])
                    d_pw = Dep()
                    kb.op("dve", CP(Pw[0][:], A8[:, l]), [d_A8], [d_pw])
                    tA = [Fp[0], Fp[1]]; tB = [tB0, tB1]; d_tA = [Dep(), Dep()]; d_tB = [Dep(), Dep()]
                    for i_ in range(BL):
                        pw = Pw[i_ % 2]
                        ta, tb_ = tA[i_ % 2], tB[i_ % 2]
                        tav = ta.cu(0, 128, 0, [[48, nb1], [24, 2], [1, 24]])
                        tbv = tb_.cu(0, 128, 0, [[48, nb1], [24, 2], [1, 24]])
                        kb.op("dve", TT(tav, CA.cu(0, 128, 0, [[48, nb1], [24, 2], [1, 24]]), bc(pw, nb1), ALU.mult),
                              [d_CAB, d_pw], [d_tA[i_ % 2], d_Fp[i_ % 2]])
                        kb.op("pool", TT(tbv, CB.cu(0, 128, 0, [[48, nb1], [24, 2], [1, 24]]), bcsw(pw, nb1), ALU.mult),
                              [d_CAB, d_pw], [d_tB[i_ % 2]])
                        kb.op("dve", TT(tav, tav, tbv, ALU.add), [d_tB[i_ % 2]], [d_tA[i_ % 2]])
                        kb.op("dve", TT(Xv(i_, 1), Xv(i_, 1), tav, ALU.add), [d_tA[i_ % 2], d_Xs, d_Xo], [d_Xo])
                        if i_ + 1 < BL:
                            pn = Pw[(i_ + 1) % 2]
                            pw_sw = pw.cu(0, 128, 24, [[-24, 2], [1, 24]])
                            kb.op("dve", TT(w_a[:], AA[:], pw[:], ALU.mult), [d_A, d_pw], [d_pq])
                            kb.op("dve", TT(w_b[:], AB[:], pw_sw, ALU.mult), [d_pw], [d_pq])
                            kb.op("dve", TT(pn[:], w_a[:], w_b[:], ALU.add), [d_pq], [d_pw])

            streams, quanta = [], []
            if use_fft:
                streams.append(fft_section); quanta.append(1)
            else:
                for j in range(N5):
                    kb.op("dve", MS(ccT[:, 3:5, j5(j)], 0.0), [], [d_cc[j]])

            if use_ssm:
                streams.append(ssmA_section); quanta.append(8)
            if streams:
                kb.interleave(streams, quanta)
            if use_ssm:
                unselW = ar.alloc([128, 8, 240], BF16)
                ld("sp", unselW[:], unselW_d[:, :, :], d_sel)

                Yb = [ar.alloc([128, K8], BF16) for _ in range(2)]; d_Yb = [Dep(), Dep()]
                yg = ar.alloc([128, 3, S], BF16); d_yg = Dep()
                for ch in range(3):
                    accs = [P(hold=True) for _ in range(N5)]
                    for j in range(N5):
                        kb.op("dve", MS(accs[j][0][:, :], 0.0), [], [accs[j][1]])
                    for g8 in range(8):
                        g = ch * 8 + g8
                        bi = mi_[0] % 2
                        mi_[0] += 1
                        ld("sp", mb[bi][:], ssm_scr[l, :, g * 896:(g + 1) * 896].rearrange("p (r x) -> p r x", r=7), d_mb[bi])
                        pt, pd = P()
                        fs = [MM(pt[:, 0:K8], mb[bi][:, 0, :], Ug[:, g, :], start=True, stop=False)]
                        for ri in range(2):
                            fs.append(MM(pt[:, 1:K8], mb[bi][:, 3 + 2 * ri, :], Xs.cu(0, 128, ri * 24 + g, [[48, K8 - 1]]),
                                         start=False, stop=False))
                            fs.append(MM(pt[:, 0:K8 - 1], mb[bi][:, 4 + 2 * ri, :],
                                         Xs.cu(0, 128, (K8 - 2) * 48 + ri * 24 + g, [[-48, K8 - 1]]),
                                         start=False, stop=(ri == 1)))
                        kb.op("pe", SEQ(fs), [d_mb[bi], d_Ug, d_Xs, d_Xo], [pd])
                        yb = g % 2
                        evac(Yb[yb][:], pt[:, 0:K8], [pd], [d_Yb[yb]])
                        for j in range(N5):
                            kb.op("pe", SEQ([MMZ(bass.AP(tensor=accs[j][0], offset=t2, ap=[[512, 128], [8, 64]]),
                                                 unselW[:, t2, 112 - 16 * g8:112 - 16 * g8 + 128],
                                                 Yb[yb][:, j * 64:(j + 1) * 64]) for t2 in range(8)]),
                                  [d_Yb[yb], d_sel], [accs[j][1]])
                    for j in range(N5):
                        kb.op("act", ACTF(yg[:, ch, j5(j)], accs[j][0][:, :], AF.Gelu_apprx_tanh), [accs[j][1]], [d_yg])
                        Prelease(accs[j])
                if DBG and s == 0 and l == 0:
                    kb.dma("sp", dbg_xs[:, :], Xs[:].rearrange("p k r g -> p (k r g)"), reads=[d_Xs, d_Xo], writes=[Dep()])
                    kb.dma("sp", dbg_yg[:, :], yg[:].rearrange("p c t -> p (c t)"), reads=[d_yg], writes=[Dep()])
                    kb.barrier()
                sg = ar.alloc([128, 512], BF16); d_sg = Dep()
                for oc in range(3):
                    for j in range(N5):
                        pt, pd = P()
                        kb.op("pe", SEQ([MM(pt[:, :], wglu[:, kt, oc * 128:(oc + 1) * 128], yg[:, kt, j5(j)],
                                            start=(kt == 0), stop=(kt == 2)) for kt in range(3)]), [d_wg, d_yg], [pd])
                        kb.op("act", ACTF(sg[:], pt[:, :], AF.Sigmoid), [pd], [d_sg])
                        kb.op("dve", TT(ccT[:, oc, j5(j)], yg[:, oc, j5(j)], sg[:], ALU.mult), [d_sg, d_yg], [d_cc[j]])
                ar.release(mkS)
            else:
                for j in range(N5):
                    kb.op("dve", MS(ccT[:, 0:3, j5(j)], 0.0), [], [d_cc[j]])
            kb.barrier()
            ar.release(mk0)
            mkO = ar.mark()
            w_out_sb = ar.alloc([128, NCH, D], BF16); d_wo = Dep()
            for kt in range(NCH):
                kb.dma("sp", w_out_sb[:, kt, :], wout_b[l, :, kt, :], writes=[d_wo])
            proj_post(l, 0, s, lambda kt, oc: w_out_sb[:, kt, oc * 128:(oc + 1) * 128], d_wo, NCH,
                      lambda kt, j: ccT[:, kt, j5(j)], lambda j: [d_cc[j]])
            kb.barrier()
            ar.release(mkO)
            if "ffn" in parts:
                mkN = ar.mark()
                actT = ar.alloc([128, NFC, 512], BF16)
                NB = 3
                upr = [ar.alloc([128, 2, 514]) for _ in range(NB)]
                d_upr = [[Dep(), Dep()] for _ in range(NB)]
                cen = [ar.alloc([128, 2, 512]) for _ in range(NB)]
                d_cen = [[Dep(), Dep()] for _ in range(NB)]
                gl = [ar.alloc([128, 512]) for _ in range(2)]; d_gl = [Dep(), Dep()]
                wu = [ar.alloc([128, NCH, 256], BF16) for _ in range(3)]; d_wu = [Dep() for _ in range(3)]
                wd = [ar.alloc([128, D], BF16) for _ in range(3)]; d_wd = [Dep() for _ in range(3)]
                y_sb = ar.alloc([128, NCH, 512]); d_y = [Dep() for _ in range(NCH)]
                sq = ar.alloc([128, NCH, 512], BF16); d_sq = [Dep() for _ in range(NCH)]
                rstd = ar.alloc([128, 512]); d_rstd = Dep()
                tmp = ar.alloc([128, 512]); d_tmp = Dep()
                pn_f = (sq, d_sq, rstd, d_rstd, tmp, d_tmp)
                d_act = [Dep() for _ in range(NFC)]
                wi_ = 0
                ci_ = 0
                gcnt = [0]

                def ffn_tail(cc_, ub):
                    gb = gcnt[0] % 2
                    gcnt[0] += 1
                    ce = cen[ub]
                    kb.op("act", ACTF(gl[gb][:], ce[:, 0, :], AF.Gelu_apprx_tanh), [d_cen[ub][0]], [d_gl[gb]])
                    kb.op("pool", TT(actT[:, cc_, :], gl[gb][:], ce[:, 1, :], ALU.mult), [d_gl[gb], d_cen[ub][1]], [d_act[cc_]])

                for j in range(N5):
                    t0 = j * 512
                    pend = []
                    pre_norm(l, 1, s, pn_f, [0, 1][:N5] if j == 0 else ([j + 1] if j + 1 < N5 else []))
                    for cc_ in range(NFC):
                        bi = wi_ % 3
                        wi_ += 1
                        ub = ci_ % NB
                        ci_ += 1
                        kb.dma("sp", wu[bi][:].rearrange("p k c -> p (k c)"), wup_b[l, cc_], writes=[d_wu[bi]])
                        u_ = upr[ub]
                        for hv in range(2):
                            pt, pd = P()
                            kb.op("pe", SEQ([MM(pt[:, :], wu[bi][:, kt, hv * 128:(hv + 1) * 128], hT[:, kt, j5(j)],
                                                start=(kt == 0), stop=(kt == NCH - 1)) for kt in range(NCH)]),
                                  [d_wu[bi], d_h[j]], [pd])
                            kb.op("act", ACTF(u_[:, hv, 1:513], pt[:, :], AF.Copy), [pd], [d_upr[ub][hv]])
                        toks = [t_ for t_ in (t0 - 1, t0 + 512) if 0 <= t_ < S]
                        if t0 - 1 < 0:
                            kb.op("pool", MS(u_[:, :, 0:1], 0.0), [], d_upr[ub])
                        if t0 + 512 >= S:
                            kb.op("pool", MS(u_[:, :, 513:514], 0.0), [], d_upr[ub])
                        if toks:
                            nh = len(toks)
                            pt, pd = P()
                            fs = []
                            for hv in range(2):
                                for kt in range(NCH):
                                    rhs = hT.cu(0, 128, kt * S + toks[0], [[513, nh]])
                                    fs.append(MM(pt[:, hv * 2:hv * 2 + nh], wu[bi][:, kt, hv * 128:(hv + 1) * 128], rhs,
                                                 start=(kt == 0), stop=(kt == NCH - 1)))
                            kb.op("pe", SEQ(fs), [d_wu[bi]] + d_h, [pd])
                            for ti, t_ in enumerate(toks):
                                col = 0 if t_ == t0 - 1 else 513
                                kb.op("dve", CP(u_[:, :, col:col + 1], bass.AP(tensor=pt, offset=ti, ap=[[512, 128], [2, 2], [1, 1]])),
                                      [pd], d_upr[ub])
                        ce = cen[ub]
                        for hv in range(2):
                            ci = hv * NFC + cc_
                            dd = [d_cen[ub][hv]]
                            kb.op("act", ACTF(ce[:, hv, :], u_[:, hv, 1:513], AF.Identity, bias=cb[:, l, ci:ci + 1],
                                              scale=cw[:, l, 1, ci:ci + 1]), [d_upr[ub][hv], d_cw, d_cb], dd)
                            kb.op("dve", STT(ce[:, hv, :], u_[:, hv, 0:512], cw[:, l, 0, ci:ci + 1], ce[:, hv, :], ALU.mult, ALU.add),
                                  [d_upr[ub][hv], d_cw], dd)
                            kb.op("dve", STT(ce[:, hv, :], u_[:, hv, 2:514], cw[:, l, 2, ci:ci + 1], ce[:, hv, :], ALU.mult, ALU.add),
                                  [d_upr[ub][hv], d_cw], dd)
                        pend.append((cc_, ub))
                        if len(pend) > 1:
                            ffn_tail(*pend.pop(0))
                    while pend:
                        ffn_tail(*pend.pop(0))
                    accs = [P(hold=True) for _ in range(NCH)]
                    for cc_ in range(NFC):
                        bi = wi_ % 3
                        wi_ += 1
                        kb.dma("sp", wd[bi][:], wdn_b[l, cc_], writes=[d_wd[bi]])
                        for oc in range(NCH):
                            kb.op("pe", MM(accs[oc][0][:, :], wd[bi][:, oc * 128:(oc + 1) * 128], actT[:, cc_, :],
                                           start=(cc_ == 0), stop=(cc_ == NFC - 1)), [d_wd[bi], d_act[cc_]], [accs[oc][1]])
                    for oc in range(NCH):
                        kb.op("dve", CP(y_sb[:, oc, :], accs[oc][0][:, :]), [accs[oc][1]], [d_y[oc]])
                        kb.op("act", ACTF(sq[:, oc, :], y_sb[:, oc, :], AF.Square), [d_y[oc]], [d_sq[oc]])
                    for a_ in accs:
                        Prelease(a_)
                    post_norm_residual(l, 1, s, j, y_sb, d_y, sq, d_sq, rstd, d_rstd, tmp, d_tmp)
                kb.barrier()
                ar.release(mkN)
        for c_ in range(NCH):
            dd = Dep()
            d_out.append(dd)
            kb.dma("sp", outT_d[s, :, c_, :], xT[:, c_, :], reads=d_x, writes=[dd])
    kb.wait_all("sp", d_out)
    kb.emit(st)
    st.close()
    print("built: %d ops, arena peak %d KiB" % (kb.ninst, ar.peak // 1024))
    return nc


def _consts(S):
    bf = ml_dtypes.bfloat16
    c = {}
    c["ident"] = np.eye(128, dtype=np.float32)
    sp = np.arange(128) // 16
    c["maskf"] = (sp[None, :] >= sp[:, None]).astype(np.float32)
    c["maskb"] = (sp[None, :] <= sp[:, None]).astype(np.float32)
    selW = np.zeros((128, 8, 240), np.float32)
    for g8 in range(8):
        for cc in range(16):
            selW[16 * g8 + cc, g8, cc + 112] = 1.0
    c["selW"] = selW.astype(bf)
    unselW = np.zeros((128, 8, 240), np.float32)
    for t2 in range(8):
        for cc in range(16):
            unselW[16 * t2 + cc, t2, cc + 112] = 1.0
    c["unselW"] = unselW.astype(bf)
    slopes = np.exp2(-8.0 * np.arange(1, 7, dtype=np.float64) / 6)
    j = np.arange(128)[:, None, None]
    kb_ = np.arange(3)[None, :, None]
    i = np.arange(128)[None, None, :]
    dist = np.abs(i + 128 - (kb_ * 128 + j))
    bias = np.where(dist[:, None] <= 128, -slopes[None, :, None, None] * dist[:, None], -1e30)
    c["biasT"] = bias.astype(np.float32).astype(bf)
    a = np.arange(64)
    th = 2 * np.pi * np.outer(a, a) / 64
    c64 = np.cos(th) / 8.0
    s64 = -np.sin(th) / 8.0
    c["c64d"] = np.concatenate([c64, c64], 1).astype(np.float32)
    c["s64d"] = np.concatenate([s64, s64], 1).astype(np.float32)
    c["mvals"] = np.broadcast_to(np.arange(9, dtype=np.float32)[None, :, None], (128, 9, 24)).copy()
    p = np.arange(S, dtype=np.int64)
    th = 2 * np.pi * ((p[:, None] * p[None, :]) % S) / S
    c["dft"] = np.stack([np.cos(th), np.sin(th)]).astype(np.float32) / np.sqrt(S)
    c["dft"] = c["dft"].astype(bf)
    return c


def _prep_shared(inp, S):
    f = lambda a: np.ascontiguousarray(np.asarray(a, dtype=np.float32))
    d = {}
    d["w_ada"] = f(inp["w_ada"]).reshape(L, NCH, 128, 6 * D)
    d["b_adaT"] = f(f(inp["b_ada"]).reshape(L, 48, 128).transpose(2, 0, 1))
    g = np.stack([f(inp[k]) for k in ("g_pre_mix", "g_post_mix", "g_pre_ffn", "g_post_ffn")])
    d["gT"] = f(g.reshape(4, L, NCH, 128).transpose(3, 1, 0, 2))
    w = f(inp["w_in"])
    wn = np.concatenate([w[:, :, 0:1024], w[:, :, 1024:1088], w[:, :, 1024:1088], w[:, :, 1088:1152],
                         w[:, :, 1088:1152], w[:, :, 1152:1280]], axis=2)
    d["w_in"] = f(wn.reshape(L, NCH, 128, WIN_COLS).transpose(0, 2, 1, 3))
    lam = np.stack([f(inp["lam_re"]), f(inp["lam_im"])])
    d["lamT"] = f(lam.transpose(2, 4, 1, 0, 3).reshape(128, L, 2, 24))
    ldt = f(inp["log_dt"])
    d["ldt"] = f(np.broadcast_to(ldt.transpose(1, 0, 2)[:, None], (2, 64, L, 24)).reshape(128, L, 24))
    b = np.stack([f(inp["b_re"]), f(inp["b_im"])])
    d["bT"] = f(b.transpose(2, 4, 1, 0, 3, 5).reshape(128, L, 2, 24, 16))
    cc = np.stack([f(inp["c_re"]), f(inp["c_im"])])
    cc = cc.transpose(4, 1, 0, 2, 3)
    d["cTs"] = f(np.concatenate([cc, cc], 0))
    dsk = f(inp["d_skip"]).reshape(L, 24, 16)
    d["dsk"] = f(np.tile(dsk.transpose(2, 0, 1), (8, 1, 1)))
    d["w_glu"] = f(f(inp["w_glu"]).reshape(L, 3, 128, 384).transpose(0, 2, 1, 3))
    d["w_fft"] = f(f(inp["w_fft"]).transpose(0, 2, 1, 3))
    sk = f(inp["sink"]).reshape(L, 3, 2)
    d["sinkT"] = f(np.repeat(sk.transpose(2, 0, 1), 64, axis=0))
    d["w_out"] = f(f(inp["w_out"]).reshape(L, NCH, 128, D).transpose(0, 2, 1, 3))
    wu = f(inp["w_up"]).reshape(L, NCH, 128, 2, NFC, 128)
    d["w_up"] = f(wu.transpose(0, 4, 2, 1, 3, 5).reshape(L, NFC, 128, NCH, 256))
    d["conv_w"] = f(f(inp["conv_w"]).reshape(L, 3, 2, NFC, 128).transpose(4, 0, 1, 2, 3).reshape(128, L, 3, 2 * NFC))
    d["conv_b"] = f(f(inp["conv_b"]).reshape(L, 2, NFC, 128).transpose(3, 0, 1, 2).reshape(128, L, 2 * NFC))
    d["w_down"] = f(inp["w_down"]).reshape(L, NFC, 128, D)
    d.update(_consts(S))
    return d


_NC_CACHE = {}


def run_cores(inp, S, n_cores, parts=("ssm", "fft", "att", "ffn"), nlayers=L):
    key = (S, tuple(parts), nlayers)
    if key not in _NC_CACHE:
        _NC_CACHE[key] = build(S, parts, nlayers)
    nc = _NC_CACHE[key]
    shared = _prep_shared(inp, S)
    x = np.asarray(inp["x"], dtype=np.float32)
    c = np.asarray(inp["c"], dtype=np.float32)
    in_maps = []
    for i in range(n_cores):
        xb = x[2 * i:2 * i + 2]
        m = dict(shared)
        m["xT"] = np.ascontiguousarray(xb.transpose(0, 2, 1).reshape(2, NCH, 128, S).transpose(0, 2, 1, 3))
        m["cT"] = np.ascontiguousarray(c[2 * i:2 * i + 2].reshape(2, NCH, 128).transpose(2, 1, 0))
        in_maps.append(m)
    res = run_bass_kernel_spmd(nc, in_maps, core_ids=list(range(n_cores)))
    outs = []
    for r in res.results:
        o = np.asarray(r["outT"])
        outs.append(o.transpose(0, 3, 2, 1).reshape(2, S, D))
    return np.concatenate(outs, 0).astype(np.float32)


def kernel(**inputs):
    S = inputs["x"].shape[1]
    return run_cores(inputs, S, 8)
```
